# Optimizing a Trainium2 kernel written in Bass

```python
import math
import jax, jax.numpy as jnp
from jax import lax
import numpy as np

D_MODEL = 1024
BATCH = 32
SEQ = 2048
DEPTH = 4

HEAD_DIM = 64
MLA_HEADS = 8
MLA_NOPE = 64
MLA_ROPE = 32
MLA_V = 64
MLA_Q_LORA = 384
MLA_KV_LORA = 256
ROPE_THETA = 10000.0
SWA_HEADS = 8
SWA_KV_HEADS = 2
SWA_WINDOW = 128
REL_BUCKETS = 32
REL_MAX_DIST = 128
FOX_HEADS = 16
D_FF = 4 * D_MODEL
D_PLE = 256
BLOCK_Q = 128
DN_ALPHA = (2 * DEPTH) ** 0.25
DN_BETA = (8 * DEPTH) ** -0.25
NORM_EPS = 1e-5
NEG_INF = -1e30
N_EVEN = (DEPTH + 1) // 2
N_ODD = DEPTH // 2
EVEN_SPLIT = (MLA_Q_LORA, MLA_KV_LORA, MLA_ROPE, SWA_HEADS * HEAD_DIM,
              SWA_KV_HEADS * HEAD_DIM, SWA_KV_HEADS * HEAD_DIM)
EVEN_IN = MLA_Q_LORA + MLA_KV_LORA + MLA_ROPE + (SWA_HEADS + 2 * SWA_KV_HEADS) * HEAD_DIM
EVEN_MIX = MLA_HEADS * MLA_V + SWA_HEADS * HEAD_DIM
ODD_SPLIT = (FOX_HEADS * HEAD_DIM, FOX_HEADS * HEAD_DIM, FOX_HEADS * HEAD_DIM, FOX_HEADS)
ODD_IN = 3 * FOX_HEADS * HEAD_DIM + FOX_HEADS
ODD_MIX = FOX_HEADS * HEAD_DIM

kernel_name = "hybrid_mla_swa_fox_deepnorm"


def _split(h, sizes):
    out, o = [], 0
    for n in sizes:
        out.append(h[..., o:o + n])
        o += n
    return out


def _layer_norm(x, g, b):
    xf = x.astype(jnp.float32)
    mu = jnp.mean(xf, -1, keepdims=True)
    var = jnp.mean(jnp.square(xf - mu), -1, keepdims=True)
    y = (xf - mu) * lax.rsqrt(var + NORM_EPS)
    return (y * g.astype(jnp.float32) + b.astype(jnp.float32)).astype(x.dtype)


def _rms_norm(x, g):
    xf = x.astype(jnp.float32)
    y = xf * lax.rsqrt(jnp.mean(jnp.square(xf), -1, keepdims=True) + NORM_EPS)
    return (y * g.astype(jnp.float32)).astype(x.dtype)


def _rope_tables(seq_len, dim):
    inv = 1.0 / (ROPE_THETA ** (jnp.arange(0, dim, 2, dtype=jnp.float32) / dim))
    ang = jnp.arange(seq_len, dtype=jnp.float32)[:, None] * inv[None, :]
    return jnp.cos(ang), jnp.sin(ang)


def _apply_rope(x, cos, sin):
    x1, x2 = jnp.split(x.astype(jnp.float32), 2, axis=-1)
    c = cos[:, None, :]
    s = sin[:, None, :]
    return jnp.concatenate([x1 * c - x2 * s, x2 * c + x1 * s], -1).astype(x.dtype)


def _t5_bucket(dist):
    exact = REL_BUCKETS // 2
    d = jnp.maximum(dist, 1).astype(jnp.float32)
    large = exact + (jnp.log(d / exact) / math.log(REL_MAX_DIST / exact)
                     * (REL_BUCKETS - exact)).astype(jnp.int32)
    large = jnp.minimum(large, REL_BUCKETS - 1)
    return jnp.where(dist < exact, dist, large)


def _mla_attend(q_nope, q_rope, k_nope, k_rope, v):
    B, S, H, _ = q_nope.shape
    nb = S // BLOCK_Q
    scale = (MLA_NOPE + MLA_ROPE) ** -0.5
    qn = q_nope.reshape(B, nb, BLOCK_Q, H, MLA_NOPE).transpose(1, 0, 2, 3, 4)
    qr = q_rope.reshape(B, nb, BLOCK_Q, H, MLA_ROPE).transpose(1, 0, 2, 3, 4)
    kpos = jnp.arange(S)

    def block(args):
        i, qn_b, qr_b = args
        s = (jnp.einsum('bqhd,bkhd->bhqk', qn_b, k_nope, preferred_element_type=jnp.float32)
             + jnp.einsum('bqhd,bkd->bhqk', qr_b, k_rope, preferred_element_type=jnp.float32)) * scale
        qpos = i * BLOCK_Q + jnp.arange(BLOCK_Q)
        s = jnp.where(kpos[None, :] <= qpos[:, None], s, NEG_INF)
        w = jax.nn.softmax(s, axis=-1).astype(v.dtype)
        return jnp.einsum('bhqk,bkhd->bqhd', w, v)

    out = lax.map(block, (jnp.arange(nb), qn, qr))
    return out.transpose(1, 0, 2, 3, 4).reshape(B, S, H * MLA_V)


def _swa_attend(q, k, v, sinks, rel_bias):
    B, S, H, d = q.shape
    KVH = k.shape[2]
    G = H // KVH
    nb = S // BLOCK_Q
    qb = q.reshape(B, nb, BLOCK_Q, KVH, G, d)

    def band(t):
        tb = t.reshape(B, nb, BLOCK_Q, KVH, d)
        prev = jnp.pad(tb, ((0, 0), (1, 0), (0, 0), (0, 0), (0, 0)))[:, :-1]
        return jnp.concatenate([prev, tb], axis=2)

    kb, vb = band(k), band(v)
    s = jnp.einsum('bnqkgd,bnskd->bnkgqs', qb, kb, preferred_element_type=jnp.float32) * (d ** -0.5)
    a = jnp.arange(BLOCK_Q)[:, None]
    col = jnp.arange(2 * BLOCK_Q)[None, :]
    dist = a + BLOCK_Q - col
    in_win = (dist >= 0) & (dist < SWA_WINDOW)
    pad = (jnp.arange(nb)[:, None, None] == 0) & (col < BLOCK_Q)[None]
    valid = in_win[None] & ~pad
    bias = rel_bias[_t5_bucket(jnp.maximum(dist, 0))].astype(jnp.float32)
    bias = bias.transpose(2, 0, 1).reshape(KVH, G, BLOCK_Q, 2 * BLOCK_Q)
    s = jnp.where(valid[None, :, None, None], s + bias, NEG_INF)
    sink = jnp.broadcast_to(sinks.astype(jnp.float32).reshape(1, 1, KVH, G, 1, 1), s.shape[:-1] + (1,))
    w = jax.nn.softmax(jnp.concatenate([s, sink], axis=-1), axis=-1)[..., :-1].astype(v.dtype)
    out = jnp.einsum('bnkgqs,bnskd->bnqkgd', w, vb)
    return out.reshape(B, S, H * d)


def _fox_attend(q, k, v, log_f):
    B, S, H, d = q.shape
    nb = S // BLOCK_Q
    c = jnp.cumsum(log_f, axis=1)
    cq = c.reshape(B, nb, BLOCK_Q, H).transpose(1, 0, 3, 2)
    ck = c.transpose(0, 2, 1)
    qb = q.reshape(B, nb, BLOCK_Q, H, d).transpose(1, 0, 2, 3, 4)
    kpos = jnp.arange(S)

    def block(args):
        i, q_b, cq_b = args
        s = jnp.einsum('bqhd,bkhd->bhqk', q_b, k, preferred_element_type=jnp.float32) * (d ** -0.5)
        s = s + cq_b[..., :, None] - ck[:, :, None, :]
        qpos = i * BLOCK_Q + jnp.arange(BLOCK_Q)
        s = jnp.where(kpos[None, :] <= qpos[:, None], s, NEG_INF)
        w = jax.nn.softmax(s, axis=-1).astype(v.dtype)
        return jnp.einsum('bhqk,bkhd->bqhd', w, v)

    out = lax.map(block, (jnp.arange(nb), qb, cq))
    return out.transpose(1, 0, 2, 3, 4).reshape(B, S, H * d)


def _even_mixer(x, w_in, q_norm, w_uq, kv_norm, w_ukv, sinks, w_out, rel_bias, cos, sin):
    B, S, _ = x.shape
    h = x @ w_in
    c_q, c_kv, k_rope, q_s, k_s, v_s = _split(h, EVEN_SPLIT)
    q = (_rms_norm(c_q, q_norm) @ w_uq).reshape(B, S, MLA_HEADS, MLA_NOPE + MLA_ROPE)
    q_nope = q[..., :MLA_NOPE]
    q_rope = _apply_rope(q[..., MLA_NOPE:], cos, sin)
    kv = (_rms_norm(c_kv, kv_norm) @ w_ukv).reshape(B, S, MLA_HEADS, MLA_NOPE + MLA_V)
    k_nope, v = kv[..., :MLA_NOPE], kv[..., MLA_NOPE:]
    k_rope = _apply_rope(k_rope[:, :, None, :], cos, sin)[:, :, 0]
    o_mla = _mla_attend(q_nope, q_rope, k_nope, k_rope, v)
    o_swa = _swa_attend(q_s.reshape(B, S, SWA_HEADS, HEAD_DIM),
                        k_s.reshape(B, S, SWA_KV_HEADS, HEAD_DIM),
                        v_s.reshape(B, S, SWA_KV_HEADS, HEAD_DIM), sinks, rel_bias)
    return jnp.concatenate([o_mla, o_swa], axis=-1) @ w_out


def _odd_mixer(x, w_in, b_f, w_out):
    B, S, _ = x.shape
    q, k, v, f = _split(x @ w_in, ODD_SPLIT)
    log_f = jax.nn.log_sigmoid((f + b_f).astype(jnp.float32))
    o = _fox_attend(q.reshape(B, S, FOX_HEADS, HEAD_DIM), k.reshape(B, S, FOX_HEADS, HEAD_DIM),
                    v.reshape(B, S, FOX_HEADS, HEAD_DIM), log_f)
    return o @ w_out


def _sq_relu_mlp(x, w_up, w_down):
    return jnp.square(jax.nn.relu(x @ w_up)) @ w_down


def setup_inputs(seed: int = 0) -> dict:
    key = jax.random.key(seed)
    ks = jax.random.split(key, 24)
    nrm = jax.random.normal
    f32 = jnp.float32
    return {
        "x": nrm(ks[0], (BATCH, SEQ, D_MODEL), f32),
        "p": nrm(ks[1], (DEPTH, BATCH, SEQ, D_PLE), f32),
        "rel_bias": 0.5 * nrm(ks[2], (REL_BUCKETS, SWA_HEADS), f32),
        "ev_w_in": nrm(ks[3], (N_EVEN, D_MODEL, EVEN_IN), f32) * D_MODEL ** -0.5,
        "ev_q_norm": 1.0 + 0.02 * nrm(ks[4], (N_EVEN, MLA_Q_LORA), f32),
        "ev_w_uq": nrm(ks[5], (N_EVEN, MLA_Q_LORA, MLA_HEADS * (MLA_NOPE + MLA_ROPE)), f32) * MLA_Q_LORA ** -0.5,
        "ev_kv_norm": 1.0 + 0.02 * nrm(ks[6], (N_EVEN, MLA_KV_LORA), f32),
        "ev_w_ukv": nrm(ks[7], (N_EVEN, MLA_KV_LORA, MLA_HEADS * (MLA_NOPE + MLA_V)), f32) * MLA_KV_LORA ** -0.5,
        "ev_sinks": 0.5 * nrm(ks[8], (N_EVEN, SWA_HEADS), f32),
        "ev_w_out": nrm(ks[9], (N_EVEN, EVEN_MIX, D_MODEL), f32) * (EVEN_MIX ** -0.5 * DN_BETA),
        "od_w_in": nrm(ks[10], (N_ODD, D_MODEL, ODD_IN), f32) * D_MODEL ** -0.5,
        "od_b_f": jax.random.uniform(ks[11], (N_ODD, FOX_HEADS), f32, 1.0, 4.0),
        "od_w_out": nrm(ks[12], (N_ODD, ODD_MIX, D_MODEL), f32) * (ODD_MIX ** -0.5 * DN_BETA),
        "ln1_g": 1.0 + 0.02 * nrm(ks[13], (DEPTH, D_MODEL), f32),
        "ln1_b": 0.02 * nrm(ks[14], (DEPTH, D_MODEL), f32),
        "w_up": nrm(ks[15], (DEPTH, D_MODEL, D_FF), f32) * D_MODEL ** -0.5,
        "w_down": nrm(ks[16], (DEPTH, D_FF, D_MODEL), f32) * (D_FF ** -0.5 * DN_BETA),
        "ln2_g": 1.0 + 0.02 * nrm(ks[17], (DEPTH, D_MODEL), f32),
        "ln2_b": 0.02 * nrm(ks[18], (DEPTH, D_MODEL), f32),
        "ple_w_proj": nrm(ks[19], (DEPTH, D_PLE, D_MODEL), f32) * D_PLE ** -0.5,
        "ple_w_gate": nrm(ks[20], (DEPTH, D_MODEL, D_MODEL), f32) * D_MODEL ** -0.5,
        "ple_b_gate": 0.02 * nrm(ks[21], (DEPTH, D_MODEL), f32),
    }


def reference(x, p, rel_bias, ev_w_in, ev_q_norm, ev_w_uq, ev_kv_norm, ev_w_ukv, ev_sinks, ev_w_out,
              od_w_in, od_b_f, od_w_out, ln1_g, ln1_b, w_up, w_down, ln2_g, ln2_b,
              ple_w_proj, ple_w_gate, ple_b_gate):
    S = x.shape[1]
    cos, sin = _rope_tables(S, MLA_ROPE)
    for i in range(DEPTH):
        j = i // 2
        if i % 2 == 0:
            m = _even_mixer(x, ev_w_in[j], ev_q_norm[j], ev_w_uq[j], ev_kv_norm[j], ev_w_ukv[j],
                            ev_sinks[j], ev_w_out[j], rel_bias, cos, sin)
        else:
            m = _odd_mixer(x, od_w_in[j], od_b_f[j], od_w_out[j])
        x = _layer_norm(DN_ALPHA * x + m, ln1_g[i], ln1_b[i])
        x = _layer_norm(DN_ALPHA * x + _sq_relu_mlp(x, w_up[i], w_down[i]), ln2_g[i], ln2_b[i])
        gate = jax.nn.sigmoid(x @ ple_w_gate[i] + ple_b_gate[i])
        x = x + gate * (p[i] @ ple_w_proj[i])
    return x
```

```python
import math
from contextlib import ExitStack, contextmanager

import numpy as np
import concourse.bass as bass
import concourse.mybir as mybir
from concourse.bass_utils import run_bass_kernel_spmd

F32 = mybir.dt.float32
BF16 = mybir.dt.bfloat16
AF = mybir.ActivationFunctionType
ALU = mybir.AluOpType

S = 2048
D = 1024
NT = 16
NCH = 4
CH = 512
DC = 8
DEPTH = 4
DFF = 4096
ALPHA = float((2 * DEPTH) ** 0.25)
EPS = 1e-5
N_CORES = 8
SEQ_PER_CORE = 4
SEM_LIMIT = 30000
WSLOT = 2048
NWSLOT = 5
WHOLD = 3

EV_COLS = dict(cq=0, ckv=384, kr=640, qs=672, ks=1184, vs=1312)


class Reg:
    __slots__ = ("name", "w", "r", "dsem", "dcnt", "excl")

    def __init__(self, name):
        self.name = name
        self.excl = False
        self.w = None
        self.r = {}
        self.dsem = None
        self.dcnt = 0


class Eng:
    def __init__(self, name, eng):
        self.name = name
        self.eng = eng
        self.sem = None
        self.cnt = 0
        self.seen = {}
        self.pending = False
        self.own = set()


class Ctx:
    def __init__(self, nc, es):
        self.nc = nc
        self.es = es
        self.dry = False
        self.nsem = 0
        self.E = {
            "pe": Eng("pe", nc.tensor),
            "act": Eng("act", nc.scalar),
            "dve": Eng("dve", nc.vector),
            "pool": Eng("pool", nc.gpsimd),
            "sp": Eng("sp", nc.sync),
        }
        for e in self.E.values():
            self._new_clock(e)
        self.all_regs = []
        self.dsem_pool = []

    def recycle(self, mark):
        for r in self.all_regs[mark:]:
            if r.dsem is not None:
                self.dsem_pool.append((r.dsem, r.dcnt))
                r.dsem = None
        del self.all_regs[mark:]

    def new_sem(self, name):
        self.nsem += 1
        return self.es.enter_context(self.nc.semaphore("%s_%d" % (name, self.nsem)))

    def _new_clock(self, e):
        e.sem = self.new_sem("clk_" + e.name)
        e.cnt = 0
        e.own.add(id(e.sem))

    def reg(self, name):
        r = Reg(name)
        self.all_regs.append(r)
        return r

    def _wait(self, E, evs):
        best = {}
        for (sem, val) in evs:
            k = id(sem)
            if k not in best or best[k][1] < val:
                best[k] = (sem, val)
        for k, (sem, val) in best.items():
            if k in E.own and E.name == "pe":
                continue
            if E.seen.get(k, 0) >= val:
                continue
            E.eng.wait_ge(sem, val)
            E.seen[k] = val

    def _collect(self, reads, writes, own=()):
        evs = []
        for r in reads:
            if r.w is not None:
                evs.append(r.w)
            if r.excl:
                evs.extend(e for e in r.r.values() if id(e[0]) not in own)
        for w in writes:
            if w.w is not None and id(w.w[0]) not in own:
                evs.append(w.w)
            evs.extend(e for e in w.r.values() if id(e[0]) not in own)
        return evs

    def _record(self, ev, reads, writes):
        k = id(ev[0])
        for r in reads:
            old = r.r.get(k)
            if old is None or old[1] < ev[1]:
                r.r[k] = ev
        for w in writes:
            w.w = ev
            w.r = {}

    def op(self, en, fn, reads=(), writes=(), inc=True):
        if self.dry:
            return
        E = self.E[en]
        self._wait(E, self._collect(reads, writes, E.own))
        ins = fn()
        if inc:
            if E.cnt >= SEM_LIMIT and not E.pending:
                self._new_clock(E)
            E.cnt += 1
            ins.then_inc(E.sem, 1)
            E.pending = False
            ev = (E.sem, E.cnt)
        else:
            E.pending = True
            ev = (E.sem, E.cnt + 1)
        self._record(ev, reads, writes)

    def dma(self, q, out, in_, reads=(), writes=(), sem_reg=None):
        if self.dry:
            return
        E = self.E[q]
        self._wait(E, self._collect(reads, writes))
        sr = sem_reg or (writes[0] if writes else reads[0])
        if sr.dsem is None:
            if self.dsem_pool:
                sr.dsem, sr.dcnt = self.dsem_pool.pop()
            else:
                sr.dsem = self.new_sem("dma")
                sr.dcnt = 0
        ins = E.eng.dma_start(out=out, in_=in_)
        sr.dcnt += 16
        ins.then_inc(sr.dsem, 16)
        ev = (sr.dsem, sr.dcnt)
        self._record(ev, reads, writes)

    def barrier(self):
        if self.dry:
            return
        evs = []
        for e in self.E.values():
            if e.cnt > 0:
                evs.append((e.sem, e.cnt))
        for r in self.all_regs:
            if r.w is not None:
                evs.append(r.w)
            evs.extend(r.r.values())
        for e in self.E.values():
            self._wait(e, evs)

    def final_wait(self, regs):
        if self.dry:
            return
        evs = []
        for r in regs:
            if r.w is not None:
                evs.append(r.w)
            evs.extend(r.r.values())
        for e in self.E.values():
            if e.cnt > 0:
                evs.append((e.sem, e.cnt))
        self._wait(self.E["sp"], evs)


class WStream:
    def __init__(self, cx, nc, es):
        self.cx = cx
        self.slots = [es.enter_context(nc.sbuf_tensor("s_wslot%d" % i, [128, WSLOT], BF16)) for i in range(NWSLOT)]
        self.regs = [cx.reg("wslot%d" % i) for i in range(NWSLOT)]
        for r in self.regs:
            r.dsem = cx.new_sem("wdma")
            r.dcnt = 0
        self.plan = []
        self.pos = 0
        self.loaded = 0

    def reset(self):
        self.pos = 0
        self.loaded = 0

    def _view(self, i, kc, ncols):
        return self.slots[i % NWSLOT][:, 0:kc * ncols].rearrange("p (k n) -> p k n", k=kc)

    def _load(self, j):
        (w, r0, nrows, c0, ncols) = self.plan[j]
        kc = nrows // 128
        src = w[r0:r0 + nrows, c0:c0 + ncols].rearrange("(k p) n -> p k n", p=128)
        self.cx.dma("pool", self._view(j, kc, ncols), src, reads=(), writes=(self.regs[j % NWSLOT],))

    def get(self, w, r0, nrows, c0, ncols):
        kc = nrows // 128
        assert kc * ncols <= WSLOT and nrows % 128 == 0
        i = self.pos
        if self.cx.dry:
            self.plan.append((w, r0, nrows, c0, ncols))
        else:
            pl = self.plan[i]
            assert pl[1:] == (r0, nrows, c0, ncols), (pl[1:], (r0, nrows, c0, ncols))
            while self.loaded < min(len(self.plan), i + NWSLOT - WHOLD + 1):
                self._load(self.loaded)
                self.loaded += 1
        self.pos += 1
        return self._view(i, kc, ncols), self.regs[i % NWSLOT]


class Builder:
    def __init__(self, nseq, layers, stop=None):
        self.nseq = nseq
        self.layers = list(layers)
        self.stop = stop
        nc = bass.Bass("TRN2", target_bir_lowering=False)
        self.nc = nc
        dt = nc.dram_tensor
        self.x_d = dt("x", [nseq, S, D], F32, kind="ExternalInput").ap()
        self.p_d = dt("p", [DEPTH, nseq, S, 256], F32, kind="ExternalInput").ap()
        self.out_d = dt("out", [nseq, S, D], F32, kind="ExternalOutput").ap()
        self.cscr_d = dt("cscr", [16, 3, S], BF16, kind="ExternalOutput").ap()
        self.w = {}
        for name, shape in [("ev_w_in", [2, 1024, 1440]), ("ev_w_uq", [2, 384, 768]), ("ev_w_ukv", [2, 256, 1024]),
                            ("ev_w_out", [2, 1024, 1024]), ("od_w_in", [2, 1024, 3088]), ("od_w_out", [2, 1024, 1024]),
                            ("w_up", [4, 1024, 4096]), ("w_down", [4, 4096, 1024]), ("ple_w_proj", [4, 256, 1024]),
                            ("ple_w_gate", [4, 1024, 1024])]:
            self.w[name] = dt(name, shape, F32, kind="ExternalInput").ap()
        self.consts_d = dt("consts", [128, 384], F32, kind="ExternalInput").ap()
        self.vecs_d = dt("vecs", [128, 192], F32, kind="ExternalInput").ap()
        self.rows_d = dt("rows", [1, 32], F32, kind="ExternalInput").ap()
        self.cs_d = dt("cs", [128, 2 * S], F32, kind="ExternalInput").ap()
        self.biasg_d = dt("biasg", [128, 2 * 8 * 128], F32, kind="ExternalInput").ap()
        with ExitStack() as es:
            self.es = es
            self.cx = Ctx(nc, es)
            self.alloc_persistent()
            self.cx.dry = True
            self.program()
            self.cx.dry = False
            self.W.reset()
            self.psi = 0
            self.program()

    @contextmanager
    def phase(self):
        cx = self.cx
        mark = len(cx.all_regs)
        with ExitStack() as ph:
            yield ph
            cx.barrier()
            cx.recycle(mark)

    def sb(self, stack, name, shape, dtype):
        self._nm = getattr(self, "_nm", 0) + 1
        return stack.enter_context(self.nc.sbuf_tensor("s%d_%s" % (self._nm, name), shape, dtype))

    def alloc_persistent(self):
        es, nc, cx = self.es, self.nc, self.cx
        self.x_f32 = self.sb(es, "x_f32", [128, DC, S], F32)
        self.xT_bf = self.sb(es, "xT_bf", [128, DC, S], BF16)
        self.R_xf = [cx.reg("xf%d" % c) for c in range(NCH)]
        self.R_xb = [cx.reg("xb%d" % c) for c in range(NCH)]
        self.W = WStream(cx, nc, es)
        self.consts = self.sb(es, "consts", [128, 384], F32)
        self.R_consts = cx.reg("consts")
        self.cbf = self.sb(es, "cbf", [128, 640], BF16)
        self.vecs = self.sb(es, "vecs", [128, 192], F32)
        self.rows = self.sb(es, "rows", [1, 32], F32)
        self.small = self.sb(es, "small", [128, 8], F32)
        self.onesf = self.sb(es, "onesf", [128, 128], F32)
        self.ones512 = self.sb(es, "ones512", [128, CH], F32)
        self.cs = self.sb(es, "cs", [128, 2 * S], BF16)
        self.R_cs = cx.reg("cs")
        self.R_cs.dsem = cx.new_sem("csdma")
        self.R_cs.dcnt = 0
        self.b8m = self.sb(es, "b8m", [128, 2 * 8 * 128], BF16)
        self.R_ebm = cx.reg("b8m")
        self.ps = es.enter_context(nc.psum_tensor("psum_all", [128, 8, 512], F32))
        self.R_ps = [cx.reg("ps%d" % b) for b in range(8)]
        for r in self.R_ps:
            r.excl = True
        self.R_cscr = cx.reg("cscr")
        self.R_out = cx.reg("outdma")
        self.psi = 0

    def psum(self):
        b = self.psi % 8
        self.psi += 1
        return self.ps[:, b, :], self.R_ps[b]

    def mm(self, out, lhsT, rhs, start, stop, reads, writes, inc):
        nc = self.nc
        self.cx.op("pe", lambda: nc.tensor.matmul(out, lhsT=lhsT, rhs=rhs, start=start, stop=stop,
                                                  skip_group_check=True), reads, writes, inc)

    def tr(self, out, in_, reads, writes, inc):
        nc = self.nc
        ident = self.consts[:, 0:128]
        self.cx.op("pe", lambda: nc.tensor.transpose(out=out, in_=in_, identity=ident), tuple(reads) + (self.R_consts,),
                   writes, inc)

    def act(self, out, in_, func, reads, writes, bias=None, scale=None):
        nc = self.nc
        kw = {}
        if bias is not None:
            kw["bias"] = bias
        if scale is not None:
            kw["scale"] = scale
        self.cx.op("act", lambda: nc.scalar.activation(out=out, in_=in_, func=func, **kw), reads, writes)

    def tt(self, out, in0, in1, op, reads, writes, eng="dve"):
        nc = self.nc
        e = nc.vector if eng == "dve" else nc.gpsimd
        self.cx.op(eng, lambda: e.tensor_tensor(out=out, in0=in0, in1=in1, op=op), reads, writes)

    def stt(self, out, in0, scalar, in1, op0, op1, reads, writes):
        nc = self.nc
        self.cx.op("dve", lambda: nc.vector.scalar_tensor_tensor(out=out, in0=in0, scalar=scalar, in1=in1, op0=op0, op1=op1),
                   reads, writes)

    def ts(self, out, in0, s1, s2, op0, op1, reads, writes):
        nc = self.nc
        if op1 is None:
            self.cx.op("dve", lambda: nc.vector.tensor_scalar(out=out, in0=in0, scalar1=s1, scalar2=None, op0=op0), reads, writes)
        else:
            self.cx.op("dve", lambda: nc.vector.tensor_scalar(out=out, in0=in0, scalar1=s1, scalar2=s2, op0=op0, op1=op1),
                       reads, writes)

    def vcopy(self, out, in_, reads, writes):
        nc = self.nc
        self.cx.op("dve", lambda: nc.vector.tensor_copy(out=out, in_=in_), reads, writes)

    def acopy(self, out, in_, reads, writes):
        self.act(out, in_, AF.Identity, reads, writes)

    def memset(self, ap, val, writes, eng="dve"):
        nc = self.nc
        e = nc.vector if eng == "dve" else nc.gpsimd
        self.cx.op(eng, lambda: e.memset(ap, val), (), writes)

    def recip(self, out, in_, reads, writes):
        nc = self.nc
        self.cx.op("dve", lambda: nc.vector.reciprocal(out=out, in_=in_), reads, writes)

    def pool_recip(self, buf, reads, writes):
        nc = self.nc
        ones = self.ones512[buf.base_partition():buf.base_partition() + 64, :]
        self.cx.op("pool", lambda: nc.gpsimd.tensor_tensor(out=buf, in0=buf, in1=ones, op=ALU.pow), reads, writes)

    def program(self):
        cx = self.cx
        import os
        dbg = int(os.environ.get("KDBG", "9"))
        if dbg >= 1:
            self.setup_consts()
        if dbg < 3:
            if dbg >= 2:
                self.load_x(0)
            cx.final_wait([self.R_out, self.R_cscr])
            return
        for s in range(self.nseq):
            self.load_x(s)
            done = False
            for li in self.layers:
                if li % 2 == 0:
                    self.even_mixer(li)
                else:
                    self.odd_mixer(li)
                if self.stop == (li, "mix"):
                    break
                with self.phase() as ph:
                    T = self.ln_alloc(ph)
                    self.layer_norm(li, 0, T)
                    if self.stop != (li, "ln1"):
                        self.ffn(li, ph)
                        self.layer_norm(li, 1, T)
                if self.stop in ((li, "ln1"), (li, "ln2")):
                    break
                self.ple(li, s)
            self.store_out(s)
        cx.final_wait([self.R_out, self.R_cscr])

    def setup_consts(self):
        cx = self.cx
        Rc = self.R_consts
        cx.dma("sp", self.consts[:], self.consts_d[:, :], writes=(Rc,))
        cx.dma("sp", self.vecs[:], self.vecs_d[:, :], writes=(Rc,))
        cx.dma("sp", self.rows[:], self.rows_d[:, :], writes=(Rc,))
        cx.dma("pool", self.cs[:], self.cs_d[:, :], writes=(self.R_cs,))
        self.memset(self.cbf[:, 0:128], 1.0, (Rc,))
        self.vcopy(self.cbf[:, 128:384], self.consts[:, 128:384], (Rc,), (Rc,))
        self.ts(self.cbf[:, 384:512], self.consts[:, 128:256], -1.0, 30000.0, ALU.add, ALU.mult, (Rc,), (Rc,))
        self.vcopy(self.cbf[:, 512:640], self.consts[:, 0:128], (Rc,), (Rc,))
        self.memset(self.small[:, 0:1], EPS, (Rc,))
        self.memset(self.small[:, 1:2], 1.0, (Rc,))
        self.memset(self.onesf[:], 1.0, (Rc,))
        self.memset(self.ones512[:], -1.0, (Rc,))
        with self.phase() as ph:
            tmp = self.sb(ph, "biasg_tmp", [128, 2048], F32)
            Rt = cx.reg("biasg_tmp")
            cx.dma("sp", tmp[:], self.biasg_d[:, :], writes=(Rt,))
            negm = self.sb(ph, "negm", [128, 128], F32)
            Rnm = cx.reg("negm")
            for kt in range(2):
                mask = self.consts[:, 256:384] if kt == 0 else self.consts[:, 128:256]
                self.ts(negm[:], mask, -1.0, 30000.0, ALU.add, ALU.mult, (Rc, Rnm), (Rnm,))
                for h in range(8):
                    o = (kt * 8 + h) * 128
                    self.tt(tmp[:, o:o + 128], tmp[:, o:o + 128], mask, ALU.mult, (Rc, Rt), (Rt,))
                    self.stt(self.b8m[:, o:o + 128], tmp[:, o:o + 128], 8.0, negm[:], ALU.mult, ALU.add, (Rt, Rnm), (self.R_ebm,))
            for jj in range(2):
                sc = 160 + jj * 16 + 5
                self.act(self.vecs[:, sc:sc + 8], self.vecs[:, sc:sc + 8], AF.Exp, (Rc,), (Rc,))

    def load_x(self, s):
        cx = self.cx
        with self.phase() as ph:
            xin = [self.sb(ph, "xin%d" % i, [128, D], F32) for i in range(4)]
            Rin = [cx.reg("xin%d" % i) for i in range(4)]
            for t in range(NT):
                b = t % 4
                c = t // 4
                cx.dma("sp", xin[b][:], self.x_d[s, t * 128:(t + 1) * 128, :], writes=(Rin[b],))
                for half in range(2):
                    pt, Rp = self.psum()
                    for j in range(4):
                        fc = half * 4 + j
                        self.tr(pt[:, j * 128:(j + 1) * 128], xin[b][:, fc * 128:(fc + 1) * 128], (Rin[b],), (Rp,), inc=(j == 3))
                    src = pt.rearrange("p (a b) -> p a b", a=4)
                    xf = self.x_f32[:, half * 4:half * 4 + 4, t * 128:(t + 1) * 128]
                    self.vcopy(xf, src, (Rp,), (self.R_xf[c],))
                    self.acopy(self.xT_bf[:, half * 4:half * 4 + 4, t * 128:(t + 1) * 128], xf, (self.R_xf[c],), (self.R_xb[c],))

    def store_out(self, s):
        cx = self.cx
        with self.phase() as ph:
            xo = [self.sb(ph, "xout%d" % i, [128, D], F32) for i in range(4)]
            Ro = [cx.reg("xout%d" % i) for i in range(4)]
            for t in range(NT):
                b = t % 4
                c = t // 4
                for half in range(2):
                    pt, Rp = self.psum()
                    for j in range(4):
                        fc = half * 4 + j
                        self.tr(pt[:, j * 128:(j + 1) * 128], self.x_f32[:, fc, t * 128:(t + 1) * 128], (self.R_xf[c],), (Rp,), inc=(j == 3))
                    if half == 0:
                        self.vcopy(xo[b][:, 0:512], pt, (Rp,), (Ro[b],))
                    else:
                        self.acopy(xo[b][:, 512:1024], pt, (Rp,), (Ro[b],))
                cx.dma("sp", self.out_d[s, t * 128:(t + 1) * 128, :], xo[b][:], reads=(Ro[b],), writes=(), sem_reg=self.R_out)
                if not cx.dry:
                    self.R_out.r[id(self.R_out.dsem)] = (self.R_out.dsem, self.R_out.dcnt)

    def resid_acc(self, m, c, pt, Rp, first):
        xf = self.x_f32[:, m, c * CH:(c + 1) * CH]
        if first:
            self.stt(xf, xf, ALPHA, pt, ALU.mult, ALU.add, (Rp, self.R_xf[c]), (self.R_xf[c],))
        else:
            self.tt(xf, xf, pt, ALU.add, (Rp, self.R_xf[c]), (self.R_xf[c],))

    def out_proj_partial(self, wname, j, r0, nkc, rhs_fn, Rrhs, first):
        wt, Rw = self.W.get(self.w[wname][j], r0, nkc * 128, 0, 1024)
        for c in range(NCH):
            for m in range(DC):
                pt, Rp = self.psum()
                for k in range(nkc):
                    self.mm(pt, wt[:, k, m * 128:(m + 1) * 128], rhs_fn(k, c), k == 0, k == nkc - 1,
                            (Rw,) + tuple(Rrhs), (Rp,), inc=(k == nkc - 1))
                self.resid_acc(m, c, pt, Rp, first)

    def ln_alloc(self, ph):
        cx = self.cx
        T = {}
        T["xb"] = self.sb(ph, "ln_xb", [128, DC, CH], BF16)
        T["xq"] = self.sb(ph, "ln_xq", [128, DC, CH], BF16)
        T["mean"] = self.sb(ph, "ln_mean", [128, CH], F32)
        T["rstd"] = self.sb(ph, "ln_rstd", [128, CH], F32)
        T["bm"] = self.sb(ph, "ln_bm", [128, CH], F32)
        T["u"] = [self.sb(ph, "ln_u%d" % i, [128, CH], F32) for i in range(2)]
        T["v"] = [self.sb(ph, "ln_v%d" % i, [128, CH], F32) for i in range(2)]
        T["R"] = (cx.reg("ln_xb"), cx.reg("ln_xq"), cx.reg("ln_st"))
        T["Ru"] = [cx.reg("ln_u%d" % i) for i in range(2)]
        T["Rv"] = [cx.reg("ln_v%d" % i) for i in range(2)]
        return T

    def layer_norm(self, li, which, T):
        cx = self.cx
        gcol = li * 40 + which * 16
        ones_bf = self.cbf[:, 0:128]
        if True:
            xb, xq, mean, rstd, bm, u, v = T["xb"], T["xq"], T["mean"], T["rstd"], T["bm"], T["u"], T["v"]
            Rxb, Rxq, Rst = T["R"]
            Ru, Rv = T["Ru"], T["Rv"]
            for c in range(NCH):
                sl = slice(c * CH, (c + 1) * CH)
                Rx = self.R_xf[c]
                self.acopy(xb[:], self.x_f32[:, :, sl], (Rx,), (Rxb,))
                self.act(xq[:], self.x_f32[:, :, sl], AF.Square, (Rx,), (Rxq,))
                p1, R1 = self.psum()
                for k in range(DC):
                    self.mm(p1, ones_bf, xb[:, k, :], k == 0, k == DC - 1, (Rxb, self.R_consts), (R1,), inc=(k == DC - 1))
                p2, R2 = self.psum()
                for k in range(DC):
                    self.mm(p2, ones_bf, xq[:, k, :], k == 0, k == DC - 1, (Rxq, self.R_consts), (R2,), inc=(k == DC - 1))
                self.ts(mean[:], p1, 1.0 / D, None, ALU.mult, None, (R1,), (Rst,))
                self.tt(bm[:], mean[:], mean[:], ALU.mult, (Rst,), (Rst,))
                self.stt(rstd[:], p2, 1.0 / D, bm[:], ALU.mult, ALU.subtract, (R2, Rst), (Rst,))
                self.act(rstd[:], rstd[:], AF.Ln, (Rst,), (Rst,), bias=self.small[:, 0:1], scale=1.0)
                self.act(rstd[:], rstd[:], AF.Exp, (Rst,), (Rst,), scale=-0.5)
                self.stt(bm[:], mean[:], -1.0, rstd[:], ALU.mult, ALU.mult, (Rst,), (Rst,))
                for m in range(DC):
                    i = m % 2
                    g = self.vecs[:, gcol + m:gcol + m + 1]
                    b = self.vecs[:, gcol + 8 + m:gcol + 8 + m + 1]
                    xf = self.x_f32[:, m, sl]
                    self.stt(u[i][:], xf, g, rstd[:], ALU.mult, ALU.mult, (Rx, Rst, self.R_consts), (Ru[i],))
                    self.stt(v[i][:], bm[:], g, u[i][:], ALU.mult, ALU.add, (Rst, Ru[i]), (Rv[i],))
                    self.act(xf, v[i][:], AF.Identity, (Rv[i],), (Rx,), bias=b, scale=1.0)
                    self.act(self.xT_bf[:, m, sl], v[i][:], AF.Identity, (Rv[i],), (self.R_xb[c],), bias=b, scale=1.0)

    def ffn(self, li, ph):
        cx = self.cx
        wup = self.w["w_up"][li]
        wdn = self.w["w_down"][li]
        if True:
            hT = self.sb(ph, "hT", [128, 8, S], BF16)
            rl = [self.sb(ph, "relu%d" % i, [128, CH], F32) for i in range(2)]
            Rh = [cx.reg("hT%d" % c) for c in range(NCH)]
            Rr = [cx.reg("relu%d" % i) for i in range(2)]
            n = 0
            for g in range(4):
                for wi in range(4):
                    wt, Rw = self.W.get(wup, 0, 1024, g * 1024 + wi * 256, 256)
                    for c in range(NCH):
                        for f in range(2):
                            fi = wi * 2 + f
                            pt, Rp = self.psum()
                            for k in range(DC):
                                self.mm(pt, wt[:, k, f * 128:(f + 1) * 128], self.xT_bf[:, k, c * CH:(c + 1) * CH],
                                        k == 0, k == DC - 1, (Rw, self.R_xb[c]), (Rp,), inc=(k == DC - 1))
                            i = n % 2
                            n += 1
                            self.act(rl[i][:], pt, AF.Relu, (Rp,), (Rr[i],))
                            self.tt(hT[:, fi, c * CH:(c + 1) * CH], rl[i][:], rl[i][:], ALU.mult, (Rr[i],), (Rh[c],))
                for wi in range(4):
                    wt, Rw = self.W.get(wdn, g * 1024, 1024, wi * 256, 256)
                    for c in range(NCH):
                        for mm_ in range(2):
                            m = wi * 2 + mm_
                            pt, Rp = self.psum()
                            for k in range(8):
                                self.mm(pt, wt[:, k, mm_ * 128:(mm_ + 1) * 128], hT[:, k, c * CH:(c + 1) * CH],
                                        k == 0, k == 7, (Rw, Rh[c]), (Rp,), inc=(k == 7))
                            self.resid_acc(m, c, pt, Rp, g == 0)

    def ple(self, li, s):
        cx = self.cx
        wg = self.w["ple_w_gate"][li]
        wp = self.w["ple_w_proj"][li]
        bcol = li * 40 + 32
        with self.phase() as ph:
            pin = [self.sb(ph, "pin%d" % i, [128, 4, 256], F32) for i in range(2)]
            pT = self.sb(ph, "pT", [128, 2, S], BF16)
            gt = [self.sb(ph, "gate%d" % i, [128, CH], F32) for i in range(2)]
            tq = [self.sb(ph, "gprod%d" % i, [128, CH], F32) for i in range(2)]
            Rpin = [cx.reg("pin%d" % i) for i in range(2)]
            RpT = [cx.reg("pT%d" % c) for c in range(NCH)]
            Rg = [cx.reg("gate%d" % i) for i in range(2)]
            Rq = [cx.reg("gprod%d" % i) for i in range(2)]
            for c in range(NCH):
                b = c % 2
                src = self.p_d[li, s, c * CH:(c + 1) * CH, :].rearrange("(t p) f -> p t f", p=128)
                cx.dma("sp", pin[b][:], src, writes=(Rpin[b],))
                for k2 in range(2):
                    pt, Rp = self.psum()
                    for t in range(4):
                        self.tr(pt[:, t * 128:(t + 1) * 128], pin[b][:, t, k2 * 128:(k2 + 1) * 128], (Rpin[b],), (Rp,), inc=(t == 3))
                    self.acopy(pT[:, k2, c * CH:(c + 1) * CH], pt, (Rp,), (RpT[c],))
            wp_ring, Rwp_ring = self.W.get(wp, 0, 256, 0, 1024)
            wpt = self.sb(ph, "wproj", [128, 2, 1024], BF16)
            Rwp = cx.reg("wproj")
            self.vcopy(wpt[:], wp_ring, (Rwp_ring,), (Rwp,))
            n = 0
            for wi in range(4):
                wt, Rw = self.W.get(wg, 0, 1024, wi * 256, 256)
                for c in range(NCH):
                    for mm_ in range(2):
                        m = wi * 2 + mm_
                        i = n % 2
                        n += 1
                        pg, Rpg = self.psum()
                        for k in range(DC):
                            self.mm(pg, wt[:, k, mm_ * 128:(mm_ + 1) * 128], self.xT_bf[:, k, c * CH:(c + 1) * CH],
                                    k == 0, k == DC - 1, (Rw, self.R_xb[c]), (Rpg,), inc=(k == DC - 1))
                        pp, Rpp = self.psum()
                        for k in range(2):
                            self.mm(pp, wpt[:, k, m * 128:(m + 1) * 128], pT[:, k, c * CH:(c + 1) * CH],
                                    k == 0, k == 1, (Rwp, RpT[c]), (Rpp,), inc=(k == 1))
                        self.act(gt[i][:], pg, AF.Sigmoid, (Rpg, self.R_consts), (Rg[i],),
                                 bias=self.vecs[:, bcol + m:bcol + m + 1], scale=1.0)
                        self.tt(tq[i][:], gt[i][:], pp, ALU.mult, (Rg[i], Rpp), (Rq[i],))
                        xf = self.x_f32[:, m, c * CH:(c + 1) * CH]
                        self.tt(xf, xf, tq[i][:], ALU.add, (Rq[i], self.R_xf[c]), (self.R_xf[c],))
            for c in range(NCH):
                for m in range(DC):
                    sl = slice(c * CH, (c + 1) * CH)
                    if m % 2 == 0:
                        self.acopy(self.xT_bf[:, m, sl], self.x_f32[:, m, sl], (self.R_xf[c],), (self.R_xb[c],))
                    else:
                        self.vcopy(self.xT_bf[:, m, sl], self.x_f32[:, m, sl], (self.R_xf[c],), (self.R_xb[c],))

    def attn_causal(self, qT, Rq, kT, Rk, KD, vaug_fn, Rv, scale, kbias_fn, Rkb, orient, dst_fn, Rdst, A, act_recip=False):
        tri = self.cbf[:, 128:256]
        ro = 0 if orient == 0 else 64
        rd = 64 - ro
        steps = [(c, j) for c in range(NCH) for j in range(4 * c + 4)]
        LA = 3
        st = {}
        accs = {}

        def front(i):
            c, j = steps[i]
            lo = max(0, j - 4 * c) * 128
            sp_, Rsp = A["sc"][A["si"] % 4]
            pt_, Rpt = A["pt"][A["si"] % 4]
            A["si"] += 1
            diag = j >= 4 * c
            self.mm(sp_[:, lo:CH], kT[0:KD, j * 128:(j + 1) * 128], qT[0:KD, c * CH + lo:(c + 1) * CH], True, not diag,
                    (Rq, Rk), (Rsp,), inc=not diag)
            if diag:
                self.mm(sp_[:, lo:lo + 128], self.cbf[:, 512:640], self.cbf[:, 384:512], False, True, (self.R_consts,), (Rsp,), inc=True)
            kw = {}
            rds = (Rsp,)
            if kbias_fn is not None:
                kw["bias"] = kbias_fn(j)
                rds = (Rsp, Rkb)
            self.act(pt_[:, lo:CH], sp_[:, lo:CH], AF.Exp, rds, (Rpt,), scale=scale, **kw)
            st[i] = (pt_, Rpt, lo)

        def back(i):
            c, j = steps[i]
            nj = 4 * c + 4
            if j == 0:
                accs[c] = A["acc"][A["ai"] % 2]
                A["ai"] += 1
            acc, Racc = accs[c]
            pt_, Rpt, lo = st.pop(i)
            self.mm(acc[:, lo:CH], vaug_fn(j), pt_[:, lo:CH], j == 0, j == nj - 1, (Rv, Rpt), (Racc,), inc=True)
            if j == nj - 1:
                rec, Rrec = A["rec"][A["ri"] % 2]
                A["ri"] += 1
                if act_recip and c % 2 == 0:
                    self.act(rec[ro:ro + 64, :], acc[rd:rd + 64, :], AF.Ln, (Racc,), (Rrec,))
                    self.act(rec[ro:ro + 64, :], rec[ro:ro + 64, :], AF.Exp, (Rrec,), (Rrec,), scale=-1.0)
                else:
                    self.recip(rec[ro:ro + 64, :], acc[rd:rd + 64, :], (Racc,), (Rrec,))
                self.tt(dst_fn(c), acc[ro:ro + 64, :], rec[ro:ro + 64, :], ALU.mult, (Racc, Rrec), (Rdst,))

        n = len(steps)
        for i in range(n + LA):
            if i < n:
                front(i)
            if i >= LA:
                back(i - LA)

    def attn_bufs(self, ph):
        cx = self.cx
        A = {"ai": 0, "si": 0, "ri": 0}
        A["sc"] = [(self.ps[:, b, :], self.R_ps[b]) for b in range(4)]
        A["acc"] = [(self.ps[:, 4 + b, :], self.R_ps[4 + b]) for b in range(2)]
        A["pt"] = []
        for i in range(4):
            A["pt"].append((self.sb(ph, "ptile%d" % i, [128, CH], BF16), cx.reg("ptile%d" % i)))
        A["rec"] = []
        for i in range(2):
            A["rec"].append((self.sb(ph, "rec%d" % i, [128, CH], F32), cx.reg("rec%d" % i)))
        return A

    def psum67(self):
        b = 6 + (self.psi % 2)
        self.psi += 1
        return self.ps[:, b, :], self.R_ps[b]

    def odd_mixer(self, li):
        cx = self.cx
        j = li // 2
        win = self.w["od_w_in"][j]
        tri_f = self.consts[:, 128:256]
        with self.phase() as ph:
            A = self.attn_bufs(ph)
            ltok = self.sb(ph, "ltok", [128, 256], F32)
            negc = self.sb(ph, "negc", [128, 256], F32)
            Rl, Rn = cx.reg("ltok"), cx.reg("negc")
            wt, Rw = self.W.get(win, 0, 1024, 3072, 16)
            pf, Rpf = self.psum()
            brow = self.rows[0:1, j * 16:(j + 1) * 16]
            for t in range(NT):
                for k in range(DC):
                    self.mm(pf[:, t * 16:(t + 1) * 16], self.xT_bf[:, k, t * 128:(t + 1) * 128], wt[:, k, :], k == 0, False,
                            (Rw, self.R_xb[t // 4]), (Rpf,), inc=False)
                self.mm(pf[:, t * 16:(t + 1) * 16], self.onesf[0:1, 0:128], brow, False, True, (self.R_consts,), (Rpf,), inc=True)
            self.act(ltok[:], pf[:, 0:256], AF.Exp, (Rpf,), (Rl,), scale=-1.0)
            self.act(ltok[:], ltok[:], AF.Ln, (Rl, self.R_consts), (Rl,), bias=self.small[:, 1:2], scale=1.0)
            l2 = self.sb(ph, "l2", [128, 256], F32)
            Rl2 = cx.reg("l2")
            self.memset(l2[:, 0:16], 0.0, (Rl2,))
            for i in range(1, NT):
                self.tt(l2[:, i * 16:(i + 1) * 16], l2[:, (i - 1) * 16:i * 16], ltok[:, (i - 1) * 16:i * 16], ALU.add,
                        (Rl2, Rl), (Rl2,))
            pc, Rpc = self.psum()
            self.mm(pc[:, 0:256], tri_f, ltok[:], True, False, (Rl, self.R_consts), (Rpc,), inc=False)
            self.mm(pc[:, 0:256], self.onesf[:, :], l2[:], False, True, (Rl2, self.R_consts), (Rpc,), inc=True)
            self.vcopy(negc[:], pc[:, 0:256], (Rpc,), (Rn,))
            with self.phase() as ph2:
                v0 = self.sb(ph2, "c_v0", [16, CH], F32)
                r1 = self.sb(ph2, "c_r1", [16, CH], F32)
                cs3 = [self.sb(ph2, "c_split%d" % i, [16, 3, CH], BF16) for i in range(2)]
                Rc0 = cx.reg("c_v0")
                Rc3 = [cx.reg("c_split%d" % i) for i in range(2)]
                for c in range(NCH):
                    pct, Rpct = self.psum()
                    for tt_ in range(4):
                        i = c * 4 + tt_
                        self.tr(pct[0:16, tt_ * 128:(tt_ + 1) * 128], negc[:, i * 16:(i + 1) * 16], (Rn,), (Rpct,), inc=(tt_ == 3))
                    b = c % 2
                    self.ts(v0[:], pct[0:16, :], -1.0, None, ALU.mult, None, (Rpct,), (Rc0,))
                    self.vcopy(cs3[b][:, 0, :], v0[:], (Rc0,), (Rc3[b],))
                    self.tt(r1[:], v0[:], cs3[b][:, 0, :], ALU.subtract, (Rc0, Rc3[b]), (Rc0,))
                    self.vcopy(cs3[b][:, 1, :], r1[:], (Rc0,), (Rc3[b],))
                    self.tt(r1[:], r1[:], cs3[b][:, 1, :], ALU.subtract, (Rc0, Rc3[b]), (Rc0,))
                    self.vcopy(cs3[b][:, 2, :], r1[:], (Rc0,), (Rc3[b],))
                    cx.dma("sp", self.cscr_d[:, :, c * CH:(c + 1) * CH], cs3[b][:], reads=(Rc3[b],), writes=(self.R_cscr,))
            vaug = self.sb(ph, "fx_vaug", [128, NT, 384], BF16)
            Rva = cx.reg("fx_vaug")
            qT = [self.sb(ph, "fx_qT%d" % i, [128, S], BF16) for i in range(2)]
            kT = [self.sb(ph, "fx_kT%d" % i, [128, S], BF16) for i in range(2)]
            RqT = [cx.reg("fx_qT%d" % i) for i in range(2)]
            RqA = [cx.reg("fx_qTa%d" % i) for i in range(2)]
            RkT = [cx.reg("fx_kT%d" % i) for i in range(2)]
            osc = self.sb(ph, "fx_osc", [128, 2, S], BF16)
            Ros = cx.reg("fx_osc")
            for i in range(2):
                self.memset(kT[i][64:67, :], 8.0, (RkT[i],))
            for pr in range(2):
                self.memset(vaug[:, :, pr * 192 + 64:pr * 192 + 128], 1.0, (Rva,))
            for G in range(4):
                wt, Rw = self.W.get(win, 0, 1024, 2048 + G * 256, 256)
                for t in range(NT):
                    pv, Rpv = self.psum()
                    for k in range(DC):
                        self.mm(pv[:, 0:256], self.xT_bf[:, k, t * 128:(t + 1) * 128], wt[:, k, :], k == 0, k == DC - 1,
                                (Rw, self.R_xb[t // 4]), (Rpv,), inc=(k == DC - 1))
                    for hh in range(2):
                        src = pv[:, 0:256].rearrange("p (a b) -> p a b", a=2)[:, :, hh * 64:hh * 64 + 64]
                        dst = vaug[:, t, :].rearrange("p (a b) -> p a b", a=2)[:, :, hh * 128:hh * 128 + 64]
                        if hh == 0:
                            self.acopy(dst, src, (Rpv,), (Rva,))
                        else:
                            self.vcopy(dst, src, (Rpv,), (Rva,))
                wq, Rwq = self.W.get(win, 0, 1024, G * 256, 256)
                wk, Rwk = self.W.get(win, 0, 1024, 1024 + G * 256, 256)
                for pr in range(2):
                    for (wt_, Rw_, dstT, Rd) in ((wq, Rwq, qT, RqT), (wk, Rwk, kT, RkT)):
                        for c in range(NCH):
                            pq, Rpq = self.psum()
                            for k in range(DC):
                                self.mm(pq, wt_[:, k, pr * 128:(pr + 1) * 128], self.xT_bf[:, k, c * CH:(c + 1) * CH],
                                        k == 0, k == DC - 1, (Rw_, self.R_xb[c]), (Rpq,), inc=(k == DC - 1))
                            self.acopy(dstT[0][0:64, c * CH:(c + 1) * CH], pq[0:64, :], (Rpq,), (Rd[0],))
                            self.vcopy(dstT[1][0:64, c * CH:(c + 1) * CH], pq[64:128, :], (Rpq,), (Rd[1],))
                    for hh in range(2):
                        h = G * 4 + pr * 2 + hh
                        for jx in range(3):
                            cx.dma("sp", qT[hh][64 + jx:65 + jx, :], self.cscr_d[h:h + 1, jx, :], reads=(self.R_cscr,), writes=(RqA[hh],))
                    for hh in range(2):
                        h = G * 4 + pr * 2 + hh
                        i4 = pr * 2 + hh
                        vo = pr * 192 + hh * 64
                        self._attn_fox(qT, RqT, RqA, kT, RkT, vaug, Rva, negc, Rn, osc, Ros, A, pr, hh, h)
                self.out_proj_partial("od_w_out", j, G * 256, 2, (lambda k, c: osc[:, k, c * CH:(c + 1) * CH]), (Ros,), G == 0)

    def _attn_fox(self, qT, RqT, RqA, kT, RkT, vaug, Rva, negc, Rn, osc, Ros, A, pr, hh, h):
        vo = pr * 192 + hh * 64
        cxq = _Multi((RqT[hh], RqA[hh]))
        self.attn_causal(qT[hh], cxq, kT[hh], RkT[hh], 67,
                         (lambda jt: vaug[:, jt, vo:vo + 128]), Rva, 0.125,
                         (lambda jt: negc[:, jt * 16 + h:jt * 16 + h + 1]), Rn, hh,
                         (lambda c: osc[hh * 64:hh * 64 + 64, pr, c * CH:(c + 1) * CH]), Ros, A)

    def even_mixer(self, li):
        cx = self.cx
        j = li // 2
        win = self.w["ev_w_in"][j]
        vb = 160 + j * 16
        ones_bf = self.cbf[:, 0:128]
        first = [True]
        with self.phase() as ph:
            cqn = self.sb(ph, "cqn", [128, 3, S], BF16)
            ckvn = self.sb(ph, "ckvn", [128, 2, S], BF16)
            kT = [self.sb(ph, "ml_kT%d" % i, [128, S], BF16) for i in range(2)]
            Rcq = [cx.reg("cqn%d" % c) for c in range(NCH)]
            Rckv = [cx.reg("ckvn%d" % c) for c in range(NCH)]
            RkT = [cx.reg("ml_kT%d" % i) for i in range(2)]
            RkR = [cx.reg("ml_kTr%d" % i) for i in range(2)]
            with self.phase() as p1:
                raw = self.sb(p1, "lat_raw", [128, 3, CH], F32)
                sq = self.sb(p1, "lat_sq", [128, 3, CH], BF16)
                rstd = self.sb(p1, "lat_rstd", [128, CH], F32)
                wrot = self.sb(p1, "wkr_rot", [128, 8, 32], BF16)
                t1 = self.sb(p1, "kr_t1", [128, CH], F32)
                t2 = self.sb(p1, "kr_t2", [128, CH], F32)
                Rraw, Rsq, Rrs, Rwr, Rt1, Rt2 = (cx.reg(n) for n in ("lat_raw", "lat_sq", "lat_rstd", "wkr_rot", "kr_t1", "kr_t2"))
                for c in range(NCH):
                    sl = slice(c * CH, (c + 1) * CH)
                    wA, RwA = self.W.get(win, 0, 1024, 0, 256)
                    wB, RwB = self.W.get(win, 0, 1024, 256, 256)
                    wC, RwC = self.W.get(win, 0, 1024, 512, 160)
                    srcs = {0: (wA, RwA, 0), 1: (wA, RwA, 128), 2: (wB, RwB, 0), 3: (wB, RwB, 128), 4: (wC, RwC, 0)}
                    for (lat, ms, nfeat, dst, Rd, ncol) in ((0, (0, 1, 2), 384.0, cqn, Rcq, vb), (1, (3, 4), 256.0, ckvn, Rckv, vb + 3)):
                        for mi, m in enumerate(ms):
                            wt_, Rw_, co = srcs[m]
                            pt, Rp = self.psum()
                            for k in range(DC):
                                self.mm(pt, wt_[:, k, co:co + 128], self.xT_bf[:, k, sl], k == 0, k == DC - 1,
                                        (Rw_, self.R_xb[c]), (Rp,), inc=(k == DC - 1))
                            self.acopy(raw[:, mi, :], pt, (Rp,), (Rraw,))
                            self.tt(sq[:, mi, :], raw[:, mi, :], raw[:, mi, :], ALU.mult, (Rraw,), (Rsq,))
                        pss, Rpss = self.psum()
                        for mi in range(len(ms)):
                            self.mm(pss, ones_bf, sq[:, mi, :], mi == 0, mi == len(ms) - 1, (Rsq, self.R_consts), (Rpss,),
                                    inc=(mi == len(ms) - 1))
                        self.act(rstd[:], pss, AF.Ln, (Rpss, self.R_consts), (Rrs,), bias=self.small[:, 0:1], scale=1.0 / nfeat)
                        self.act(rstd[:], rstd[:], AF.Exp, (Rrs,), (Rrs,), scale=-0.5)
                        for mi in range(len(ms)):
                            self.stt(dst[:, mi, sl], raw[:, mi, :], self.vecs[:, ncol + mi:ncol + mi + 1], rstd[:], ALU.mult, ALU.mult,
                                     (Rraw, Rrs, self.R_consts), (Rd[c],))
                    if c == 0:
                        self.ts(wrot[:, :, 0:16], wC[:, :, 144:160], -1.0, None, ALU.mult, None, (RwC,), (Rwr,))
                        self.vcopy(wrot[:, :, 16:32], wC[:, :, 128:144], (RwC,), (Rwr,))
                    pk, Rpk = self.psum()
                    for k in range(DC):
                        self.mm(pk[0:32, :], wC[:, k, 128:160], self.xT_bf[:, k, sl], k == 0, k == DC - 1, (RwC, self.R_xb[c]), (Rpk,),
                                inc=(k == DC - 1))
                    pr_, Rpr = self.psum()
                    for k in range(DC):
                        self.mm(pr_[0:32, :], wrot[:, k, :], self.xT_bf[:, k, sl], k == 0, k == DC - 1, (Rwr, self.R_xb[c]), (Rpr,),
                                inc=(k == DC - 1))
                    self.tt(t1[0:32, :], pk[0:32, :], self.cs[0:32, c * CH:(c + 1) * CH], ALU.mult, (Rpk, self.R_cs), (Rt1,))
                    self.tt(t2[0:32, :], pr_[0:32, :], self.cs[0:32, S + c * CH:S + (c + 1) * CH], ALU.mult, (Rpr, self.R_cs), (Rt2,))
                    self.tt(kT[0][64:96, sl], t1[0:32, :], t2[0:32, :], ALU.add, (Rt1, Rt2), (RkR[0],))
                    self.acopy(kT[1][64:96, sl], kT[0][64:96, sl], (RkR[0],), (RkR[1],))
            with self.phase() as p2:
                A = None
                qs = self.sb(p2, "sw_q", [128, 2, S], BF16)
                ks2 = self.sb(p2, "sw_k2", [128, S], BF16)
                vs = self.sb(p2, "sw_v", [128, NT, 128], BF16)
                wks2 = self.sb(p2, "sw_wk2", [128, 8, 128], BF16)
                osc = self.sb(p2, "sw_osc", [128, 2, S], BF16)
                pp = [self.sb(p2, "sw_p%d" % i, [128, 2, CH], BF16) for i in range(2)]
                rec = [self.sb(p2, "sw_rec%d" % i, [128, CH], F32) for i in range(2)]
                Rqs, Rks, Rvs, Rwk2, Ros = (cx.reg(n) for n in ("sw_q", "sw_k2", "sw_v", "sw_wk2", "sw_osc"))
                Rpp = [cx.reg("sw_p%d" % i) for i in range(2)]
                Rrec = [cx.reg("sw_rec%d" % i) for i in range(2)]
                self.memset(vs[:, :, 64:128], 1.0, (Rvs,))
                nb = 0
                for g in range(2):
                    wq, Rwq = self.W.get(win, 0, 1024, EV_COLS["qs"] + g * 256, 256)
                    for c in range(NCH):
                        for m in range(2):
                            pt, Rp = self.psum()
                            for k in range(DC):
                                self.mm(pt, wq[:, k, m * 128:(m + 1) * 128], self.xT_bf[:, k, c * CH:(c + 1) * CH], k == 0, k == DC - 1,
                                        (Rwq, self.R_xb[c]), (Rp,), inc=(k == DC - 1))
                            self.acopy(qs[:, m, c * CH:(c + 1) * CH], pt, (Rp,), (Rqs,))
                    wkv, Rwkv = self.W.get(win, 0, 1024, EV_COLS["ks"], 256)
                    for rep in range(2):
                        self.vcopy(wks2[:, :, rep * 64:(rep + 1) * 64], wkv[:, :, g * 64:(g + 1) * 64], (Rwkv,), (Rwk2,))
                    for c in range(NCH):
                        pt, Rp = self.psum()
                        for k in range(DC):
                            self.mm(pt, wks2[:, k, :], self.xT_bf[:, k, c * CH:(c + 1) * CH], k == 0, k == DC - 1,
                                    (Rwk2, self.R_xb[c]), (Rp,), inc=(k == DC - 1))
                        self.vcopy(ks2[:, c * CH:(c + 1) * CH], pt, (Rp,), (Rks,))
                    for t8 in range(2):
                        pt, Rp = self.psum()
                        for tt_ in range(8):
                            t = t8 * 8 + tt_
                            for k in range(DC):
                                self.mm(pt[:, tt_ * 64:(tt_ + 1) * 64], self.xT_bf[:, k, t * 128:(t + 1) * 128],
                                        wkv[:, k, 128 + g * 64:128 + (g + 1) * 64], k == 0, k == DC - 1,
                                        (Rwkv, self.R_xb[t // 4]), (Rp,), inc=(k == DC - 1))
                        self.acopy(vs[:, t8 * 8:(t8 + 1) * 8, 0:64], pt.rearrange("p (a b) -> p a b", a=8), (Rp,), (Rvs,))
                    def sw_front(n, i):
                        kts = (1,) if n == 0 else (0, 1)
                        ident_bf = self.cbf[:, 512:640]
                        for kt in kts:
                            ktile = n - 1 + kt
                            for half in range(2):
                                b = kt * 2 + half
                                sp_, Rsp = self.ps[:, b, :], self.R_ps[b]
                                for jc in range(2):
                                    hi = 2 * jc + half
                                    o = (kt * 8 + 4 * g + hi) * 128
                                    self.mm(sp_[:, jc * 128:(jc + 1) * 128], ks2[half * 64:(half + 1) * 64, ktile * 128:(ktile + 1) * 128],
                                            qs[half * 64:(half + 1) * 64, jc, n * 128:(n + 1) * 128], True, False, (Rks, Rqs), (Rsp,), inc=False)
                                    self.mm(sp_[:, jc * 128:(jc + 1) * 128], ident_bf, self.b8m[:, o:o + 128], False, True,
                                            (self.R_ebm, self.R_consts), (Rsp,), inc=(jc == 1))
                                dst = pp[i][:, kt, :].rearrange("p (a b) -> p a b", a=4)[:, half:4:2, :]
                                self.act(dst, sp_[:, 0:256].rearrange("p (a b) -> p a b", a=2), AF.Exp, (Rsp,), (Rpp[i],), scale=0.125)

                    def sw_back(n, i):
                        kts = (1,) if n == 0 else (0, 1)
                        ab = 4 + i
                        acc, Racc = self.ps[:, ab, :], self.R_ps[ab]
                        for ki, kt in enumerate(kts):
                            ktile = n - 1 + kt
                            self.mm(acc, vs[:, ktile, :], pp[i][:, kt, :], ki == 0, ki == len(kts) - 1, (Rvs, Rpp[i]), (Racc,),
                                    inc=(ki == len(kts) - 1))
                        for hi in range(4):
                            sc = vb + 5 + 4 * g + hi
                            self.ts(rec[i][0:64, hi * 128:(hi + 1) * 128], acc[64:128, hi * 128:(hi + 1) * 128],
                                    self.vecs[64:128, sc:sc + 1], None, ALU.add, None, (Racc, self.R_consts), (Rrec[i],))
                        self.act(rec[i][0:64, :], rec[i][0:64, :], AF.Ln, (Rrec[i],), (Rrec[i],))
                        self.act(rec[i][0:64, :], rec[i][0:64, :], AF.Exp, (Rrec[i],), (Rrec[i],), scale=-1.0)
                        for half in range(2):
                            src = acc[0:64, :].rearrange("p (a b) -> p a b", a=4)[:, half:4:2, :]
                            rcs = rec[i][0:64, :].rearrange("p (a b) -> p a b", a=4)[:, half:4:2, :]
                            dst = osc[half * 64:(half + 1) * 64, :, n * 128:(n + 1) * 128]
                            self.tt(dst, src, rcs, ALU.mult, (Racc, Rrec[i]), (Ros,))

                    for n in range(NT + 1):
                        if n < NT:
                            sw_front(n, n % 2)
                        if n >= 1:
                            sw_back(n - 1, (n - 1) % 2)
                    self.out_proj_partial("ev_w_out", j, 512 + g * 256, 2, (lambda k, c: osc[:, k, c * CH:(c + 1) * CH]), (Ros,), first[0])
                    first[0] = False
            with self.phase() as p3:
                A = self.attn_bufs(p3)
                qT = [self.sb(p3, "ml_qT%d" % i, [128, S], BF16) for i in range(2)]
                vaug = self.sb(p3, "ml_vaug", [128, NT, 192], BF16)
                osc = self.sb(p3, "ml_osc", [128, 2, S], BF16)
                wqrot = self.sb(p3, "ml_wqrot", [128, 3, 2, 32], BF16)
                t1 = self.sb(p3, "ml_t1", [128, CH], F32)
                t2 = self.sb(p3, "ml_t2", [128, CH], F32)
                RqT = [cx.reg("ml_qT%d" % i) for i in range(2)]
                Rva, Ros, Rwr, Rt1, Rt2 = (cx.reg(n) for n in ("ml_vaug", "ml_osc", "ml_wqrot", "ml_t1", "ml_t2"))
                self.memset(vaug[:, :, 64:128], 1.0, (Rva,))
                scale = float(96.0 ** -0.5)
                for pr in range(4):
                    wkv, Rwkv = self.W.get(self.w["ev_w_ukv"][j], 0, 256, 0, 1024)
                    for t4 in range(4):
                        pv, Rpv = self.psum()
                        for tt_ in range(4):
                            t = t4 * 4 + tt_
                            for hh in range(2):
                                h = pr * 2 + hh
                                for k in range(2):
                                    self.mm(pv[:, tt_ * 128 + hh * 64:tt_ * 128 + hh * 64 + 64], ckvn[:, k, t * 128:(t + 1) * 128],
                                            wkv[:, k, h * 128 + 64:h * 128 + 128], k == 0, k == 1, (Rwkv, Rckv[t // 4]), (Rpv,),
                                            inc=(k == 1 and hh == 1 and tt_ == 3))
                        for hh in range(2):
                            src = pv.rearrange("p (a b) -> p a b", a=4)[:, :, hh * 64:hh * 64 + 64]
                            dst = vaug[:, t4 * 4:(t4 + 1) * 4, hh * 128:hh * 128 + 64]
                            if hh == 0:
                                self.acopy(dst, src, (Rpv,), (Rva,))
                            else:
                                self.vcopy(dst, src, (Rpv,), (Rva,))
                    for hh in range(2):
                        h = pr * 2 + hh
                        for c in range(NCH):
                            pk, Rpk = self.psum()
                            for k in range(2):
                                self.mm(pk[0:64, :], wkv[:, k, h * 128:h * 128 + 64], ckvn[:, k, c * CH:(c + 1) * CH], k == 0, k == 1,
                                        (Rwkv, Rckv[c]), (Rpk,), inc=(k == 1))
                            self.acopy(kT[hh][0:64, c * CH:(c + 1) * CH], pk[0:64, :], (Rpk,), (RkT[hh],))
                    wq, Rwq = self.W.get(self.w["ev_w_uq"][j], 0, 384, pr * 192, 192)
                    for hh in range(2):
                        self.ts(wqrot[:, :, hh, 0:16], wq[:, :, hh * 96 + 80:hh * 96 + 96], -1.0, None, ALU.mult, None, (Rwq,), (Rwr,))
                        self.vcopy(wqrot[:, :, hh, 16:32], wq[:, :, hh * 96 + 64:hh * 96 + 80], (Rwq,), (Rwr,))
                    for hh in range(2):
                        for c in range(NCH):
                            sl = slice(c * CH, (c + 1) * CH)
                            pq, Rpq = self.psum()
                            for k in range(3):
                                self.mm(pq[0:96, :], wq[:, k, hh * 96:(hh + 1) * 96], cqn[:, k, sl], k == 0, k == 2, (Rwq, Rcq[c]), (Rpq,),
                                        inc=(k == 2))
                            prr, Rprr = self.psum()
                            for k in range(3):
                                self.mm(prr[0:32, :], wqrot[:, k, hh, :], cqn[:, k, sl], k == 0, k == 2, (Rwr, Rcq[c]), (Rprr,), inc=(k == 2))
                            self.acopy(qT[hh][0:64, sl], pq[0:64, :], (Rpq,), (RqT[hh],))
                            self.tt(t1[64:96, :], pq[64:96, :], self.cs[64:96, c * CH:(c + 1) * CH], ALU.mult, (Rpq, self.R_cs), (Rt1,))
                            self.tt(t2[64:96, :], prr[0:32, :], self.cs[0:32, S + c * CH:S + (c + 1) * CH], ALU.mult, (Rprr, self.R_cs), (Rt2,))
                            self.tt(qT[hh][64:96, sl], t1[64:96, :], t2[64:96, :], ALU.add, (Rt1, Rt2), (RqT[hh],))
                    for hh in range(2):
                        vo = hh * 64
                        self.attn_causal(qT[hh], RqT[hh], kT[hh], _Multi((RkT[hh], RkR[hh])), 96,
                                         (lambda jt, vo=vo: vaug[:, jt, vo:vo + 128]), Rva, scale, None, None, hh,
                                         (lambda c, hh=hh, pr=pr: osc[hh * 64:hh * 64 + 64, pr % 2, c * CH:(c + 1) * CH]), Ros, A,
                                         act_recip=True)
                    if pr % 2 == 1:
                        self.out_proj_partial("ev_w_out", j, (pr - 1) * 128, 2, (lambda k, c: osc[:, k, c * CH:(c + 1) * CH]), (Ros,), first[0])
                        first[0] = False


class _Multi:
    def __init__(self, parts):
        self.parts = parts


def _expand(regs):
    out = []
    for r in regs:
        if isinstance(r, _Multi):
            out.extend(r.parts)
        else:
            out.append(r)
    return out


_orig_collect = Ctx._collect
_orig_record = Ctx._record


def _collect2(self, reads, writes, own=()):
    return _orig_collect(self, _expand(reads), _expand(writes), own)


def _record2(self, ev, reads, writes):
    return _orig_record(self, ev, _expand(reads), _expand(writes))


Ctx._collect = _collect2
Ctx._record = _record2


def _t5_bucket(dist):
    exact = 16
    d = np.maximum(dist, 1).astype(np.float32)
    large = exact + (np.log(d / np.float32(exact)) / np.float32(math.log(128 / exact)) * np.float32(32 - exact)).astype(np.int32)
    large = np.minimum(large, 31)
    return np.where(dist < exact, dist, large)


def host_consts(inputs):
    consts = np.zeros((128, 384), np.float32)
    consts[:, 0:128] = np.eye(128, dtype=np.float32)
    s_idx = np.arange(128)[:, None]
    t_idx = np.arange(128)[None, :]
    consts[:, 128:256] = (s_idx <= t_idx).astype(np.float32)
    consts[:, 256:384] = (s_idx > t_idx).astype(np.float32)
    vecs = np.zeros((128, 192), np.float32)

    def pc(v):
        return np.ascontiguousarray(v.reshape(-1, 128).T)
    for i in range(DEPTH):
        b = i * 40
        vecs[:, b + 0:b + 8] = pc(inputs["ln1_g"][i])
        vecs[:, b + 8:b + 16] = pc(inputs["ln1_b"][i])
        vecs[:, b + 16:b + 24] = pc(inputs["ln2_g"][i])
        vecs[:, b + 24:b + 32] = pc(inputs["ln2_b"][i])
        vecs[:, b + 32:b + 40] = pc(inputs["ple_b_gate"][i])
    for j in range(2):
        b = 160 + j * 16
        vecs[:, b:b + 3] = pc(inputs["ev_q_norm"][j])
        vecs[:, b + 3:b + 5] = pc(inputs["ev_kv_norm"][j])
    rows = np.zeros((1, 32), np.float32)
    for j in range(2):
        vecs[:, 160 + j * 16 + 5:160 + j * 16 + 13] = inputs["ev_sinks"][j][None, :]
        rows[0, j * 16:(j + 1) * 16] = inputs["od_b_f"][j]
    inv = (1.0 / (np.float32(10000.0) ** (np.arange(0, 32, 2, dtype=np.float32) / np.float32(32)))).astype(np.float32)
    ang = (np.arange(S, dtype=np.float32)[None, :] * inv[:, None]).astype(np.float32)
    cs = np.zeros((128, 2 * S), np.float32)
    cs[:, 0:S] = np.tile(np.cos(ang), (8, 1))
    cs[:, S:2 * S] = np.tile(np.sin(ang), (8, 1))
    kk = np.arange(128)[:, None]
    a = np.arange(128)[None, :]
    biasg = np.zeros((128, 2, 8, 128), np.float32)
    for kt in range(2):
        dist = (a + 128 - kk) if kt == 0 else (a - kk)
        bk = _t5_bucket(np.maximum(dist, 0))
        g = inputs["rel_bias"][bk]
        biasg[:, kt] = np.transpose(g, (0, 2, 1))
    return dict(consts=consts, vecs=vecs, rows=rows, cs=cs, biasg=np.ascontiguousarray(biasg.reshape(128, 2048)))


WEIGHT_NAMES = ["ev_w_in", "ev_w_uq", "ev_w_ukv", "ev_w_out", "od_w_in", "od_w_out", "w_up", "w_down", "ple_w_proj", "ple_w_gate"]

_CACHE = {}


def get_builder(nseq, layers, stop=None):
    key = (nseq, tuple(layers), stop)
    if key not in _CACHE:
        _CACHE[key] = Builder(nseq, layers, stop)
    return _CACHE[key]


def run(inputs, n_cores, nseq, layers=(0, 1, 2, 3), stop=None, batch0=0):
    bld = get_builder(nseq, layers, stop)
    hc = host_consts(inputs)
    in_maps = []
    for c in range(n_cores):
        b0 = batch0 + c * nseq
        m = {"x": np.ascontiguousarray(inputs["x"][b0:b0 + nseq]),
             "p": np.ascontiguousarray(inputs["p"][:, b0:b0 + nseq])}
        for wn in WEIGHT_NAMES:
            m[wn] = np.ascontiguousarray(inputs[wn])
        m.update(hc)
        in_maps.append(m)
    res = run_bass_kernel_spmd(bld.nc, in_maps, core_ids=list(range(n_cores)))
    return np.concatenate([np.asarray(r["out"]) for r in res.results], axis=0)


def kernel(**inputs):
    inputs = {k: np.asarray(v) for k, v in inputs.items()}
    out = run(inputs, N_CORES, SEQ_PER_CORE)
    return out.astype(np.float32)
```

```python
import math
from contextlib import ExitStack, contextmanager

import numpy as np
import concourse.bass as bass
import concourse.mybir as mybir
from concourse.bass_utils import run_bass_kernel_spmd

F32 = mybir.dt.float32
BF16 = mybir.dt.bfloat16
AF = mybir.ActivationFunctionType
ALU = mybir.AluOpType

S = 2048
D = 1024
NT = 16
NCH = 4
CH = 512
DC = 8
DEPTH = 4
DFF = 4096
ALPHA = float((2 * DEPTH) ** 0.25)
EPS = 1e-5
N_CORES = 8
SEQ_PER_CORE = 4
SEM_LIMIT = 30000
WSLOT = 2048
NWSLOT = 5
WHOLD = 3

EV_COLS = dict(cq=0, ckv=384, kr=640, qs=672, ks=1184, vs=1312)


class Reg:
    __slots__ = ("name", "w", "r", "dsem", "dcnt", "excl")

    def __init__(self, name):
        self.name = name
        self.excl = False
        self.w = None
        self.r = {}
        self.dsem = None
        self.dcnt = 0


class Eng:
    def __init__(self, name, eng):
        self.name = name
        self.eng = eng
        self.sem = None
        self.cnt = 0
        self.seen = {}
        self.pending = False
        self.own = set()


class Ctx:
    def __init__(self, nc, es):
        self.nc = nc
        self.es = es
        self.dry = False
        self.nsem = 0
        self.E = {
            "pe": Eng("pe", nc.tensor),
            "act": Eng("act", nc.scalar),
            "dve": Eng("dve", nc.vector),
            "pool": Eng("pool", nc.gpsimd),
            "sp": Eng("sp", nc.sync),
        }
        for e in self.E.values():
            self._new_clock(e)
        self.all_regs = []
        self.dsem_pool = []

    def recycle(self, mark):
        for r in self.all_regs[mark:]:
            if r.dsem is not None:
                self.dsem_pool.append((r.dsem, r.dcnt))
                r.dsem = None
        del self.all_regs[mark:]

    def new_sem(self, name):
        self.nsem += 1
        return self.es.enter_context(self.nc.semaphore("%s_%d" % (name, self.nsem)))

    def _new_clock(self, e):
        e.sem = self.new_sem("clk_" + e.name)
        e.cnt = 0
        e.own.add(id(e.sem))

    def reg(self, name):
        r = Reg(name)
        self.all_regs.append(r)
        return r

    def _wait(self, E, evs):
        best = {}
        for (sem, val) in evs:
            k = id(sem)
            if k not in best or best[k][1] < val:
                best[k] = (sem, val)
        for k, (sem, val) in best.items():
            if k in E.own and E.name == "pe":
                continue
            if E.seen.get(k, 0) >= val:
                continue
            E.eng.wait_ge(sem, val)
            E.seen[k] = val

    def _collect(self, reads, writes, own=()):
        evs = []
        for r in reads:
            if r.w is not None:
                evs.append(r.w)
            if r.excl:
                evs.extend(e for e in r.r.values() if id(e[0]) not in own)
        for w in writes:
            if w.w is not None and id(w.w[0]) not in own:
                evs.append(w.w)
            evs.extend(e for e in w.r.values() if id(e[0]) not in own)
        return evs

    def _record(self, ev, reads, writes):
        k = id(ev[0])
        for r in reads:
            old = r.r.get(k)
            if old is None or old[1] < ev[1]:
                r.r[k] = ev
        for w in writes:
            w.w = ev
            w.r = {}

    def op(self, en, fn, reads=(), writes=(), inc=True):
        if self.dry:
            return
        E = self.E[en]
        self._wait(E, self._collect(reads, writes, E.own))
        ins = fn()
        if inc:
            if E.cnt >= SEM_LIMIT and not E.pending:
                self._new_clock(E)
            E.cnt += 1
            ins.then_inc(E.sem, 1)
            E.pending = False
            ev = (E.sem, E.cnt)
        else:
            E.pending = True
            ev = (E.sem, E.cnt + 1)
        self._record(ev, reads, writes)

    def dma(self, q, out, in_, reads=(), writes=(), sem_reg=None):
        if self.dry:
            return
        E = self.E[q]
        self._wait(E, self._collect(reads, writes))
        sr = sem_reg or (writes[0] if writes else reads[0])
        if sr.dsem is None:
            if self.dsem_pool:
                sr.dsem, sr.dcnt = self.dsem_pool.pop()
            else:
                sr.dsem = self.new_sem("dma")
                sr.dcnt = 0
        ins = E.eng.dma_start(out=out, in_=in_)
        sr.dcnt += 16
        ins.then_inc(sr.dsem, 16)
        ev = (sr.dsem, sr.dcnt)
        self._record(ev, reads, writes)

    def barrier(self):
        if self.dry:
            return
        evs = []
        for e in self.E.values():
            if e.cnt > 0:
                evs.append((e.sem, e.cnt))
        for r in self.all_regs:
            if r.w is not None:
                evs.append(r.w)
            evs.extend(r.r.values())
        for e in self.E.values():
            self._wait(e, evs)

    def final_wait(self, regs):
        if self.dry:
            return
        evs = []
        for r in regs:
            if r.w is not None:
                evs.append(r.w)
            evs.extend(r.r.values())
        for e in self.E.values():
            if e.cnt > 0:
                evs.append((e.sem, e.cnt))
        self._wait(self.E["sp"], evs)


class WStream:
    def __init__(self, cx, nc, es):
        self.cx = cx
        self.slots = [es.enter_context(nc.sbuf_tensor("s_wslot%d" % i, [128, WSLOT], BF16)) for i in range(NWSLOT)]
        self.regs = [cx.reg("wslot%d" % i) for i in range(NWSLOT)]
        for r in self.regs:
            r.dsem = cx.new_sem("wdma")
            r.dcnt = 0
        self.plan = []
        self.pos = 0
        self.loaded = 0

    def reset(self):
        self.pos = 0
        self.loaded = 0

    def _view(self, i, kc, ncols):
        return self.slots[i % NWSLOT][:, 0:kc * ncols].rearrange("p (k n) -> p k n", k=kc)

    def _load(self, j):
        (w, r0, nrows, c0, ncols) = self.plan[j]
        kc = nrows // 128
        src = w[r0:r0 + nrows, c0:c0 + ncols].rearrange("(k p) n -> p k n", p=128)
        self.cx.dma("pool", self._view(j, kc, ncols), src, reads=(), writes=(self.regs[j % NWSLOT],))

    def get(self, w, r0, nrows, c0, ncols):
        kc = nrows // 128
        assert kc * ncols <= WSLOT and nrows % 128 == 0
        i = self.pos
        if self.cx.dry:
            self.plan.append((w, r0, nrows, c0, ncols))
        else:
            pl = self.plan[i]
            assert pl[1:] == (r0, nrows, c0, ncols), (pl[1:], (r0, nrows, c0, ncols))
            while self.loaded < min(len(self.plan), i + NWSLOT - WHOLD + 1):
                self._load(self.loaded)
                self.loaded += 1
        self.pos += 1
        return self._view(i, kc, ncols), self.regs[i % NWSLOT]


class Builder:
    def __init__(self, nseq, layers, stop=None):
        self.nseq = nseq
        self.layers = list(layers)
        self.stop = stop
        nc = bass.Bass("TRN2", target_bir_lowering=False)
        self.nc = nc
        dt = nc.dram_tensor
        self.x_d = dt("x", [nseq, S, D], F32, kind="ExternalInput").ap()
        self.p_d = dt("p", [DEPTH, nseq, S, 256], F32, kind="ExternalInput").ap()
        self.out_d = dt("out", [nseq, S, D], F32, kind="ExternalOutput").ap()
        self.cscr_d = dt("cscr", [16, 3, S], BF16, kind="ExternalOutput").ap()
        self.w = {}
        for name, shape in [("ev_w_in", [2, 1024, 1440]), ("ev_w_uq", [2, 384, 768]), ("ev_w_ukv", [2, 256, 1024]),
                            ("ev_w_out", [2, 1024, 1024]), ("od_w_in", [2, 1024, 3088]), ("od_w_out", [2, 1024, 1024]),
                            ("w_up", [4, 1024, 4096]), ("w_down", [4, 4096, 1024]), ("ple_w_proj", [4, 256, 1024]),
                            ("ple_w_gate", [4, 1024, 1024])]:
            self.w[name] = dt(name, shape, F32, kind="ExternalInput").ap()
        self.consts_d = dt("consts", [128, 384], F32, kind="ExternalInput").ap()
        self.vecs_d = dt("vecs", [128, 192], F32, kind="ExternalInput").ap()
        self.rows_d = dt("rows", [1, 32], F32, kind="ExternalInput").ap()
        self.cs_d = dt("cs", [128, 2 * S], F32, kind="ExternalInput").ap()
        self.biasg_d = dt("biasg", [128, 2 * 8 * 128], F32, kind="ExternalInput").ap()
        with ExitStack() as es:
            self.es = es
            self.cx = Ctx(nc, es)
            self.alloc_persistent()
            self.cx.dry = True
            self.program()
            self.cx.dry = False
            self.W.reset()
            self.psi = 0
            self.program()

    @contextmanager
    def phase(self):
        cx = self.cx
        mark = len(cx.all_regs)
        with ExitStack() as ph:
            yield ph
            cx.barrier()
            cx.recycle(mark)

    def sb(self, stack, name, shape, dtype):
        self._nm = getattr(self, "_nm", 0) + 1
        return stack.enter_context(self.nc.sbuf_tensor("s%d_%s" % (self._nm, name), shape, dtype))

    def alloc_persistent(self):
        es, nc, cx = self.es, self.nc, self.cx
        self.x_f32 = self.sb(es, "x_f32", [128, DC, S], F32)
        self.xT_bf = self.sb(es, "xT_bf", [128, DC, S], BF16)
        self.R_xf = [cx.reg("xf%d" % c) for c in range(NCH)]
        self.R_xb = [cx.reg("xb%d" % c) for c in range(NCH)]
        self.W = WStream(cx, nc, es)
        self.consts = self.sb(es, "consts", [128, 384], F32)
        self.R_consts = cx.reg("consts")
        self.cbf = self.sb(es, "cbf", [128, 640], BF16)
        self.vecs = self.sb(es, "vecs", [128, 192], F32)
        self.rows = self.sb(es, "rows", [1, 32], F32)
        self.small = self.sb(es, "small", [128, 8], F32)
        self.onesf = self.sb(es, "onesf", [128, 128], F32)
        self.ones512 = self.sb(es, "ones512", [128, CH], F32)
        self.cs = self.sb(es, "cs", [128, 2 * S], BF16)
        self.R_cs = cx.reg("cs")
        self.R_cs.dsem = cx.new_sem("csdma")
        self.R_cs.dcnt = 0
        self.b8m = self.sb(es, "b8m", [128, 2 * 8 * 128], BF16)
        self.R_ebm = cx.reg("b8m")
        self.ps = es.enter_context(nc.psum_tensor("psum_all", [128, 8, 512], F32))
        self.R_ps = [cx.reg("ps%d" % b) for b in range(8)]
        for r in self.R_ps:
            r.excl = True
        self.R_cscr = cx.reg("cscr")
        self.R_out = cx.reg("outdma")
        self.psi = 0

    def psum(self):
        b = self.psi % 8
        self.psi += 1
        return self.ps[:, b, :], self.R_ps[b]

    def mm(self, out, lhsT, rhs, start, stop, reads, writes, inc):
        nc = self.nc
        self.cx.op("pe", lambda: nc.tensor.matmul(out, lhsT=lhsT, rhs=rhs, start=start, stop=stop,
                                                  skip_group_check=True), reads, writes, inc)

    def tr(self, out, in_, reads, writes, inc):
        nc = self.nc
        ident = self.consts[:, 0:128]
        self.cx.op("pe", lambda: nc.tensor.transpose(out=out, in_=in_, identity=ident), tuple(reads) + (self.R_consts,),
                   writes, inc)

    def act(self, out, in_, func, reads, writes, bias=None, scale=None):
        nc = self.nc
        kw = {}
        if bias is not None:
            kw["bias"] = bias
        if scale is not None:
            kw["scale"] = scale
        self.cx.op("act", lambda: nc.scalar.activation(out=out, in_=in_, func=func, **kw), reads, writes)

    def tt(self, out, in0, in1, op, reads, writes, eng="dve"):
        nc = self.nc
        e = nc.vector if eng == "dve" else nc.gpsimd
        self.cx.op(eng, lambda: e.tensor_tensor(out=out, in0=in0, in1=in1, op=op), reads, writes)

    def stt(self, out, in0, scalar, in1, op0, op1, reads, writes):
        nc = self.nc
        self.cx.op("dve", lambda: nc.vector.scalar_tensor_tensor(out=out, in0=in0, scalar=scalar, in1=in1, op0=op0, op1=op1),
                   reads, writes)

    def ts(self, out, in0, s1, s2, op0, op1, reads, writes):
        nc = self.nc
        if op1 is None:
            self.cx.op("dve", lambda: nc.vector.tensor_scalar(out=out, in0=in0, scalar1=s1, scalar2=None, op0=op0), reads, writes)
        else:
            self.cx.op("dve", lambda: nc.vector.tensor_scalar(out=out, in0=in0, scalar1=s1, scalar2=s2, op0=op0, op1=op1),
                       reads, writes)

    def vcopy(self, out, in_, reads, writes):
        nc = self.nc
        self.cx.op("dve", lambda: nc.vector.tensor_copy(out=out, in_=in_), reads, writes)

    def acopy(self, out, in_, reads, writes):
        self.act(out, in_, AF.Identity, reads, writes)

    def memset(self, ap, val, writes, eng="dve"):
        nc = self.nc
        e = nc.vector if eng == "dve" else nc.gpsimd
        self.cx.op(eng, lambda: e.memset(ap, val), (), writes)

    def recip(self, out, in_, reads, writes):
        nc = self.nc
        self.cx.op("dve", lambda: nc.vector.reciprocal(out=out, in_=in_), reads, writes)

    def pool_recip(self, buf, reads, writes):
        nc = self.nc
        ones = self.ones512[buf.base_partition():buf.base_partition() + 64, :]
        self.cx.op("pool", lambda: nc.gpsimd.tensor_tensor(out=buf, in0=buf, in1=ones, op=ALU.pow), reads, writes)

    def program(self):
        cx = self.cx
        import os
        dbg = int(os.environ.get("KDBG", "9"))
        if dbg >= 1:
            self.setup_consts()
        if dbg < 3:
            if dbg >= 2:
                self.load_x(0)
            cx.final_wait([self.R_out, self.R_cscr])
            return
        for s in range(self.nseq):
            self.load_x(s)
            done = False
            for li in self.layers:
                if li % 2 == 0:
                    self.even_mixer(li)
                else:
                    self.odd_mixer(li)
                if self.stop == (li, "mix"):
                    break
                with self.phase() as ph:
                    T = self.ln_alloc(ph)
                    self.layer_norm(li, 0, T)
                    if self.stop != (li, "ln1"):
                        self.ffn(li, ph)
                        self.layer_norm(li, 1, T)
                if self.stop in ((li, "ln1"), (li, "ln2")):
                    break
                self.ple(li, s)
            self.store_out(s)
        cx.final_wait([self.R_out, self.R_cscr])

    def setup_consts(self):
        cx = self.cx
        Rc = self.R_consts
        cx.dma("sp", self.consts[:], self.consts_d[:, :], writes=(Rc,))
        cx.dma("sp", self.vecs[:], self.vecs_d[:, :], writes=(Rc,))
        cx.dma("sp", self.rows[:], self.rows_d[:, :], writes=(Rc,))
        cx.dma("pool", self.cs[:], self.cs_d[:, :], writes=(self.R_cs,))
        self.memset(self.cbf[:, 0:128], 1.0, (Rc,))
        self.vcopy(self.cbf[:, 128:384], self.consts[:, 128:384], (Rc,), (Rc,))
        self.ts(self.cbf[:, 384:512], self.consts[:, 128:256], -1.0, 30000.0, ALU.add, ALU.mult, (Rc,), (Rc,))
        self.vcopy(self.cbf[:, 512:640], self.consts[:, 0:128], (Rc,), (Rc,))
        self.memset(self.small[:, 0:1], EPS, (Rc,))
        self.memset(self.small[:, 1:2], 1.0, (Rc,))
        self.memset(self.onesf[:], 1.0, (Rc,))
        self.memset(self.ones512[:], -1.0, (Rc,))
        with self.phase() as ph:
            tmp = self.sb(ph, "biasg_tmp", [128, 2048], F32)
            Rt = cx.reg("biasg_tmp")
            cx.dma("sp", tmp[:], self.biasg_d[:, :], writes=(Rt,))
            negm = self.sb(ph, "negm", [128, 128], F32)
            Rnm = cx.reg("negm")
            for kt in range(2):
                mask = self.consts[:, 256:384] if kt == 0 else self.consts[:, 128:256]
                self.ts(negm[:], mask, -1.0, 30000.0, ALU.add, ALU.mult, (Rc, Rnm), (Rnm,))
                for h in range(8):
                    o = (kt * 8 + h) * 128
                    self.tt(tmp[:, o:o + 128], tmp[:, o:o + 128], mask, ALU.mult, (Rc, Rt), (Rt,))
                    self.stt(self.b8m[:, o:o + 128], tmp[:, o:o + 128], 8.0, negm[:], ALU.mult, ALU.add, (Rt, Rnm), (self.R_ebm,))
            for jj in range(2):
                sc = 160 + jj * 16 + 5
                self.act(self.vecs[:, sc:sc + 8], self.vecs[:, sc:sc + 8], AF.Exp, (Rc,), (Rc,))

    def load_x(self, s):
        cx = self.cx
        with self.phase() as ph:
            xin = [self.sb(ph, "xin%d" % i, [128, D], F32) for i in range(4)]
            Rin = [cx.reg("xin%d" % i) for i in range(4)]
            for t in range(NT):
                b = t % 4
                c = t // 4
                cx.dma("sp", xin[b][:], self.x_d[s, t * 128:(t + 1) * 128, :], writes=(Rin[b],))
                for half in range(2):
                    pt, Rp = self.psum()
                    for j in range(4):
                        fc = half * 4 + j
                        self.tr(pt[:, j * 128:(j + 1) * 128], xin[b][:, fc * 128:(fc + 1) * 128], (Rin[b],), (Rp,), inc=(j == 3))
                    src = pt.rearrange("p (a b) -> p a b", a=4)
                    xf = self.x_f32[:, half * 4:half * 4 + 4, t * 128:(t + 1) * 128]
                    self.vcopy(xf, src, (Rp,), (self.R_xf[c],))
                    self.acopy(self.xT_bf[:, half * 4:half * 4 + 4, t * 128:(t + 1) * 128], xf, (self.R_xf[c],), (self.R_xb[c],))

    def store_out(self, s):
        cx = self.cx
        with self.phase() as ph:
            xo = [self.sb(ph, "xout%d" % i, [128, D], F32) for i in range(4)]
            Ro = [cx.reg("xout%d" % i) for i in range(4)]
            for t in range(NT):
                b = t % 4
                c = t // 4
                for half in range(2):
                    pt, Rp = self.psum()
                    for j in range(4):
                        fc = half * 4 + j
                        self.tr(pt[:, j * 128:(j + 1) * 128], self.x_f32[:, fc, t * 128:(t + 1) * 128], (self.R_xf[c],), (Rp,), inc=(j == 3))
                    if half == 0:
                        self.vcopy(xo[b][:, 0:512], pt, (Rp,), (Ro[b],))
                    else:
                        self.acopy(xo[b][:, 512:1024], pt, (Rp,), (Ro[b],))
                cx.dma("sp", self.out_d[s, t * 128:(t + 1) * 128, :], xo[b][:], reads=(Ro[b],), writes=(), sem_reg=self.R_out)
                if not cx.dry:
                    self.R_out.r[id(self.R_out.dsem)] = (self.R_out.dsem, self.R_out.dcnt)

    def resid_acc(self, m, c, pt, Rp, first):
        xf = self.x_f32[:, m, c * CH:(c + 1) * CH]
        if first:
            self.stt(xf, xf, ALPHA, pt, ALU.mult, ALU.add, (Rp, self.R_xf[c]), (self.R_xf[c],))
        else:
            self.tt(xf, xf, pt, ALU.add, (Rp, self.R_xf[c]), (self.R_xf[c],))

    def out_proj_partial(self, wname, j, r0, nkc, rhs_fn, Rrhs, first):
        wt, Rw = self.W.get(self.w[wname][j], r0, nkc * 128, 0, 1024)
        for c in range(NCH):
            for m in range(DC):
                pt, Rp = self.psum()
                for k in range(nkc):
                    self.mm(pt, wt[:, k, m * 128:(m + 1) * 128], rhs_fn(k, c), k == 0, k == nkc - 1,
                            (Rw,) + tuple(Rrhs), (Rp,), inc=(k == nkc - 1))
                self.resid_acc(m, c, pt, Rp, first)

    def ln_alloc(self, ph):
        cx = self.cx
        T = {}
        T["xb"] = self.sb(ph, "ln_xb", [128, DC, CH], BF16)
        T["xq"] = self.sb(ph, "ln_xq", [128, DC, CH], BF16)
        T["mean"] = self.sb(ph, "ln_mean", [128, CH], F32)
        T["rstd"] = self.sb(ph, "ln_rstd", [128, CH], F32)
        T["bm"] = self.sb(ph, "ln_bm", [128, CH], F32)
        T["u"] = [self.sb(ph, "ln_u%d" % i, [128, CH], F32) for i in range(2)]
        T["v"] = [self.sb(ph, "ln_v%d" % i, [128, CH], F32) for i in range(2)]
        T["R"] = (cx.reg("ln_xb"), cx.reg("ln_xq"), cx.reg("ln_st"))
        T["Ru"] = [cx.reg("ln_u%d" % i) for i in range(2)]
        T["Rv"] = [cx.reg("ln_v%d" % i) for i in range(2)]
        return T

    def layer_norm(self, li, which, T):
        cx = self.cx
        gcol = li * 40 + which * 16
        ones_bf = self.cbf[:, 0:128]
        if True:
            xb, xq, mean, rstd, bm, u, v = T["xb"], T["xq"], T["mean"], T["rstd"], T["bm"], T["u"], T["v"]
            Rxb, Rxq, Rst = T["R"]
            Ru, Rv = T["Ru"], T["Rv"]
            for c in range(NCH):
                sl = slice(c * CH, (c + 1) * CH)
                Rx = self.R_xf[c]
                self.acopy(xb[:], self.x_f32[:, :, sl], (Rx,), (Rxb,))
                self.act(xq[:], self.x_f32[:, :, sl], AF.Square, (Rx,), (Rxq,))
                p1, R1 = self.psum()
                for k in range(DC):
                    self.mm(p1, ones_bf, xb[:, k, :], k == 0, k == DC - 1, (Rxb, self.R_consts), (R1,), inc=(k == DC - 1))
                p2, R2 = self.psum()
                for k in range(DC):
                    self.mm(p2, ones_bf, xq[:, k, :], k == 0, k == DC - 1, (Rxq, self.R_consts), (R2,), inc=(k == DC - 1))
                self.ts(mean[:], p1, 1.0 / D, None, ALU.mult, None, (R1,), (Rst,))
                self.tt(bm[:], mean[:], mean[:], ALU.mult, (Rst,), (Rst,))
                self.stt(rstd[:], p2, 1.0 / D, bm[:], ALU.mult, ALU.subtract, (R2, Rst), (Rst,))
                self.act(rstd[:], rstd[:], AF.Ln, (Rst,), (Rst,), bias=self.small[:, 0:1], scale=1.0)
                self.act(rstd[:], rstd[:], AF.Exp, (Rst,), (Rst,), scale=-0.5)
                self.stt(bm[:], mean[:], -1.0, rstd[:], ALU.mult, ALU.mult, (Rst,), (Rst,))
                for m in range(DC):
                    i = m % 2
                    g = self.vecs[:, gcol + m:gcol + m + 1]
                    b = self.vecs[:, gcol + 8 + m:gcol + 8 + m + 1]
                    xf = self.x_f32[:, m, sl]
                    self.stt(u[i][:], xf, g, rstd[:], ALU.mult, ALU.mult, (Rx, Rst, self.R_consts), (Ru[i],))
                    self.stt(v[i][:], bm[:], g, u[i][:], ALU.mult, ALU.add, (Rst, Ru[i]), (Rv[i],))
                    self.act(xf, v[i][:], AF.Identity, (Rv[i],), (Rx,), bias=b, scale=1.0)
                    self.act(self.xT_bf[:, m, sl], v[i][:], AF.Identity, (Rv[i],), (self.R_xb[c],), bias=b, scale=1.0)

    def ffn(self, li, ph):
        cx = self.cx
        wup = self.w["w_up"][li]
        wdn = self.w["w_down"][li]
        if True:
            hT = self.sb(ph, "hT", [128, 8, S], BF16)
            rl = [self.sb(ph, "relu%d" % i, [128, CH], F32) for i in range(2)]
            Rh = [cx.reg("hT%d" % c) for c in range(NCH)]
            Rr = [cx.reg("relu%d" % i) for i in range(2)]
            n = 0
            for g in range(4):
                for wi in range(4):
                    wt, Rw = self.W.get(wup, 0, 1024, g * 1024 + wi * 256, 256)
                    for c in range(NCH):
                        for f in range(2):
                            fi = wi * 2 + f
                            pt, Rp = self.psum()
                            for k in range(DC):
                                self.mm(pt, wt[:, k, f * 128:(f + 1) * 128], self.xT_bf[:, k, c * CH:(c + 1) * CH],
                                        k == 0, k == DC - 1, (Rw, self.R_xb[c]), (Rp,), inc=(k == DC - 1))
                            i = n % 2
                            n += 1
                            self.act(rl[i][:], pt, AF.Relu, (Rp,), (Rr[i],))
                            self.tt(hT[:, fi, c * CH:(c + 1) * CH], rl[i][:], rl[i][:], ALU.mult, (Rr[i],), (Rh[c],))
                for wi in range(4):
                    wt, Rw = self.W.get(wdn, g * 1024, 1024, wi * 256, 256)
                    for c in range(NCH):
                        for mm_ in range(2):
                            m = wi * 2 + mm_
                            pt, Rp = self.psum()
                            for k in range(8):
                                self.mm(pt, wt[:, k, mm_ * 128:(mm_ + 1) * 128], hT[:, k, c * CH:(c + 1) * CH],
                                        k == 0, k == 7, (Rw, Rh[c]), (Rp,), inc=(k == 7))
                            self.resid_acc(m, c, pt, Rp, g == 0)

    def ple(self, li, s):
        cx = self.cx
        wg = self.w["ple_w_gate"][li]
        wp = self.w["ple_w_proj"][li]
        bcol = li * 40 + 32
        with self.phase() as ph:
            pin = [self.sb(ph, "pin%d" % i, [128, 4, 256], F32) for i in range(2)]
            pT = self.sb(ph, "pT", [128, 2, S], BF16)
            gt = [self.sb(ph, "gate%d" % i, [128, CH], F32) for i in range(2)]
            tq = [self.sb(ph, "gprod%d" % i, [128, CH], F32) for i in range(2)]
            Rpin = [cx.reg("pin%d" % i) for i in range(2)]
            RpT = [cx.reg("pT%d" % c) for c in range(NCH)]
            Rg = [cx.reg("gate%d" % i) for i in range(2)]
            Rq = [cx.reg("gprod%d" % i) for i in range(2)]
            for c in range(NCH):
                b = c % 2
                src = self.p_d[li, s, c * CH:(c + 1) * CH, :].rearrange("(t p) f -> p t f", p=128)
                cx.dma("sp", pin[b][:], src, writes=(Rpin[b],))
                for k2 in range(2):
                    pt, Rp = self.psum()
                    for t in range(4):
                        self.tr(pt[:, t * 128:(t + 1) * 128], pin[b][:, t, k2 * 128:(k2 + 1) * 128], (Rpin[b],), (Rp,), inc=(t == 3))
                    self.acopy(pT[:, k2, c * CH:(c + 1) * CH], pt, (Rp,), (RpT[c],))
            wp_ring, Rwp_ring = self.W.get(wp, 0, 256, 0, 1024)
            wpt = self.sb(ph, "wproj", [128, 2, 1024], BF16)
            Rwp = cx.reg("wproj")
            self.vcopy(wpt[:], wp_ring, (Rwp_ring,), (Rwp,))
            n = 0
            for wi in range(4):
                wt, Rw = self.W.get(wg, 0, 1024, wi * 256, 256)
                for c in range(NCH):
                    for mm_ in range(2):
                        m = wi * 2 + mm_
                        i = n % 2
                        n += 1
                        pg, Rpg = self.psum()
                        for k in range(DC):
                            self.mm(pg, wt[:, k, mm_ * 128:(mm_ + 1) * 128], self.xT_bf[:, k, c * CH:(c + 1) * CH],
                                    k == 0, k == DC - 1, (Rw, self.R_xb[c]), (Rpg,), inc=(k == DC - 1))
                        pp, Rpp = self.psum()
                        for k in range(2):
                            self.mm(pp, wpt[:, k, m * 128:(m + 1) * 128], pT[:, k, c * CH:(c + 1) * CH],
                                    k == 0, k == 1, (Rwp, RpT[c]), (Rpp,), inc=(k == 1))
                        self.act(gt[i][:], pg, AF.Sigmoid, (Rpg, self.R_consts), (Rg[i],),
                                 bias=self.vecs[:, bcol + m:bcol + m + 1], scale=1.0)
                        self.tt(tq[i][:], gt[i][:], pp, ALU.mult, (Rg[i], Rpp), (Rq[i],))
                        xf = self.x_f32[:, m, c * CH:(c + 1) * CH]
                        self.tt(xf, xf, tq[i][:], ALU.add, (Rq[i], self.R_xf[c]), (self.R_xf[c],))
            for c in range(NCH):
                for m in range(DC):
                    sl = slice(c * CH, (c + 1) * CH)
                    if m % 2 == 0:
                        self.acopy(self.xT_bf[:, m, sl], self.x_f32[:, m, sl], (self.R_xf[c],), (self.R_xb[c],))
                    else:
                        self.vcopy(self.xT_bf[:, m, sl], self.x_f32[:, m, sl], (self.R_xf[c],), (self.R_xb[c],))

    def attn_causal(self, qT, Rq, kT, Rk, KD, vaug_fn, Rv, scale, kbias_fn, Rkb, orient, dst_fn, Rdst, A, act_recip=False):
        tri = self.cbf[:, 128:256]
        ro = 0 if orient == 0 else 64
        rd = 64 - ro
        steps = [(c, j) for c in range(NCH) for j in range(4 * c + 4)]
        LA = 3
        st = {}
        accs = {}

        def front(i):
            c, j = steps[i]
            lo = max(0, j - 4 * c) * 128
            sp_, Rsp = A["sc"][A["si"] % 4]
            pt_, Rpt = A["pt"][A["si"] % 4]
            A["si"] += 1
            diag = j >= 4 * c
            self.mm(sp_[:, lo:CH], kT[0:KD, j * 128:(j + 1) * 128], qT[0:KD, c * CH + lo:(c + 1) * CH], True, not diag,
                    (Rq, Rk), (Rsp,), inc=not diag)
            if diag:
                self.mm(sp_[:, lo:lo + 128], self.cbf[:, 512:640], self.cbf[:, 384:512], False, True, (self.R_consts,), (Rsp,), inc=True)
            kw = {}
            rds = (Rsp,)
            if kbias_fn is not None:
                kw["bias"] = kbias_fn(j)
                rds = (Rsp, Rkb)
            self.act(pt_[:, lo:CH], sp_[:, lo:CH], AF.Exp, rds, (Rpt,), scale=scale, **kw)
            st[i] = (pt_, Rpt, lo)

        def back(i):
            c, j = steps[i]
            nj = 4 * c + 4
            if j == 0:
                accs[c] = A["acc"][A["ai"] % 2]
                A["ai"] += 1
            acc, Racc = accs[c]
            pt_, Rpt, lo = st.pop(i)
            self.mm(acc[:, lo:CH], vaug_fn(j), pt_[:, lo:CH], j == 0, j == nj - 1, (Rv, Rpt), (Racc,), inc=True)
            if j == nj - 1:
                rec, Rrec = A["rec"][A["ri"] % 2]
                A["ri"] += 1
                if act_recip and c % 2 == 0:
                    self.act(rec[ro:ro + 64, :], acc[rd:rd + 64, :], AF.Ln, (Racc,), (Rrec,))
                    self.act(rec[ro:ro + 64, :], rec[ro:ro + 64, :], AF.Exp, (Rrec,), (Rrec,), scale=-1.0)
                else:
                    self.recip(rec[ro:ro + 64, :], acc[rd:rd + 64, :], (Racc,), (Rrec,))
                self.tt(dst_fn(c), acc[ro:ro + 64, :], rec[ro:ro + 64, :], ALU.mult, (Racc, Rrec), (Rdst,))

        n = len(steps)
        for i in range(n + LA):
            if i < n:
                front(i)
            if i >= LA:
                back(i - LA)

    def attn_bufs(self, ph):
        cx = self.cx
        A = {"ai": 0, "si": 0, "ri": 0}
        A["sc"] = [(self.ps[:, b, :], self.R_ps[b]) for b in range(4)]
        A["acc"] = [(self.ps[:, 4 + b, :], self.R_ps[4 + b]) for b in range(2)]
        A["pt"] = []
        for i in range(4):
            A["pt"].append((self.sb(ph, "ptile%d" % i, [128, CH], BF16), cx.reg("ptile%d" % i)))
        A["rec"] = []
        for i in range(2):
            A["rec"].append((self.sb(ph, "rec%d" % i, [128, CH], F32), cx.reg("rec%d" % i)))
        return A

    def psum67(self):
        b = 6 + (self.psi % 2)
        self.psi += 1
        return self.ps[:, b, :], self.R_ps[b]

    def odd_mixer(self, li):
        cx = self.cx
        j = li // 2
        win = self.w["od_w_in"][j]
        tri_f = self.consts[:, 128:256]
        with self.phase() as ph:
            A = self.attn_bufs(ph)
            ltok = self.sb(ph, "ltok", [128, 256], F32)
            negc = self.sb(ph, "negc", [128, 256], F32)
            Rl, Rn = cx.reg("ltok"), cx.reg("negc")
            wt, Rw = self.W.get(win, 0, 1024, 3072, 16)
            pf, Rpf = self.psum()
            brow = self.rows[0:1, j * 16:(j + 1) * 16]
            for t in range(NT):
                for k in range(DC):
                    self.mm(pf[:, t * 16:(t + 1) * 16], self.xT_bf[:, k, t * 128:(t + 1) * 128], wt[:, k, :], k == 0, False,
                            (Rw, self.R_xb[t // 4]), (Rpf,), inc=False)
                self.mm(pf[:, t * 16:(t + 1) * 16], self.onesf[0:1, 0:128], brow, False, True, (self.R_consts,), (Rpf,), inc=True)
            self.act(ltok[:], pf[:, 0:256], AF.Exp, (Rpf,), (Rl,), scale=-1.0)
            self.act(ltok[:], ltok[:], AF.Ln, (Rl, self.R_consts), (Rl,), bias=self.small[:, 1:2], scale=1.0)
            l2 = self.sb(ph, "l2", [128, 256], F32)
            Rl2 = cx.reg("l2")
            self.memset(l2[:, 0:16], 0.0, (Rl2,))
            for i in range(1, NT):
                self.tt(l2[:, i * 16:(i + 1) * 16], l2[:, (i - 1) * 16:i * 16], ltok[:, (i - 1) * 16:i * 16], ALU.add,
                        (Rl2, Rl), (Rl2,))
            pc, Rpc = self.psum()
            self.mm(pc[:, 0:256], tri_f, ltok[:], True, False, (Rl, self.R_consts), (Rpc,), inc=False)
            self.mm(pc[:, 0:256], self.onesf[:, :], l2[:], False, True, (Rl2, self.R_consts), (Rpc,), inc=True)
            self.vcopy(negc[:], pc[:, 0:256], (Rpc,), (Rn,))
            with self.phase() as ph2:
                v0 = self.sb(ph2, "c_v0", [16, CH], F32)
                r1 = self.sb(ph2, "c_r1", [16, CH], F32)
                cs3 = [self.sb(ph2, "c_split%d" % i, [16, 3, CH], BF16) for i in range(2)]
                Rc0 = cx.reg("c_v0")
                Rc3 = [cx.reg("c_split%d" % i) for i in range(2)]
                for c in range(NCH):
                    pct, Rpct = self.psum()
                    for tt_ in range(4):
                        i = c * 4 + tt_
                        self.tr(pct[0:16, tt_ * 128:(tt_ + 1) * 128], negc[:, i * 16:(i + 1) * 16], (Rn,), (Rpct,), inc=(tt_ == 3))
                    b = c % 2
                    self.ts(v0[:], pct[0:16, :], -1.0, None, ALU.mult, None, (Rpct,), (Rc0,))
                    self.vcopy(cs3[b][:, 0, :], v0[:], (Rc0,), (Rc3[b],))
                    self.tt(r1[:], v0[:], cs3[b][:, 0, :], ALU.subtract, (Rc0, Rc3[b]), (Rc0,))
                    self.vcopy(cs3[b][:, 1, :], r1[:], (Rc0,), (Rc3[b],))
                    self.tt(r1[:], r1[:], cs3[b][:, 1, :], ALU.subtract, (Rc0, Rc3[b]), (Rc0,))
                    self.vcopy(cs3[b][:, 2, :], r1[:], (Rc0,), (Rc3[b],))
                    cx.dma("sp", self.cscr_d[:, :, c * CH:(c + 1) * CH], cs3[b][:], reads=(Rc3[b],), writes=(self.R_cscr,))
            vaug = self.sb(ph, "fx_vaug", [128, NT, 384], BF16)
            Rva = cx.reg("fx_vaug")
            qT = [self.sb(ph, "fx_qT%d" % i, [128, S], BF16) for i in range(2)]
            kT = [self.sb(ph, "fx_kT%d" % i, [128, S], BF16) for i in range(2)]
            RqT = [cx.reg("fx_qT%d" % i) for i in range(2)]
            RqA = [cx.reg("fx_qTa%d" % i) for i in range(2)]
            RkT = [cx.reg("fx_kT%d" % i) for i in range(2)]
            RkA = [cx.reg("fx_kTa%d" % i) for i in range(2)]
            osc = self.sb(ph, "fx_osc", [128, 2, S], BF16)
            Ros = cx.reg("fx_osc")
            for i in range(2):
                self.memset(kT[i][64:70, :], 8.0, (RkT[i], RkA[i]))
                self.memset(qT[i][64:70, :], -8.0, (RqT[i], RqA[i]))
            for pr in range(2):
                self.memset(vaug[:, :, pr * 192 + 64:pr * 192 + 128], 1.0, (Rva,))
            for G in range(4):
                wt, Rw = self.W.get(win, 0, 1024, 2048 + G * 256, 256)
                for t in range(NT):
                    pv, Rpv = self.psum()
                    for k in range(DC):
                        self.mm(pv[:, 0:256], self.xT_bf[:, k, t * 128:(t + 1) * 128], wt[:, k, :], k == 0, k == DC - 1,
                                (Rw, self.R_xb[t // 4]), (Rpv,), inc=(k == DC - 1))
                    for hh in range(2):
                        src = pv[:, 0:256].rearrange("p (a b) -> p a b", a=2)[:, :, hh * 64:hh * 64 + 64]
                        dst = vaug[:, t, :].rearrange("p (a b) -> p a b", a=2)[:, :, hh * 128:hh * 128 + 64]
                        self.vcopy(dst, src, (Rpv,), (Rva,))
                wq, Rwq = self.W.get(win, 0, 1024, G * 256, 256)
                wk, Rwk = self.W.get(win, 0, 1024, 1024 + G * 256, 256)
                for pr in range(2):
                    for (wt_, Rw_, dstT, Rd) in ((wq, Rwq, qT, RqT), (wk, Rwk, kT, RkT)):
                        for c in range(NCH):
                            pq, Rpq = self.psum()
                            for k in range(DC):
                                self.mm(pq, wt_[:, k, pr * 128:(pr + 1) * 128], self.xT_bf[:, k, c * CH:(c + 1) * CH],
                                        k == 0, k == DC - 1, (Rw_, self.R_xb[c]), (Rpq,), inc=(k == DC - 1))
                            self.vcopy(dstT[0][0:64, c * CH:(c + 1) * CH], pq[0:64, :], (Rpq,), (Rd[0],))
                            self.vcopy(dstT[1][0:64, c * CH:(c + 1) * CH], pq[64:128, :], (Rpq,), (Rd[1],))
                    for hh in range(2):
                        h = G * 4 + pr * 2 + hh
                        for jx in range(3):
                            cx.dma("sp", qT[hh][64 + jx:65 + jx, :], self.cscr_d[h:h + 1, jx, :], reads=(self.R_cscr,), writes=(RqA[hh],))
                            cx.dma("sp", kT[hh][67 + jx:68 + jx, :], self.cscr_d[h:h + 1, jx, :], reads=(self.R_cscr,), writes=(RkA[hh],))
                    for hh in range(2):
                        h = G * 4 + pr * 2 + hh
                        i4 = pr * 2 + hh
                        vo = pr * 192 + hh * 64
                        self._attn_fox(qT, RqT, RqA, kT, RkT, RkA, vaug, Rva, osc, Ros, A, pr, hh, h)
                self.out_proj_partial("od_w_out", j, G * 256, 2, (lambda k, c: osc[:, k, c * CH:(c + 1) * CH]), (Ros,), G == 0)

    def _attn_fox(self, qT, RqT, RqA, kT, RkT, RkA, vaug, Rva, osc, Ros, A, pr, hh, h):
        vo = pr * 192 + hh * 64
        cxq = _Multi((RqT[hh], RqA[hh]))
        cxk = _Multi((RkT[hh], RkA[hh]))
        self.attn_causal(qT[hh], cxq, kT[hh], cxk, 70,
                         (lambda jt: vaug[:, jt, vo:vo + 128]), Rva, 0.125,
                         None, None, hh,
                         (lambda c: osc[hh * 64:hh * 64 + 64, pr, c * CH:(c + 1) * CH]), Ros, A)

    def even_mixer(self, li):
        cx = self.cx
        j = li // 2
        win = self.w["ev_w_in"][j]
        vb = 160 + j * 16
        ones_bf = self.cbf[:, 0:128]
        first = [True]
        with self.phase() as ph:
            cqn = self.sb(ph, "cqn", [128, 3, S], BF16)
            ckvn = self.sb(ph, "ckvn", [128, 2, S], BF16)
            kT = [self.sb(ph, "ml_kT%d" % i, [128, S], BF16) for i in range(2)]
            Rcq = [cx.reg("cqn%d" % c) for c in range(NCH)]
            Rckv = [cx.reg("ckvn%d" % c) for c in range(NCH)]
            RkT = [cx.reg("ml_kT%d" % i) for i in range(2)]
            RkR = [cx.reg("ml_kTr%d" % i) for i in range(2)]
            with self.phase() as p1:
                raws = [self.sb(p1, "lat_raw%d" % i, [128, 3, CH], F32) for i in range(2)]
                sqs = [self.sb(p1, "lat_sq%d" % i, [128, 3, CH], BF16) for i in range(2)]
                rstds = [self.sb(p1, "lat_rstd%d" % i, [128, CH], F32) for i in range(2)]
                wrot = self.sb(p1, "wkr_rot", [128, 8, 32], BF16)
                t1 = self.sb(p1, "kr_t1", [128, CH], F32)
                t2 = self.sb(p1, "kr_t2", [128, CH], F32)
                Rraws = [cx.reg("lat_raw%d" % i) for i in range(2)]
                Rsqs = [cx.reg("lat_sq%d" % i) for i in range(2)]
                Rrss = [cx.reg("lat_rstd%d" % i) for i in range(2)]
                Rwr, Rt1, Rt2 = (cx.reg(n) for n in ("wkr_rot", "kr_t1", "kr_t2"))
                for c in range(NCH):
                    sl = slice(c * CH, (c + 1) * CH)
                    wA, RwA = self.W.get(win, 0, 1024, 0, 256)
                    wB, RwB = self.W.get(win, 0, 1024, 256, 256)
                    wC, RwC = self.W.get(win, 0, 1024, 512, 160)
                    srcs = {0: (wA, RwA, 0), 1: (wA, RwA, 128), 2: (wB, RwB, 0), 3: (wB, RwB, 128), 4: (wC, RwC, 0)}
                    for (lat, ms, nfeat, dst, Rd, ncol) in ((0, (0, 1, 2), 384.0, cqn, Rcq, vb), (1, (3, 4), 256.0, ckvn, Rckv, vb + 3)):
                        raw, sq, rstd = raws[lat], sqs[lat], rstds[lat]
                        Rraw, Rsq, Rrs = Rraws[lat], Rsqs[lat], Rrss[lat]
                        for mi, m in enumerate(ms):
                            wt_, Rw_, co = srcs[m]
                            pt, Rp = self.psum()
                            for k in range(DC):
                                self.mm(pt, wt_[:, k, co:co + 128], self.xT_bf[:, k, sl], k == 0, k == DC - 1,
                                        (Rw_, self.R_xb[c]), (Rp,), inc=(k == DC - 1))
                            self.acopy(raw[:, mi, :], pt, (Rp,), (Rraw,))
                            self.tt(sq[:, mi, :], raw[:, mi, :], raw[:, mi, :], ALU.mult, (Rraw,), (Rsq,))
                        pss, Rpss = self.psum()
                        for mi in range(len(ms)):
                            self.mm(pss, ones_bf, sq[:, mi, :], mi == 0, mi == len(ms) - 1, (Rsq, self.R_consts), (Rpss,),
                                    inc=(mi == len(ms) - 1))
                        self.act(rstd[:], pss, AF.Ln, (Rpss, self.R_consts), (Rrs,), bias=self.small[:, 0:1], scale=1.0 / nfeat)
                        self.act(rstd[:], rstd[:], AF.Exp, (Rrs,), (Rrs,), scale=-0.5)
                        for mi in range(len(ms)):
                            self.stt(dst[:, mi, sl], raw[:, mi, :], self.vecs[:, ncol + mi:ncol + mi + 1], rstd[:], ALU.mult, ALU.mult,
                                     (Rraw, Rrs, self.R_consts), (Rd[c],))
                    if c == 0:
                        self.ts(wrot[:, :, 0:16], wC[:, :, 144:160], -1.0, None, ALU.mult, None, (RwC,), (Rwr,))
                        self.vcopy(wrot[:, :, 16:32], wC[:, :, 128:144], (RwC,), (Rwr,))
                    pk, Rpk = self.psum()
                    for k in range(DC):
                        self.mm(pk[0:32, :], wC[:, k, 128:160], self.xT_bf[:, k, sl], k == 0, k == DC - 1, (RwC, self.R_xb[c]), (Rpk,),
                                inc=(k == DC - 1))
                    pr_, Rpr = self.psum()
                    for k in range(DC):
                        self.mm(pr_[0:32, :], wrot[:, k, :], self.xT_bf[:, k, sl], k == 0, k == DC - 1, (Rwr, self.R_xb[c]), (Rpr,),
                                inc=(k == DC - 1))
                    self.tt(t1[0:32, :], pk[0:32, :], self.cs[0:32, c * CH:(c + 1) * CH], ALU.mult, (Rpk, self.R_cs), (Rt1,))
                    self.tt(t2[0:32, :], pr_[0:32, :], self.cs[0:32, S + c * CH:S + (c + 1) * CH], ALU.mult, (Rpr, self.R_cs), (Rt2,))
                    self.tt(kT[0][64:96, sl], t1[0:32, :], t2[0:32, :], ALU.add, (Rt1, Rt2), (RkR[0],))
                    self.acopy(kT[1][64:96, sl], kT[0][64:96, sl], (RkR[0],), (RkR[1],))
            with self.phase() as p2:
                A = None
                qs = self.sb(p2, "sw_q", [128, 2, S], BF16)
                ks2 = self.sb(p2, "sw_k2", [128, S], BF16)
                vs = self.sb(p2, "sw_v", [128, NT, 128], BF16)
                wks2 = self.sb(p2, "sw_wk2", [128, 8, 128], BF16)
                osc = self.sb(p2, "sw_osc", [128, 2, S], BF16)
                pp = [self.sb(p2, "sw_p%d" % i, [128, 2, CH], BF16) for i in range(2)]
                rec = [self.sb(p2, "sw_rec%d" % i, [128, CH], F32) for i in range(2)]
                Rqs, Rks, Rvs, Rwk2, Ros = (cx.reg(n) for n in ("sw_q", "sw_k2", "sw_v", "sw_wk2", "sw_osc"))
                Rpp = [cx.reg("sw_p%d" % i) for i in range(2)]
                Rrec = [cx.reg("sw_rec%d" % i) for i in range(2)]
                self.memset(vs[:, :, 64:128], 1.0, (Rvs,))
                nb = 0
                for g in range(2):
                    wq, Rwq = self.W.get(win, 0, 1024, EV_COLS["qs"] + g * 256, 256)
                    for c in range(NCH):
                        for m in range(2):
                            pt, Rp = self.psum()
                            for k in range(DC):
                                self.mm(pt, wq[:, k, m * 128:(m + 1) * 128], self.xT_bf[:, k, c * CH:(c + 1) * CH], k == 0, k == DC - 1,
                                        (Rwq, self.R_xb[c]), (Rp,), inc=(k == DC - 1))
                            self.acopy(qs[:, m, c * CH:(c + 1) * CH], pt, (Rp,), (Rqs,))
                    wkv, Rwkv = self.W.get(win, 0, 1024, EV_COLS["ks"], 256)
                    for rep in range(2):
                        self.vcopy(wks2[:, :, rep * 64:(rep + 1) * 64], wkv[:, :, g * 64:(g + 1) * 64], (Rwkv,), (Rwk2,))
                    for c in range(NCH):
                        pt, Rp = self.psum()
                        for k in range(DC):
                            self.mm(pt, wks2[:, k, :], self.xT_bf[:, k, c * CH:(c + 1) * CH], k == 0, k == DC - 1,
                                    (Rwk2, self.R_xb[c]), (Rp,), inc=(k == DC - 1))
                        self.vcopy(ks2[:, c * CH:(c + 1) * CH], pt, (Rp,), (Rks,))
                    for t8 in range(2):
                        pt, Rp = self.psum()
                        for tt_ in range(8):
                            t = t8 * 8 + tt_
                            for k in range(DC):
                                self.mm(pt[:, tt_ * 64:(tt_ + 1) * 64], self.xT_bf[:, k, t * 128:(t + 1) * 128],
                                        wkv[:, k, 128 + g * 64:128 + (g + 1) * 64], k == 0, k == DC - 1,
                                        (Rwkv, self.R_xb[t // 4]), (Rp,), inc=(k == DC - 1))
                        self.acopy(vs[:, t8 * 8:(t8 + 1) * 8, 0:64], pt.rearrange("p (a b) -> p a b", a=8), (Rp,), (Rvs,))
                    def sw_front(n, i):
                        kts = (1,) if n == 0 else (0, 1)
                        ident_bf = self.cbf[:, 512:640]
                        for kt in kts:
                            ktile = n - 1 + kt
                            for half in range(2):
                                b = kt * 2 + half
                                sp_, Rsp = self.ps[:, b, :], self.R_ps[b]
                                for jc in range(2):
                                    hi = 2 * jc + half
                                    o = (kt * 8 + 4 * g + hi) * 128
                                    self.mm(sp_[:, jc * 128:(jc + 1) * 128], ks2[half * 64:(half + 1) * 64, ktile * 128:(ktile + 1) * 128],
                                            qs[half * 64:(half + 1) * 64, jc, n * 128:(n + 1) * 128], True, False, (Rks, Rqs), (Rsp,), inc=False)
                                    self.mm(sp_[:, jc * 128:(jc + 1) * 128], ident_bf, self.b8m[:, o:o + 128], False, True,
                                            (self.R_ebm, self.R_consts), (Rsp,), inc=(jc == 1))
                                dst = pp[i][:, kt, :].rearrange("p (a b) -> p a b", a=4)[:, half:4:2, :]
                                self.act(dst, sp_[:, 0:256].rearrange("p (a b) -> p a b", a=2), AF.Exp, (Rsp,), (Rpp[i],), scale=0.125)

                    def sw_back(n, i):
                        kts = (1,) if n == 0 else (0, 1)
                        ab = 4 + i
                        acc, Racc = self.ps[:, ab, :], self.R_ps[ab]
                        for ki, kt in enumerate(kts):
                            ktile = n - 1 + kt
                            self.mm(acc, vs[:, ktile, :], pp[i][:, kt, :], ki == 0, ki == len(kts) - 1, (Rvs, Rpp[i]), (Racc,),
                                    inc=(ki == len(kts) - 1))
                        for hi in range(4):
                            sc = vb + 5 + 4 * g + hi
                            self.ts(rec[i][0:64, hi * 128:(hi + 1) * 128], acc[64:128, hi * 128:(hi + 1) * 128],
                                    self.vecs[64:128, sc:sc + 1], None, ALU.add, None, (Racc, self.R_consts), (Rrec[i],))
                        self.act(rec[i][0:64, :], rec[i][0:64, :], AF.Ln, (Rrec[i],), (Rrec[i],))
                        self.act(rec[i][0:64, :], rec[i][0:64, :], AF.Exp, (Rrec[i],), (Rrec[i],), scale=-1.0)
                        for half in range(2):
                            src = acc[0:64, :].rearrange("p (a b) -> p a b", a=4)[:, half:4:2, :]
                            rcs = rec[i][0:64, :].rearrange("p (a b) -> p a b", a=4)[:, half:4:2, :]
                            dst = osc[half * 64:(half + 1) * 64, :, n * 128:(n + 1) * 128]
                            self.tt(dst, src, rcs, ALU.mult, (Racc, Rrec[i]), (Ros,))

                    for n in range(NT + 1):
                        if n < NT:
                            sw_front(n, n % 2)
                        if n >= 1:
                            sw_back(n - 1, (n - 1) % 2)
                    self.out_proj_partial("ev_w_out", j, 512 + g * 256, 2, (lambda k, c: osc[:, k, c * CH:(c + 1) * CH]), (Ros,), first[0])
                    first[0] = False
            with self.phase() as p3:
                A = self.attn_bufs(p3)
                qT = [self.sb(p3, "ml_qT%d" % i, [128, S], BF16) for i in range(2)]
                vaug = self.sb(p3, "ml_vaug", [128, NT, 192], BF16)
                osc = self.sb(p3, "ml_osc", [128, 2, S], BF16)
                wqrot = self.sb(p3, "ml_wqrot", [128, 3, 2, 32], BF16)
                t1 = self.sb(p3, "ml_t1", [128, CH], F32)
                t2 = self.sb(p3, "ml_t2", [128, CH], F32)
                RqT = [cx.reg("ml_qT%d" % i) for i in range(2)]
                Rva, Ros, Rwr, Rt1, Rt2 = (cx.reg(n) for n in ("ml_vaug", "ml_osc", "ml_wqrot", "ml_t1", "ml_t2"))
                self.memset(vaug[:, :, 64:128], 1.0, (Rva,))
                scale = float(96.0 ** -0.5)
                for pr in range(4):
                    wkv, Rwkv = self.W.get(self.w["ev_w_ukv"][j], 0, 256, 0, 1024)
                    for t4 in range(4):
                        pv, Rpv = self.psum()
                        for tt_ in range(4):
                            t = t4 * 4 + tt_
                            for hh in range(2):
                                h = pr * 2 + hh
                                for k in range(2):
                                    self.mm(pv[:, tt_ * 128 + hh * 64:tt_ * 128 + hh * 64 + 64], ckvn[:, k, t * 128:(t + 1) * 128],
                                            wkv[:, k, h * 128 + 64:h * 128 + 128], k == 0, k == 1, (Rwkv, Rckv[t // 4]), (Rpv,),
                                            inc=(k == 1 and hh == 1 and tt_ == 3))
                        for hh in range(2):
                            src = pv.rearrange("p (a b) -> p a b", a=4)[:, :, hh * 64:hh * 64 + 64]
                            dst = vaug[:, t4 * 4:(t4 + 1) * 4, hh * 128:hh * 128 + 64]
                            if hh == 0:
                                self.acopy(dst, src, (Rpv,), (Rva,))
                            else:
                                self.vcopy(dst, src, (Rpv,), (Rva,))
                    for hh in range(2):
                        h = pr * 2 + hh
                        for c in range(NCH):
                            pk, Rpk = self.psum()
                            for k in range(2):
                                self.mm(pk[0:64, :], wkv[:, k, h * 128:h * 128 + 64], ckvn[:, k, c * CH:(c + 1) * CH], k == 0, k == 1,
                                        (Rwkv, Rckv[c]), (Rpk,), inc=(k == 1))
                            self.acopy(kT[hh][0:64, c * CH:(c + 1) * CH], pk[0:64, :], (Rpk,), (RkT[hh],))
                    wq, Rwq = self.W.get(self.w["ev_w_uq"][j], 0, 384, pr * 192, 192)
                    for hh in range(2):
                        self.ts(wqrot[:, :, hh, 0:16], wq[:, :, hh * 96 + 80:hh * 96 + 96], -1.0, None, ALU.mult, None, (Rwq,), (Rwr,))
                        self.vcopy(wqrot[:, :, hh, 16:32], wq[:, :, hh * 96 + 64:hh * 96 + 80], (Rwq,), (Rwr,))
                    for hh in range(2):
                        for c in range(NCH):
                            sl = slice(c * CH, (c + 1) * CH)
                            pq, Rpq = self.psum()
                            for k in range(3):
                                self.mm(pq[0:96, :], wq[:, k, hh * 96:(hh + 1) * 96], cqn[:, k, sl], k == 0, k == 2, (Rwq, Rcq[c]), (Rpq,),
                                        inc=(k == 2))
                            prr, Rprr = self.psum()
                            for k in range(3):
                                self.mm(prr[0:32, :], wqrot[:, k, hh, :], cqn[:, k, sl], k == 0, k == 2, (Rwr, Rcq[c]), (Rprr,), inc=(k == 2))
                            self.acopy(qT[hh][0:64, sl], pq[0:64, :], (Rpq,), (RqT[hh],))
                            self.tt(t1[64:96, :], pq[64:96, :], self.cs[64:96, c * CH:(c + 1) * CH], ALU.mult, (Rpq, self.R_cs), (Rt1,))
                            self.tt(t2[64:96, :], prr[0:32, :], self.cs[0:32, S + c * CH:S + (c + 1) * CH], ALU.mult, (Rprr, self.R_cs), (Rt2,))
                            self.tt(qT[hh][64:96, sl], t1[64:96, :], t2[64:96, :], ALU.add, (Rt1, Rt2), (RqT[hh],))
                    for hh in range(2):
                        vo = hh * 64
                        self.attn_causal(qT[hh], RqT[hh], kT[hh], _Multi((RkT[hh], RkR[hh])), 96,
                                         (lambda jt, vo=vo: vaug[:, jt, vo:vo + 128]), Rva, scale, None, None, hh,
                                         (lambda c, hh=hh, pr=pr: osc[hh * 64:hh * 64 + 64, pr % 2, c * CH:(c + 1) * CH]), Ros, A,
                                         act_recip=True)
                    if pr % 2 == 1:
                        self.out_proj_partial("ev_w_out", j, (pr - 1) * 128, 2, (lambda k, c: osc[:, k, c * CH:(c + 1) * CH]), (Ros,), first[0])
                        first[0] = False


class _Multi:
    def __init__(self, parts):
        self.parts = parts


def _expand(regs):
    out = []
    for r in regs:
        if isinstance(r, _Multi):
            out.extend(r.parts)
        else:
            out.append(r)
    return out


_orig_collect = Ctx._collect
_orig_record = Ctx._record


def _collect2(self, reads, writes, own=()):
    return _orig_collect(self, _expand(reads), _expand(writes), own)


def _record2(self, ev, reads, writes):
    return _orig_record(self, ev, _expand(reads), _expand(writes))


Ctx._collect = _collect2
Ctx._record = _record2


def _t5_bucket(dist):
    exact = 16
    d = np.maximum(dist, 1).astype(np.float32)
    large = exact + (np.log(d / np.float32(exact)) / np.float32(math.log(128 / exact)) * np.float32(32 - exact)).astype(np.int32)
    large = np.minimum(large, 31)
    return np.where(dist < exact, dist, large)


def host_consts(inputs):
    consts = np.zeros((128, 384), np.float32)
    consts[:, 0:128] = np.eye(128, dtype=np.float32)
    s_idx = np.arange(128)[:, None]
    t_idx = np.arange(128)[None, :]
    consts[:, 128:256] = (s_idx <= t_idx).astype(np.float32)
    consts[:, 256:384] = (s_idx > t_idx).astype(np.float32)
    vecs = np.zeros((128, 192), np.float32)

    def pc(v):
        return np.ascontiguousarray(v.reshape(-1, 128).T)
    for i in range(DEPTH):
        b = i * 40
        vecs[:, b + 0:b + 8] = pc(inputs["ln1_g"][i])
        vecs[:, b + 8:b + 16] = pc(inputs["ln1_b"][i])
        vecs[:, b + 16:b + 24] = pc(inputs["ln2_g"][i])
        vecs[:, b + 24:b + 32] = pc(inputs["ln2_b"][i])
        vecs[:, b + 32:b + 40] = pc(inputs["ple_b_gate"][i])
    for j in range(2):
        b = 160 + j * 16
        vecs[:, b:b + 3] = pc(inputs["ev_q_norm"][j])
        vecs[:, b + 3:b + 5] = pc(inputs["ev_kv_norm"][j])
    rows = np.zeros((1, 32), np.float32)
    for j in range(2):
        vecs[:, 160 + j * 16 + 5:160 + j * 16 + 13] = inputs["ev_sinks"][j][None, :]
        rows[0, j * 16:(j + 1) * 16] = inputs["od_b_f"][j]
    inv = (1.0 / (np.float32(10000.0) ** (np.arange(0, 32, 2, dtype=np.float32) / np.float32(32)))).astype(np.float32)
    ang = (np.arange(S, dtype=np.float32)[None, :] * inv[:, None]).astype(np.float32)
    cs = np.zeros((128, 2 * S), np.float32)
    cs[:, 0:S] = np.tile(np.cos(ang), (8, 1))
    cs[:, S:2 * S] = np.tile(np.sin(ang), (8, 1))
    kk = np.arange(128)[:, None]
    a = np.arange(128)[None, :]
    biasg = np.zeros((128, 2, 8, 128), np.float32)
    for kt in range(2):
        dist = (a + 128 - kk) if kt == 0 else (a - kk)
        bk = _t5_bucket(np.maximum(dist, 0))
        g = inputs["rel_bias"][bk]
        biasg[:, kt] = np.transpose(g, (0, 2, 1))
    return dict(consts=consts, vecs=vecs, rows=rows, cs=cs, biasg=np.ascontiguousarray(biasg.reshape(128, 2048)))


WEIGHT_NAMES = ["ev_w_in", "ev_w_uq", "ev_w_ukv", "ev_w_out", "od_w_in", "od_w_out", "w_up", "w_down", "ple_w_proj", "ple_w_gate"]

_CACHE = {}


def get_builder(nseq, layers, stop=None):
    key = (nseq, tuple(layers), stop)
    if key not in _CACHE:
        _CACHE[key] = Builder(nseq, layers, stop)
    return _CACHE[key]


def run(inputs, n_cores, nseq, layers=(0, 1, 2, 3), stop=None, batch0=0):
    bld = get_builder(nseq, layers, stop)
    hc = host_consts(inputs)
    in_maps = []
    for c in range(n_cores):
        b0 = batch0 + c * nseq
        m = {"x": np.ascontiguousarray(inputs["x"][b0:b0 + nseq]),
             "p": np.ascontiguousarray(inputs["p"][:, b0:b0 + nseq])}
        for wn in WEIGHT_NAMES:
            m[wn] = np.ascontiguousarray(inputs[wn])
        m.update(hc)
        in_maps.append(m)
    res = run_bass_kernel_spmd(bld.nc, in_maps, core_ids=list(range(n_cores)))
    return np.concatenate([np.asarray(r["out"]) for r in res.results], axis=0)


def kernel(**inputs):
    inputs = {k: np.asarray(v) for k, v in inputs.items()}
    out = run(inputs, N_CORES, SEQ_PER_CORE)
    return out.astype(np.float32)
```

```python
import math
from contextlib import ExitStack, contextmanager

import numpy as np
import concourse.bass as bass
import concourse.mybir as mybir
from concourse.bass_utils import run_bass_kernel_spmd

F32 = mybir.dt.float32
BF16 = mybir.dt.bfloat16
AF = mybir.ActivationFunctionType
ALU = mybir.AluOpType

S = 2048
D = 1024
NT = 16
NCH = 4
CH = 512
DC = 8
DEPTH = 4
DFF = 4096
ALPHA = float((2 * DEPTH) ** 0.25)
EPS = 1e-5
N_CORES = 8
SEQ_PER_CORE = 4
SEM_LIMIT = 30000
WSLOT = 2048
NWSLOT = 5
WHOLD = 3

EV_COLS = dict(cq=0, ckv=384, kr=640, qs=672, ks=1184, vs=1312)


class Reg:
    __slots__ = ("name", "w", "r", "dsem", "dcnt", "excl")

    def __init__(self, name):
        self.name = name
        self.excl = False
        self.w = None
        self.r = {}
        self.dsem = None
        self.dcnt = 0


class Eng:
    def __init__(self, name, eng):
        self.name = name
        self.eng = eng
        self.sem = None
        self.cnt = 0
        self.seen = {}
        self.pending = False
        self.own = set()


class Ctx:
    def __init__(self, nc, es):
        self.nc = nc
        self.es = es
        self.dry = False
        self.nsem = 0
        self.E = {
            "pe": Eng("pe", nc.tensor),
            "act": Eng("act", nc.scalar),
            "dve": Eng("dve", nc.vector),
            "pool": Eng("pool", nc.gpsimd),
            "sp": Eng("sp", nc.sync),
        }
        for e in self.E.values():
            self._new_clock(e)
        self.all_regs = []
        self.dsem_pool = []

    def recycle(self, mark):
        for r in self.all_regs[mark:]:
            if r.dsem is not None:
                self.dsem_pool.append((r.dsem, r.dcnt))
                r.dsem = None
        del self.all_regs[mark:]

    def new_sem(self, name):
        self.nsem += 1
        return self.es.enter_context(self.nc.semaphore("%s_%d" % (name, self.nsem)))

    def _new_clock(self, e):
        e.sem = self.new_sem("clk_" + e.name)
        e.cnt = 0
        e.own.add(id(e.sem))

    def reg(self, name):
        r = Reg(name)
        self.all_regs.append(r)
        return r

    def _wait(self, E, evs):
        best = {}
        for (sem, val) in evs:
            k = id(sem)
            if k not in best or best[k][1] < val:
                best[k] = (sem, val)
        for k, (sem, val) in best.items():
            if k in E.own and E.name == "pe":
                continue
            if E.seen.get(k, 0) >= val:
                continue
            E.eng.wait_ge(sem, val)
            E.seen[k] = val

    def _collect(self, reads, writes, own=()):
        evs = []
        for r in reads:
            if r.w is not None:
                evs.append(r.w)
            if r.excl:
                evs.extend(e for e in r.r.values() if id(e[0]) not in own)
        for w in writes:
            if w.w is not None and id(w.w[0]) not in own:
                evs.append(w.w)
            evs.extend(e for e in w.r.values() if id(e[0]) not in own)
        return evs

    def _record(self, ev, reads, writes):
        k = id(ev[0])
        for r in reads:
            old = r.r.get(k)
            if old is None or old[1] < ev[1]:
                r.r[k] = ev
        for w in writes:
            w.w = ev
            w.r = {}

    def op(self, en, fn, reads=(), writes=(), inc=True):
        if self.dry:
            return
        E = self.E[en]
        self._wait(E, self._collect(reads, writes, E.own))
        ins = fn()
        if inc:
            if E.cnt >= SEM_LIMIT and not E.pending:
                self._new_clock(E)
            E.cnt += 1
            ins.then_inc(E.sem, 1)
            E.pending = False
            ev = (E.sem, E.cnt)
        else:
            E.pending = True
            ev = (E.sem, E.cnt + 1)
        self._record(ev, reads, writes)

    def dma(self, q, out, in_, reads=(), writes=(), sem_reg=None):
        if self.dry:
            return
        E = self.E[q]
        self._wait(E, self._collect(reads, writes))
        sr = sem_reg or (writes[0] if writes else reads[0])
        if sr.dsem is None:
            if self.dsem_pool:
                sr.dsem, sr.dcnt = self.dsem_pool.pop()
            else:
                sr.dsem = self.new_sem("dma")
                sr.dcnt = 0
        ins = E.eng.dma_start(out=out, in_=in_)
        sr.dcnt += 16
        ins.then_inc(sr.dsem, 16)
        ev = (sr.dsem, sr.dcnt)
        self._record(ev, reads, writes)

    def barrier(self):
        if self.dry:
            return
        evs = []
        for e in self.E.values():
            if e.cnt > 0:
                evs.append((e.sem, e.cnt))
        for r in self.all_regs:
            if r.w is not None:
                evs.append(r.w)
            evs.extend(r.r.values())
        for e in self.E.values():
            self._wait(e, evs)

    def final_wait(self, regs):
        if self.dry:
            return
        evs = []
        for r in regs:
            if r.w is not None:
                evs.append(r.w)
            evs.extend(r.r.values())
        for e in self.E.values():
            if e.cnt > 0:
                evs.append((e.sem, e.cnt))
        self._wait(self.E["sp"], evs)


class WStream:
    def __init__(self, cx, nc, es):
        self.cx = cx
        self.slots = [es.enter_context(nc.sbuf_tensor("s_wslot%d" % i, [128, WSLOT], BF16)) for i in range(NWSLOT)]
        self.regs = [cx.reg("wslot%d" % i) for i in range(NWSLOT)]
        for r in self.regs:
            r.dsem = cx.new_sem("wdma")
            r.dcnt = 0
        self.plan = []
        self.pos = 0
        self.loaded = 0

    def reset(self):
        self.pos = 0
        self.loaded = 0

    def _view(self, i, kc, ncols):
        return self.slots[i % NWSLOT][:, 0:kc * ncols].rearrange("p (k n) -> p k n", k=kc)

    def _load(self, j):
        (w, r0, nrows, c0, ncols) = self.plan[j]
        kc = nrows // 128
        src = w[r0:r0 + nrows, c0:c0 + ncols].rearrange("(k p) n -> p k n", p=128)
        self.cx.dma("pool", self._view(j, kc, ncols), src, reads=(), writes=(self.regs[j % NWSLOT],))

    def get(self, w, r0, nrows, c0, ncols):
        kc = nrows // 128
        assert kc * ncols <= WSLOT and nrows % 128 == 0
        i = self.pos
        if self.cx.dry:
            self.plan.append((w, r0, nrows, c0, ncols))
        else:
            pl = self.plan[i]
            assert pl[1:] == (r0, nrows, c0, ncols), (pl[1:], (r0, nrows, c0, ncols))
            while self.loaded < min(len(self.plan), i + NWSLOT - WHOLD + 1):
                self._load(self.loaded)
                self.loaded += 1
        self.pos += 1
        return self._view(i, kc, ncols), self.regs[i % NWSLOT]


class Builder:
    def __init__(self, nseq, layers, stop=None):
        self.nseq = nseq
        self.layers = list(layers)
        self.stop = stop
        nc = bass.Bass("TRN2", target_bir_lowering=False)
        self.nc = nc
        dt = nc.dram_tensor
        self.x_d = dt("x", [nseq, S, D], F32, kind="ExternalInput").ap()
        self.p_d = dt("p", [DEPTH, nseq, S, 256], F32, kind="ExternalInput").ap()
        self.out_d = dt("out", [nseq, S, D], F32, kind="ExternalOutput").ap()
        self.cscr_d = dt("cscr", [16, 3, S], BF16, kind="ExternalOutput").ap()
        self.w = {}
        for name, shape in [("ev_w_in", [2, 1024, 1440]), ("ev_w_uq", [2, 384, 768]), ("ev_w_ukv", [2, 256, 1024]),
                            ("ev_w_out", [2, 1024, 1024]), ("od_w_in", [2, 1024, 3088]), ("od_w_out", [2, 1024, 1024]),
                            ("w_up", [4, 1024, 4096]), ("w_down", [4, 4096, 1024]), ("ple_w_proj", [4, 256, 1024]),
                            ("ple_w_gate", [4, 1024, 1024])]:
            self.w[name] = dt(name, shape, F32, kind="ExternalInput").ap()
        self.consts_d = dt("consts", [128, 384], F32, kind="ExternalInput").ap()
        self.vecs_d = dt("vecs", [128, 192], F32, kind="ExternalInput").ap()
        self.rows_d = dt("rows", [1, 32], F32, kind="ExternalInput").ap()
        self.cs_d = dt("cs", [128, 2 * S], F32, kind="ExternalInput").ap()
        self.biasg_d = dt("biasg", [128, 2 * 8 * 128], F32, kind="ExternalInput").ap()
        with ExitStack() as es:
            self.es = es
            self.cx = Ctx(nc, es)
            self.alloc_persistent()
            self.cx.dry = True
            self.program()
            self.cx.dry = False
            self.W.reset()
            self.psi = 0
            self.program()

    @contextmanager
    def phase(self):
        cx = self.cx
        mark = len(cx.all_regs)
        with ExitStack() as ph:
            yield ph
            cx.barrier()
            cx.recycle(mark)

    def sb(self, stack, name, shape, dtype):
        self._nm = getattr(self, "_nm", 0) + 1
        return stack.enter_context(self.nc.sbuf_tensor("s%d_%s" % (self._nm, name), shape, dtype))

    def alloc_persistent(self):
        es, nc, cx = self.es, self.nc, self.cx
        self.x_f32 = self.sb(es, "x_f32", [128, DC, S], F32)
        self.xT_bf = self.sb(es, "xT_bf", [128, DC, S], BF16)
        self.R_xfm = [[cx.reg("xf%d_%d" % (c, m)) for m in range(DC)] for c in range(NCH)]
        self.R_xf = [_Multi(self.R_xfm[c]) for c in range(NCH)]
        self.R_xbm = [[cx.reg("xb%d_%d" % (c, m)) for m in range(DC)] for c in range(NCH)]
        self.R_xb = [_Multi(self.R_xbm[c]) for c in range(NCH)]
        self.W = WStream(cx, nc, es)
        self.consts = self.sb(es, "consts", [128, 384], F32)
        self.R_consts = cx.reg("consts")
        self.cbf = self.sb(es, "cbf", [128, 640], BF16)
        self.vecs = self.sb(es, "vecs", [128, 192], F32)
        self.rows = self.sb(es, "rows", [1, 32], F32)
        self.small = self.sb(es, "small", [128, 8], F32)
        self.onesf = self.sb(es, "onesf", [128, 128], F32)
        self.cs = self.sb(es, "cs", [128, 2 * S], BF16)
        self.R_cs = cx.reg("cs")
        self.R_cs.dsem = cx.new_sem("csdma")
        self.R_cs.dcnt = 0
        self.b8m = self.sb(es, "b8m", [128, 2 * 8 * 128], BF16)
        self.R_ebm = cx.reg("b8m")
        self.ps = es.enter_context(nc.psum_tensor("psum_all", [128, 8, 512], F32))
        self.R_ps = [cx.reg("ps%d" % b) for b in range(8)]
        for r in self.R_ps:
            r.excl = True
        self.R_cscr = cx.reg("cscr")
        self.R_out = cx.reg("outdma")
        self.psi = 0

    def psum(self):
        b = self.psi % 8
        self.psi += 1
        return self.ps[:, b, :], self.R_ps[b]

    def mm(self, out, lhsT, rhs, start, stop, reads, writes, inc):
        nc = self.nc
        self.cx.op("pe", lambda: nc.tensor.matmul(out, lhsT=lhsT, rhs=rhs, start=start, stop=stop,
                                                  skip_group_check=True), reads, writes, inc)

    def tr(self, out, in_, reads, writes, inc):
        nc = self.nc
        ident = self.consts[:, 0:128]
        self.cx.op("pe", lambda: nc.tensor.transpose(out=out, in_=in_, identity=ident), tuple(reads) + (self.R_consts,),
                   writes, inc)

    def act(self, out, in_, func, reads, writes, bias=None, scale=None):
        nc = self.nc
        kw = {}
        if bias is not None:
            kw["bias"] = bias
        if scale is not None:
            kw["scale"] = scale
        self.cx.op("act", lambda: nc.scalar.activation(out=out, in_=in_, func=func, **kw), reads, writes)

    def tt(self, out, in0, in1, op, reads, writes, eng="dve"):
        nc = self.nc
        e = nc.vector if eng == "dve" else nc.gpsimd
        self.cx.op(eng, lambda: e.tensor_tensor(out=out, in0=in0, in1=in1, op=op), reads, writes)

    def stt(self, out, in0, scalar, in1, op0, op1, reads, writes):
        nc = self.nc
        self.cx.op("dve", lambda: nc.vector.scalar_tensor_tensor(out=out, in0=in0, scalar=scalar, in1=in1, op0=op0, op1=op1),
                   reads, writes)

    def ts(self, out, in0, s1, s2, op0, op1, reads, writes):
        nc = self.nc
        if op1 is None:
            self.cx.op("dve", lambda: nc.vector.tensor_scalar(out=out, in0=in0, scalar1=s1, scalar2=None, op0=op0), reads, writes)
        else:
            self.cx.op("dve", lambda: nc.vector.tensor_scalar(out=out, in0=in0, scalar1=s1, scalar2=s2, op0=op0, op1=op1),
                       reads, writes)

    def vcopy(self, out, in_, reads, writes):
        nc = self.nc
        self.cx.op("dve", lambda: nc.vector.tensor_copy(out=out, in_=in_), reads, writes)

    def acopy(self, out, in_, reads, writes):
        self.act(out, in_, AF.Identity, reads, writes)

    def memset(self, ap, val, writes, eng="dve"):
        nc = self.nc
        e = nc.vector if eng == "dve" else nc.gpsimd
        self.cx.op(eng, lambda: e.memset(ap, val), (), writes)

    def recip(self, out, in_, reads, writes):
        nc = self.nc
        self.cx.op("dve", lambda: nc.vector.reciprocal(out=out, in_=in_), reads, writes)

    def pool_recip(self, buf, reads, writes):
        nc = self.nc
        ones = self.ones512[buf.base_partition():buf.base_partition() + 64, :]
        self.cx.op("pool", lambda: nc.gpsimd.tensor_tensor(out=buf, in0=buf, in1=ones, op=ALU.pow), reads, writes)

    def program(self):
        cx = self.cx
        import os
        dbg = int(os.environ.get("KDBG", "9"))
        if dbg >= 1:
            self.setup_consts()
        if dbg < 3:
            if dbg >= 2:
                self.load_x(0)
            cx.final_wait([self.R_out, self.R_cscr])
            return
        for s in range(self.nseq):
            self.load_x(s)
            done = False
            for li in self.layers:
                if li % 2 == 0:
                    self.even_mixer(li)
                else:
                    self.odd_mixer(li)
                if self.stop == (li, "mix"):
                    break
                with self.phase() as ph:
                    T = self.ln_alloc(ph)
                    self.layer_norm(li, 0, T)
                    if self.stop != (li, "ln1"):
                        self.ffn(li, ph)
                        if self.stop == (li, "ln2"):
                            self.layer_norm(li, 1, T)
                if self.stop in ((li, "ln1"), (li, "ln2")):
                    break
                self.ln2_ple(li, s)
            self.store_out(s)
        cx.final_wait([self.R_out, self.R_cscr])

    def setup_consts(self):
        cx = self.cx
        Rc = self.R_consts
        cx.dma("sp", self.consts[:], self.consts_d[:, :], writes=(Rc,))
        cx.dma("sp", self.vecs[:], self.vecs_d[:, :], writes=(Rc,))
        cx.dma("sp", self.rows[:], self.rows_d[:, :], writes=(Rc,))
        cx.dma("pool", self.cs[:], self.cs_d[:, :], writes=(self.R_cs,))
        self.memset(self.cbf[:, 0:128], 1.0, (Rc,))
        self.vcopy(self.cbf[:, 128:384], self.consts[:, 128:384], (Rc,), (Rc,))
        self.ts(self.cbf[:, 384:512], self.consts[:, 128:256], -1.0, 30000.0, ALU.add, ALU.mult, (Rc,), (Rc,))
        self.vcopy(self.cbf[:, 512:640], self.consts[:, 0:128], (Rc,), (Rc,))
        self.memset(self.small[:, 0:1], EPS, (Rc,))
        self.memset(self.small[:, 1:2], 1.0, (Rc,))
        self.memset(self.onesf[:], 1.0, (Rc,))
        with self.phase() as ph:
            tmp = self.sb(ph, "biasg_tmp", [128, 2048], F32)
            Rt = cx.reg("biasg_tmp")
            cx.dma("sp", tmp[:], self.biasg_d[:, :], writes=(Rt,))
            negm = self.sb(ph, "negm", [128, 128], F32)
            Rnm = cx.reg("negm")
            for kt in range(2):
                mask = self.consts[:, 256:384] if kt == 0 else self.consts[:, 128:256]
                self.ts(negm[:], mask, -1.0, 30000.0, ALU.add, ALU.mult, (Rc, Rnm), (Rnm,))
                for h in range(8):
                    o = (kt * 8 + h) * 128
                    self.tt(tmp[:, o:o + 128], tmp[:, o:o + 128], mask, ALU.mult, (Rc, Rt), (Rt,))
                    self.stt(self.b8m[:, o:o + 128], tmp[:, o:o + 128], 8.0, negm[:], ALU.mult, ALU.add, (Rt, Rnm), (self.R_ebm,))
            for jj in range(2):
                sc = 160 + jj * 16 + 5
                self.act(self.vecs[:, sc:sc + 8], self.vecs[:, sc:sc + 8], AF.Exp, (Rc,), (Rc,))

    def load_x(self, s):
        cx = self.cx
        with self.phase() as ph:
            xin = [self.sb(ph, "xin%d" % i, [128, D], F32) for i in range(4)]
            Rin = [cx.reg("xin%d" % i) for i in range(4)]
            for t in range(NT):
                b = t % 4
                c = t // 4
                cx.dma("sp", xin[b][:], self.x_d[s, t * 128:(t + 1) * 128, :], writes=(Rin[b],))
                for half in range(2):
                    pt, Rp = self.psum()
                    for j in range(4):
                        fc = half * 4 + j
                        self.tr(pt[:, j * 128:(j + 1) * 128], xin[b][:, fc * 128:(fc + 1) * 128], (Rin[b],), (Rp,), inc=(j == 3))
                    src = pt.rearrange("p (a b) -> p a b", a=4)
                    xf = self.x_f32[:, half * 4:half * 4 + 4, t * 128:(t + 1) * 128]
                    self.vcopy(xf, src, (Rp,), (self.R_xf[c],))
                    self.acopy(self.xT_bf[:, half * 4:half * 4 + 4, t * 128:(t + 1) * 128], xf, (self.R_xf[c],), (self.R_xb[c],))

    def store_out(self, s):
        cx = self.cx
        with self.phase() as ph:
            xo = [self.sb(ph, "xout%d" % i, [128, D], F32) for i in range(4)]
            Ro = [cx.reg("xout%d" % i) for i in range(4)]
            for t in range(NT):
                b = t % 4
                c = t // 4
                for half in range(2):
                    pt, Rp = self.psum()
                    for j in range(4):
                        fc = half * 4 + j
                        self.tr(pt[:, j * 128:(j + 1) * 128], self.x_f32[:, fc, t * 128:(t + 1) * 128], (self.R_xf[c],), (Rp,), inc=(j == 3))
                    if half == 0:
                        self.vcopy(xo[b][:, 0:512], pt, (Rp,), (Ro[b],))
                    else:
                        self.acopy(xo[b][:, 512:1024], pt, (Rp,), (Ro[b],))
                cx.dma("sp", self.out_d[s, t * 128:(t + 1) * 128, :], xo[b][:], reads=(Ro[b],), writes=(), sem_reg=self.R_out)
                if not cx.dry:
                    self.R_out.r[id(self.R_out.dsem)] = (self.R_out.dsem, self.R_out.dcnt)

    def resid_acc(self, m, c, pt, Rp, first):
        xf = self.x_f32[:, m, c * CH:(c + 1) * CH]
        if first:
            self.stt(xf, xf, ALPHA, pt, ALU.mult, ALU.add, (Rp, self.R_xfm[c][m]), (self.R_xfm[c][m],))
        else:
            self.tt(xf, xf, pt, ALU.add, (Rp, self.R_xfm[c][m]), (self.R_xfm[c][m],))

    def out_proj_partial(self, wname, j, r0, nkc, rhs_fn, Rrhs, first):
        wt, Rw = self.W.get(self.w[wname][j], r0, nkc * 128, 0, 1024)
        for c in range(NCH):
            for m in range(DC):
                pt, Rp = self.psum()
                for k in range(nkc):
                    self.mm(pt, wt[:, k, m * 128:(m + 1) * 128], rhs_fn(k, c), k == 0, k == nkc - 1,
                            (Rw,) + tuple(Rrhs), (Rp,), inc=(k == nkc - 1))
                self.resid_acc(m, c, pt, Rp, first)

    def ln_alloc(self, ph):
        cx = self.cx
        T = {}
        T["xb"] = self.sb(ph, "ln_xb", [128, DC, CH], BF16)
        T["xq"] = self.sb(ph, "ln_xq", [128, DC, CH], BF16)
        T["mean"] = self.sb(ph, "ln_mean", [128, CH], F32)
        T["rstd"] = [self.sb(ph, "ln_rstd%d" % i, [128, CH], F32) for i in range(2)]
        T["bm"] = [self.sb(ph, "ln_bm%d" % i, [128, CH], F32) for i in range(2)]
        T["u"] = [self.sb(ph, "ln_u%d" % i, [128, CH], F32) for i in range(2)]
        T["R"] = (cx.reg("ln_xb"), cx.reg("ln_xq"), cx.reg("ln_mean"))
        T["Rst"] = [cx.reg("ln_st%d" % i) for i in range(2)]
        T["Ru"] = [cx.reg("ln_u%d" % i) for i in range(2)]
        return T

    def layer_norm(self, li, which, T):
        self.ln_stats(T, 0)
        for c in range(NCH):
            if c + 1 < NCH:
                self.ln_stats(T, c + 1)
            self.ln_norm(li, which, T, c)

    def ln_chunk(self, li, which, T, c):
        self.ln_stats(T, c)
        self.ln_norm(li, which, T, c)

    def ln_stats(self, T, c):
        ones_bf = self.cbf[:, 0:128]
        xb, xq, mean = T["xb"], T["xq"], T["mean"]
        rstd, bm = T["rstd"][c % 2], T["bm"][c % 2]
        Rxb, Rxq, Rmean = T["R"]
        Rst = T["Rst"][c % 2]
        sl = slice(c * CH, (c + 1) * CH)
        Rx = self.R_xf[c]
        self.acopy(xb[:], self.x_f32[:, :, sl], (Rx,), (Rxb,))
        self.act(xq[:], self.x_f32[:, :, sl], AF.Square, (Rx,), (Rxq,))
        p1, R1 = self.psum()
        for k in range(DC):
            self.mm(p1, ones_bf, xb[:, k, :], k == 0, k == DC - 1, (Rxb, self.R_consts), (R1,), inc=(k == DC - 1))
        p2, R2 = self.psum()
        for k in range(DC):
            self.mm(p2, ones_bf, xq[:, k, :], k == 0, k == DC - 1, (Rxq, self.R_consts), (R2,), inc=(k == DC - 1))
        self.ts(mean[:], p1, 1.0 / D, None, ALU.mult, None, (R1,), (Rmean,))
        self.tt(bm[:], mean[:], mean[:], ALU.mult, (Rmean,), (Rst,))
        self.stt(rstd[:], p2, 1.0 / D, bm[:], ALU.mult, ALU.subtract, (R2, Rst), (Rst,))
        self.act(rstd[:], rstd[:], AF.Ln, (Rst,), (Rst,), bias=self.small[:, 0:1], scale=1.0)
        self.act(rstd[:], rstd[:], AF.Exp, (Rst,), (Rst,), scale=-0.5)
        self.stt(bm[:], mean[:], -1.0, rstd[:], ALU.mult, ALU.mult, (Rmean, Rst), (Rst,))

    def ln_norm(self, li, which, T, c):
        gcol = li * 40 + which * 16
        rstd, bm = T["rstd"][c % 2], T["bm"][c % 2]
        Rst = T["Rst"][c % 2]
        u, Ru = T["u"], T["Ru"]
        sl = slice(c * CH, (c + 1) * CH)
        for m in range(DC):
            Rx = self.R_xfm[c][m]
            i = m % 2
            g = self.vecs[:, gcol + m:gcol + m + 1]
            b = self.vecs[:, gcol + 8 + m:gcol + 8 + m + 1]
            xf = self.x_f32[:, m, sl]
            self.stt(u[i][:], xf, g, rstd[:], ALU.mult, ALU.mult, (Rx, Rst, self.R_consts), (Ru[i],))
            self.stt(u[i][:], bm[:], g, u[i][:], ALU.mult, ALU.add, (Rst, Ru[i]), (Ru[i],))
            self.act(xf, u[i][:], AF.Identity, (Ru[i],), (Rx,), bias=b, scale=1.0)
            self.act(self.xT_bf[:, m, sl], u[i][:], AF.Identity, (Ru[i],), (self.R_xb[c],), bias=b, scale=1.0)

    def ffn(self, li, ph):
        cx = self.cx
        wup = self.w["w_up"][li]
        wdn = self.w["w_down"][li]
        if True:
            hT = self.sb(ph, "hT", [128, 8, S], BF16)
            rl = [self.sb(ph, "relu%d" % i, [128, CH], F32) for i in range(2)]
            Rh = [cx.reg("hT%d" % c) for c in range(NCH)]
            Rr = [cx.reg("relu%d" % i) for i in range(2)]
            n = 0
            for g in range(4):
                for wi in range(4):
                    wt, Rw = self.W.get(wup, 0, 1024, g * 1024 + wi * 256, 256)
                    for c in range(NCH):
                        for f in range(2):
                            fi = wi * 2 + f
                            pt, Rp = self.psum()
                            for k in range(DC):
                                self.mm(pt, wt[:, k, f * 128:(f + 1) * 128], self.xT_bf[:, k, c * CH:(c + 1) * CH],
                                        k == 0, k == DC - 1, (Rw, self.R_xb[c]), (Rp,), inc=(k == DC - 1))
                            i = n % 2
                            n += 1
                            self.act(rl[i][:], pt, AF.Relu, (Rp,), (Rr[i],))
                            self.tt(hT[:, fi, c * CH:(c + 1) * CH], rl[i][:], rl[i][:], ALU.mult, (Rr[i],), (Rh[c],))
                for wi in range(4):
                    wt, Rw = self.W.get(wdn, g * 1024, 1024, wi * 256, 256)
                    for c in range(NCH):
                        for mm_ in range(2):
                            m = wi * 2 + mm_
                            pt, Rp = self.psum()
                            for k in range(8):
                                self.mm(pt, wt[:, k, mm_ * 128:(mm_ + 1) * 128], hT[:, k, c * CH:(c + 1) * CH],
                                        k == 0, k == 7, (Rw, Rh[c]), (Rp,), inc=(k == 7))
                            self.resid_acc(m, c, pt, Rp, g == 0)

    def ln2_ple(self, li, s):
        cx = self.cx
        wg = self.w["ple_w_gate"][li]
        wp = self.w["ple_w_proj"][li]
        bcol = li * 40 + 32
        with self.phase() as ph:
            T = self.ln_alloc(ph)
            pin = [self.sb(ph, "pin%d" % i, [128, 2, 256], F32) for i in range(2)]
            pT = self.sb(ph, "pT", [128, 2, S], BF16)
            gt = [self.sb(ph, "gate%d" % i, [128, CH], F32) for i in range(2)]
            tq = [self.sb(ph, "gprod%d" % i, [128, CH], F32) for i in range(2)]
            wgl = self.sb(ph, "wgate", [128, 8, 1024], BF16)
            wpl = self.sb(ph, "wproj", [128, 2, 1024], BF16)
            Rpin = [cx.reg("pin%d" % i) for i in range(2)]
            RpT = [cx.reg("pT%d" % c) for c in range(NCH)]
            Rg = [cx.reg("gate%d" % i) for i in range(2)]
            Rq = [cx.reg("gprod%d" % i) for i in range(2)]
            Rwg = [cx.reg("wgate%d" % i) for i in range(4)]
            Rwp = cx.reg("wproj")
            self.ln_stats(T, 0)
            for hc in range(2 * NCH):
                b = hc % 2
                c = hc // 2
                src = self.p_d[li, s, hc * 256:(hc + 1) * 256, :].rearrange("(t p) f -> p t f", p=128)
                cx.dma("sp", pin[b][:], src, writes=(Rpin[b],))
                for k2 in range(2):
                    pt, Rp = self.psum()
                    for t in range(2):
                        self.tr(pt[:, t * 128:(t + 1) * 128], pin[b][:, t, k2 * 128:(k2 + 1) * 128], (Rpin[b],), (Rp,), inc=(t == 1))
                    self.acopy(pT[:, k2, hc * 256:(hc + 1) * 256], pt[:, 0:256], (Rp,), (RpT[c],))
            wp_ring, Rwp_ring = self.W.get(wp, 0, 256, 0, 1024)
            self.vcopy(wpl[:], wp_ring, (Rwp_ring,), (Rwp,))
            for wi in range(4):
                wt, Rw = self.W.get(wg, 0, 1024, wi * 256, 256)
                if wi % 2 == 0:
                    self.vcopy(wgl[:, :, wi * 256:(wi + 1) * 256], wt, (Rw,), (Rwg[wi],))
                else:
                    self.acopy(wgl[:, :, wi * 256:(wi + 1) * 256], wt, (Rw,), (Rwg[wi],))
            cnt = [0]

            def ple_chunk(c):
                sl = slice(c * CH, (c + 1) * CH)
                for m in range(DC):
                    i = cnt[0] % 2
                    cnt[0] += 1
                    pg, Rpg = self.psum()
                    for k in range(DC):
                        self.mm(pg, wgl[:, k, m * 128:(m + 1) * 128], self.xT_bf[:, k, sl],
                                k == 0, k == DC - 1, (Rwg[m // 2], self.R_xb[c]), (Rpg,), inc=(k == DC - 1))
                    pp, Rpp = self.psum()
                    for k in range(2):
                        self.mm(pp, wpl[:, k, m * 128:(m + 1) * 128], pT[:, k, sl],
                                k == 0, k == 1, (Rwp, RpT[c]), (Rpp,), inc=(k == 1))
                    self.act(gt[i][:], pg, AF.Sigmoid, (Rpg, self.R_consts), (Rg[i],),
                             bias=self.vecs[:, bcol + m:bcol + m + 1], scale=1.0)
                    self.tt(tq[i][:], gt[i][:], pp, ALU.mult, (Rg[i], Rpp), (Rq[i],))
                    xf = self.x_f32[:, m, sl]
                    self.tt(xf, xf, tq[i][:], ALU.add, (Rq[i], self.R_xfm[c][m]), (self.R_xfm[c][m],))
                for m in range(DC):
                    if m % 2 == 0:
                        self.acopy(self.xT_bf[:, m, sl], self.x_f32[:, m, sl], (self.R_xfm[c][m],), (self.R_xb[c],))
                    else:
                        self.vcopy(self.xT_bf[:, m, sl], self.x_f32[:, m, sl], (self.R_xfm[c][m],), (self.R_xb[c],))

            for step in range(NCH + 2):
                if step + 1 < NCH:
                    self.ln_stats(T, step + 1)
                if step < NCH:
                    self.ln_norm(li, 1, T, step)
                if 1 <= step <= NCH:
                    ple_chunk(step - 1)

    def attn_causal(self, qT, Rq, kT, Rk, KD, vaug_fn, Rv, scale, kbias_fn, Rkb, orient, dst_fn, Rdst, A, act_recip=False):
        tri = self.cbf[:, 128:256]
        ro = 0 if orient == 0 else 64
        rd = 64 - ro
        steps = [(c, j) for c in range(NCH) for j in range(4 * c + 4)]
        LA = 3
        st = {}
        accs = {}

        def front(i):
            c, j = steps[i]
            lo = max(0, j - 4 * c) * 128
            sp_, Rsp = A["sc"][A["si"] % 4]
            pt_, Rpt = A["pt"][A["si"] % 4]
            A["si"] += 1
            diag = j >= 4 * c
            self.mm(sp_[:, lo:CH], kT[0:KD, j * 128:(j + 1) * 128], qT[0:KD, c * CH + lo:(c + 1) * CH], True, not diag,
                    (Rq, Rk), (Rsp,), inc=not diag)
            if diag:
                self.mm(sp_[:, lo:lo + 128], self.cbf[:, 512:640], self.cbf[:, 384:512], False, True, (self.R_consts,), (Rsp,), inc=True)
            kw = {}
            rds = (Rsp,)
            if kbias_fn is not None:
                kw["bias"] = kbias_fn(j)
                rds = (Rsp, Rkb)
            self.act(pt_[:, lo:CH], sp_[:, lo:CH], AF.Exp, rds, (Rpt,), scale=scale, **kw)
            st[i] = (pt_, Rpt, lo)

        def back(i):
            c, j = steps[i]
            nj = 4 * c + 4
            if j == 0:
                accs[c] = A["acc"][A["ai"] % 2]
                A["ai"] += 1
            acc, Racc = accs[c]
            pt_, Rpt, lo = st.pop(i)
            self.mm(acc[:, lo:CH], vaug_fn(j), pt_[:, lo:CH], j == 0, j == nj - 1, (Rv, Rpt), (Racc,), inc=True)
            if j == nj - 1:
                rec, Rrec = A["rec"][A["ri"] % 2]
                A["ri"] += 1
                if act_recip and c % 2 == 0:
                    self.act(rec[ro:ro + 64, :], acc[rd:rd + 64, :], AF.Ln, (Racc,), (Rrec,))
                    self.act(rec[ro:ro + 64, :], rec[ro:ro + 64, :], AF.Exp, (Rrec,), (Rrec,), scale=-1.0)
                else:
                    self.recip(rec[ro:ro + 64, :], acc[rd:rd + 64, :], (Racc,), (Rrec,))
                self.tt(dst_fn(c), acc[ro:ro + 64, :], rec[ro:ro + 64, :], ALU.mult, (Racc, Rrec), (Rdst,))

        n = len(steps)
        for i in range(n + LA):
            if i < n:
                front(i)
            if i >= LA:
                back(i - LA)

    def attn_bufs(self, ph):
        cx = self.cx
        A = {"ai": 0, "si": 0, "ri": 0}
        A["sc"] = [(self.ps[:, b, :], self.R_ps[b]) for b in range(4)]
        A["acc"] = [(self.ps[:, 4 + b, :], self.R_ps[4 + b]) for b in range(2)]
        A["pt"] = []
        for i in range(4):
            A["pt"].append((self.sb(ph, "ptile%d" % i, [128, CH], BF16), cx.reg("ptile%d" % i)))
        A["rec"] = []
        for i in range(2):
            A["rec"].append((self.sb(ph, "rec%d" % i, [128, CH], F32), cx.reg("rec%d" % i)))
        return A

    def psum67(self):
        b = 6 + (self.psi % 2)
        self.psi += 1
        return self.ps[:, b, :], self.R_ps[b]

    def odd_mixer(self, li):
        cx = self.cx
        j = li // 2
        win = self.w["od_w_in"][j]
        tri_f = self.consts[:, 128:256]
        with self.phase() as ph:
            A = self.attn_bufs(ph)
            ltok = self.sb(ph, "ltok", [128, 256], F32)
            negc = self.sb(ph, "negc", [128, 256], F32)
            Rl, Rn = cx.reg("ltok"), cx.reg("negc")
            wt, Rw = self.W.get(win, 0, 1024, 3072, 16)
            pf, Rpf = self.psum()
            brow = self.rows[0:1, j * 16:(j + 1) * 16]
            for t in range(NT):
                for k in range(DC):
                    self.mm(pf[:, t * 16:(t + 1) * 16], self.xT_bf[:, k, t * 128:(t + 1) * 128], wt[:, k, :], k == 0, False,
                            (Rw, self.R_xb[t // 4]), (Rpf,), inc=False)
                self.mm(pf[:, t * 16:(t + 1) * 16], self.onesf[0:1, 0:128], brow, False, True, (self.R_consts,), (Rpf,), inc=True)
            self.act(ltok[:], pf[:, 0:256], AF.Exp, (Rpf,), (Rl,), scale=-1.0)
            self.act(ltok[:], ltok[:], AF.Ln, (Rl, self.R_consts), (Rl,), bias=self.small[:, 1:2], scale=1.0)
            l2 = self.sb(ph, "l2", [128, 256], F32)
            Rl2 = cx.reg("l2")
            self.memset(l2[:, 0:16], 0.0, (Rl2,))
            for i in range(1, NT):
                self.tt(l2[:, i * 16:(i + 1) * 16], l2[:, (i - 1) * 16:i * 16], ltok[:, (i - 1) * 16:i * 16], ALU.add,
                        (Rl2, Rl), (Rl2,))
            pc, Rpc = self.psum()
            self.mm(pc[:, 0:256], tri_f, ltok[:], True, False, (Rl, self.R_consts), (Rpc,), inc=False)
            self.mm(pc[:, 0:256], self.onesf[:, :], l2[:], False, True, (Rl2, self.R_consts), (Rpc,), inc=True)
            self.vcopy(negc[:], pc[:, 0:256], (Rpc,), (Rn,))
            with self.phase() as ph2:
                v0 = self.sb(ph2, "c_v0", [16, CH], F32)
                r1 = self.sb(ph2, "c_r1", [16, CH], F32)
                cs3 = [self.sb(ph2, "c_split%d" % i, [16, 3, CH], BF16) for i in range(2)]
                Rc0 = cx.reg("c_v0")
                Rc3 = [cx.reg("c_split%d" % i) for i in range(2)]
                for c in range(NCH):
                    pct, Rpct = self.psum()
                    for tt_ in range(4):
                        i = c * 4 + tt_
                        self.tr(pct[0:16, tt_ * 128:(tt_ + 1) * 128], negc[:, i * 16:(i + 1) * 16], (Rn,), (Rpct,), inc=(tt_ == 3))
                    b = c % 2
                    self.ts(v0[:], pct[0:16, :], -1.0, None, ALU.mult, None, (Rpct,), (Rc0,))
                    self.vcopy(cs3[b][:, 0, :], v0[:], (Rc0,), (Rc3[b],))
                    self.tt(r1[:], v0[:], cs3[b][:, 0, :], ALU.subtract, (Rc0, Rc3[b]), (Rc0,))
                    self.vcopy(cs3[b][:, 1, :], r1[:], (Rc0,), (Rc3[b],))
                    self.tt(r1[:], r1[:], cs3[b][:, 1, :], ALU.subtract, (Rc0, Rc3[b]), (Rc0,))
                    self.vcopy(cs3[b][:, 2, :], r1[:], (Rc0,), (Rc3[b],))
                    cx.dma("sp", self.cscr_d[:, :, c * CH:(c + 1) * CH], cs3[b][:], reads=(Rc3[b],), writes=(self.R_cscr,))
            vaug = self.sb(ph, "fx_vaug", [128, NT, 384], BF16)
            Rva = cx.reg("fx_vaug")
            qT = [self.sb(ph, "fx_qT%d" % i, [128, S], BF16) for i in range(2)]
            kT = [self.sb(ph, "fx_kT%d" % i, [128, S], BF16) for i in range(2)]
            RqT = [cx.reg("fx_qT%d" % i) for i in range(2)]
            RqA = [cx.reg("fx_qTa%d" % i) for i in range(2)]
            RkT = [cx.reg("fx_kT%d" % i) for i in range(2)]
            RkA = [cx.reg("fx_kTa%d" % i) for i in range(2)]
            osc = self.sb(ph, "fx_osc", [128, 2, S], BF16)
            Ros = cx.reg("fx_osc")
            for i in range(2):
                self.memset(kT[i][64:70, :], 8.0, (RkT[i], RkA[i]))
                self.memset(qT[i][64:70, :], -8.0, (RqT[i], RqA[i]))
            for pr in range(2):
                self.memset(vaug[:, :, pr * 192 + 64:pr * 192 + 128], 1.0, (Rva,))
            for G in range(4):
                wt, Rw = self.W.get(win, 0, 1024, 2048 + G * 256, 256)
                for t in range(NT):
                    pv, Rpv = self.psum()
                    for k in range(DC):
                        self.mm(pv[:, 0:256], self.xT_bf[:, k, t * 128:(t + 1) * 128], wt[:, k, :], k == 0, k == DC - 1,
                                (Rw, self.R_xb[t // 4]), (Rpv,), inc=(k == DC - 1))
                    for hh in range(2):
                        src = pv[:, 0:256].rearrange("p (a b) -> p a b", a=2)[:, :, hh * 64:hh * 64 + 64]
                        dst = vaug[:, t, :].rearrange("p (a b) -> p a b", a=2)[:, :, hh * 128:hh * 128 + 64]
                        self.vcopy(dst, src, (Rpv,), (Rva,))
                wq, Rwq = self.W.get(win, 0, 1024, G * 256, 256)
                wk, Rwk = self.W.get(win, 0, 1024, 1024 + G * 256, 256)
                for pr in range(2):
                    for (wt_, Rw_, dstT, Rd) in ((wq, Rwq, qT, RqT), (wk, Rwk, kT, RkT)):
                        for c in range(NCH):
                            pq, Rpq = self.psum()
                            for k in range(DC):
                                self.mm(pq, wt_[:, k, pr * 128:(pr + 1) * 128], self.xT_bf[:, k, c * CH:(c + 1) * CH],
                                        k == 0, k == DC - 1, (Rw_, self.R_xb[c]), (Rpq,), inc=(k == DC - 1))
                            self.vcopy(dstT[0][0:64, c * CH:(c + 1) * CH], pq[0:64, :], (Rpq,), (Rd[0],))
                            self.vcopy(dstT[1][0:64, c * CH:(c + 1) * CH], pq[64:128, :], (Rpq,), (Rd[1],))
                    for hh in range(2):
                        h = G * 4 + pr * 2 + hh
                        for jx in range(3):
                            cx.dma("sp", qT[hh][64 + jx:65 + jx, :], self.cscr_d[h:h + 1, jx, :], reads=(self.R_cscr,), writes=(RqA[hh],))
                            cx.dma("sp", kT[hh][67 + jx:68 + jx, :], self.cscr_d[h:h + 1, jx, :], reads=(self.R_cscr,), writes=(RkA[hh],))
                    for hh in range(2):
                        h = G * 4 + pr * 2 + hh
                        i4 = pr * 2 + hh
                        vo = pr * 192 + hh * 64
                        self._attn_fox(qT, RqT, RqA, kT, RkT, RkA, vaug, Rva, osc, Ros, A, pr, hh, h)
                self.out_proj_partial("od_w_out", j, G * 256, 2, (lambda k, c: osc[:, k, c * CH:(c + 1) * CH]), (Ros,), G == 0)

    def _attn_fox(self, qT, RqT, RqA, kT, RkT, RkA, vaug, Rva, osc, Ros, A, pr, hh, h):
        vo = pr * 192 + hh * 64
        cxq = _Multi((RqT[hh], RqA[hh]))
        cxk = _Multi((RkT[hh], RkA[hh]))
        self.attn_causal(qT[hh], cxq, kT[hh], cxk, 70,
                         (lambda jt: vaug[:, jt, vo:vo + 128]), Rva, 0.125,
                         None, None, hh,
                         (lambda c: osc[hh * 64:hh * 64 + 64, pr, c * CH:(c + 1) * CH]), Ros, A)

    def even_mixer(self, li):
        cx = self.cx
        j = li // 2
        win = self.w["ev_w_in"][j]
        vb = 160 + j * 16
        ones_bf = self.cbf[:, 0:128]
        first = [True]
        with self.phase() as ph:
            cqn = self.sb(ph, "cqn", [128, 3, S], BF16)
            ckvn = self.sb(ph, "ckvn", [128, 2, S], BF16)
            kT = [self.sb(ph, "ml_kT%d" % i, [128, S], BF16) for i in range(2)]
            Rcq = [cx.reg("cqn%d" % c) for c in range(NCH)]
            Rckv = [cx.reg("ckvn%d" % c) for c in range(NCH)]
            RkT = [cx.reg("ml_kT%d" % i) for i in range(2)]
            RkR = [cx.reg("ml_kTr%d" % i) for i in range(2)]
            with self.phase() as p1:
                raws = [self.sb(p1, "lat_raw%d" % i, [128, 3, CH], F32) for i in range(2)]
                sqs = [self.sb(p1, "lat_sq%d" % i, [128, 3, CH], BF16) for i in range(2)]
                rstds = [self.sb(p1, "lat_rstd%d" % i, [128, CH], F32) for i in range(2)]
                wrot = self.sb(p1, "wkr_rot", [128, 8, 32], BF16)
                t1 = self.sb(p1, "kr_t1", [128, CH], F32)
                t2 = self.sb(p1, "kr_t2", [128, CH], F32)
                Rraws = [cx.reg("lat_raw%d" % i) for i in range(2)]
                Rsqs = [cx.reg("lat_sq%d" % i) for i in range(2)]
                Rrss = [cx.reg("lat_rstd%d" % i) for i in range(2)]
                Rwr, Rt1, Rt2 = (cx.reg(n) for n in ("wkr_rot", "kr_t1", "kr_t2"))
                for c in range(NCH):
                    sl = slice(c * CH, (c + 1) * CH)
                    wA, RwA = self.W.get(win, 0, 1024, 0, 256)
                    wB, RwB = self.W.get(win, 0, 1024, 256, 256)
                    wC, RwC = self.W.get(win, 0, 1024, 512, 160)
                    srcs = {0: (wA, RwA, 0), 1: (wA, RwA, 128), 2: (wB, RwB, 0), 3: (wB, RwB, 128), 4: (wC, RwC, 0)}
                    for (lat, ms, nfeat, dst, Rd, ncol) in ((0, (0, 1, 2), 384.0, cqn, Rcq, vb), (1, (3, 4), 256.0, ckvn, Rckv, vb + 3)):
                        raw, sq, rstd = raws[lat], sqs[lat], rstds[lat]
                        Rraw, Rsq, Rrs = Rraws[lat], Rsqs[lat], Rrss[lat]
                        for mi, m in enumerate(ms):
                            wt_, Rw_, co = srcs[m]
                            pt, Rp = self.psum()
                            for k in range(DC):
                                self.mm(pt, wt_[:, k, co:co + 128], self.xT_bf[:, k, sl], k == 0, k == DC - 1,
                                        (Rw_, self.R_xb[c]), (Rp,), inc=(k == DC - 1))
                            self.acopy(raw[:, mi, :], pt, (Rp,), (Rraw,))
                            self.tt(sq[:, mi, :], raw[:, mi, :], raw[:, mi, :], ALU.mult, (Rraw,), (Rsq,))
                        pss, Rpss = self.psum()
                        for mi in range(len(ms)):
                            self.mm(pss, ones_bf, sq[:, mi, :], mi == 0, mi == len(ms) - 1, (Rsq, self.R_consts), (Rpss,),
                                    inc=(mi == len(ms) - 1))
                        self.act(rstd[:], pss, AF.Ln, (Rpss, self.R_consts), (Rrs,), bias=self.small[:, 0:1], scale=1.0 / nfeat)
                        self.act(rstd[:], rstd[:], AF.Exp, (Rrs,), (Rrs,), scale=-0.5)
                        for mi in range(len(ms)):
                            self.stt(dst[:, mi, sl], raw[:, mi, :], self.vecs[:, ncol + mi:ncol + mi + 1], rstd[:], ALU.mult, ALU.mult,
                                     (Rraw, Rrs, self.R_consts), (Rd[c],))
                    if c == 0:
                        self.ts(wrot[:, :, 0:16], wC[:, :, 144:160], -1.0, None, ALU.mult, None, (RwC,), (Rwr,))
                        self.vcopy(wrot[:, :, 16:32], wC[:, :, 128:144], (RwC,), (Rwr,))
                    pk, Rpk = self.psum()
                    for k in range(DC):
                        self.mm(pk[0:32, :], wC[:, k, 128:160], self.xT_bf[:, k, sl], k == 0, k == DC - 1, (RwC, self.R_xb[c]), (Rpk,),
                                inc=(k == DC - 1))
                    pr_, Rpr = self.psum()
                    for k in range(DC):
                        self.mm(pr_[0:32, :], wrot[:, k, :], self.xT_bf[:, k, sl], k == 0, k == DC - 1, (Rwr, self.R_xb[c]), (Rpr,),
                                inc=(k == DC - 1))
                    self.tt(t1[0:32, :], pk[0:32, :], self.cs[0:32, c * CH:(c + 1) * CH], ALU.mult, (Rpk, self.R_cs), (Rt1,))
                    self.tt(t2[0:32, :], pr_[0:32, :], self.cs[0:32, S + c * CH:S + (c + 1) * CH], ALU.mult, (Rpr, self.R_cs), (Rt2,))
                    self.tt(kT[0][64:96, sl], t1[0:32, :], t2[0:32, :], ALU.add, (Rt1, Rt2), (RkR[0],))
                    self.acopy(kT[1][64:96, sl], kT[0][64:96, sl], (RkR[0],), (RkR[1],))
            with self.phase() as p2:
                A = None
                qs = self.sb(p2, "sw_q", [128, 2, S], BF16)
                ks2 = self.sb(p2, "sw_k2", [128, S], BF16)
                vs = self.sb(p2, "sw_v", [128, NT, 128], BF16)
                wks2 = self.sb(p2, "sw_wk2", [128, 8, 128], BF16)
                osc = self.sb(p2, "sw_osc", [128, 2, S], BF16)
                pp = [self.sb(p2, "sw_p%d" % i, [128, 2, CH], BF16) for i in range(2)]
                rec = [self.sb(p2, "sw_rec%d" % i, [128, CH], F32) for i in range(2)]
                Rqs, Rks, Rvs, Rwk2, Ros = (cx.reg(n) for n in ("sw_q", "sw_k2", "sw_v", "sw_wk2", "sw_osc"))
                Rpp = [cx.reg("sw_p%d" % i) for i in range(2)]
                Rrec = [cx.reg("sw_rec%d" % i) for i in range(2)]
                self.memset(vs[:, :, 64:128], 1.0, (Rvs,))
                nb = 0
                for g in range(2):
                    wq, Rwq = self.W.get(win, 0, 1024, EV_COLS["qs"] + g * 256, 256)
                    for c in range(NCH):
                        for m in range(2):
                            pt, Rp = self.psum()
                            for k in range(DC):
                                self.mm(pt, wq[:, k, m * 128:(m + 1) * 128], self.xT_bf[:, k, c * CH:(c + 1) * CH], k == 0, k == DC - 1,
                                        (Rwq, self.R_xb[c]), (Rp,), inc=(k == DC - 1))
                            self.acopy(qs[:, m, c * CH:(c + 1) * CH], pt, (Rp,), (Rqs,))
                    wkv, Rwkv = self.W.get(win, 0, 1024, EV_COLS["ks"], 256)
                    for rep in range(2):
                        self.vcopy(wks2[:, :, rep * 64:(rep + 1) * 64], wkv[:, :, g * 64:(g + 1) * 64], (Rwkv,), (Rwk2,))
                    for c in range(NCH):
                        pt, Rp = self.psum()
                        for k in range(DC):
                            self.mm(pt, wks2[:, k, :], self.xT_bf[:, k, c * CH:(c + 1) * CH], k == 0, k == DC - 1,
                                    (Rwk2, self.R_xb[c]), (Rp,), inc=(k == DC - 1))
                        self.vcopy(ks2[:, c * CH:(c + 1) * CH], pt, (Rp,), (Rks,))
                    for t8 in range(2):
                        pt, Rp = self.psum()
                        for tt_ in range(8):
                            t = t8 * 8 + tt_
                            for k in range(DC):
                                self.mm(pt[:, tt_ * 64:(tt_ + 1) * 64], self.xT_bf[:, k, t * 128:(t + 1) * 128],
                                        wkv[:, k, 128 + g * 64:128 + (g + 1) * 64], k == 0, k == DC - 1,
                                        (Rwkv, self.R_xb[t // 4]), (Rp,), inc=(k == DC - 1))
                        self.acopy(vs[:, t8 * 8:(t8 + 1) * 8, 0:64], pt.rearrange("p (a b) -> p a b", a=8), (Rp,), (Rvs,))
                    def sw_front(n, i):
                        kts = (1,) if n == 0 else (0, 1)
                        ident_bf = self.cbf[:, 512:640]
                        for kt in kts:
                            ktile = n - 1 + kt
                            for half in range(2):
                                b = kt * 2 + half
                                sp_, Rsp = self.ps[:, b, :], self.R_ps[b]
                                for jc in range(2):
                                    hi = 2 * jc + half
                                    o = (kt * 8 + 4 * g + hi) * 128
                                    self.mm(sp_[:, jc * 128:(jc + 1) * 128], ks2[half * 64:(half + 1) * 64, ktile * 128:(ktile + 1) * 128],
                                            qs[half * 64:(half + 1) * 64, jc, n * 128:(n + 1) * 128], True, False, (Rks, Rqs), (Rsp,), inc=False)
                                    self.mm(sp_[:, jc * 128:(jc + 1) * 128], ident_bf, self.b8m[:, o:o + 128], False, True,
                                            (self.R_ebm, self.R_consts), (Rsp,), inc=(jc == 1))
                                dst = pp[i][:, kt, :].rearrange("p (a b) -> p a b", a=4)[:, half:4:2, :]
                                self.act(dst, sp_[:, 0:256].rearrange("p (a b) -> p a b", a=2), AF.Exp, (Rsp,), (Rpp[i],), scale=0.125)

                    def sw_back(n, i):
                        kts = (1,) if n == 0 else (0, 1)
                        ab = 4 + i
                        acc, Racc = self.ps[:, ab, :], self.R_ps[ab]
                        for ki, kt in enumerate(kts):
                            ktile = n - 1 + kt
                            self.mm(acc, vs[:, ktile, :], pp[i][:, kt, :], ki == 0, ki == len(kts) - 1, (Rvs, Rpp[i]), (Racc,),
                                    inc=(ki == len(kts) - 1))
                        for hi in range(4):
                            sc = vb + 5 + 4 * g + hi
                            self.ts(rec[i][0:64, hi * 128:(hi + 1) * 128], acc[64:128, hi * 128:(hi + 1) * 128],
                                    self.vecs[64:128, sc:sc + 1], None, ALU.add, None, (Racc, self.R_consts), (Rrec[i],))
                        self.act(rec[i][0:64, :], rec[i][0:64, :], AF.Ln, (Rrec[i],), (Rrec[i],))
                        self.act(rec[i][0:64, :], rec[i][0:64, :], AF.Exp, (Rrec[i],), (Rrec[i],), scale=-1.0)
                        for half in range(2):
                            src = acc[0:64, :].rearrange("p (a b) -> p a b", a=4)[:, half:4:2, :]
                            rcs = rec[i][0:64, :].rearrange("p (a b) -> p a b", a=4)[:, half:4:2, :]
                            dst = osc[half * 64:(half + 1) * 64, :, n * 128:(n + 1) * 128]
                            self.tt(dst, src, rcs, ALU.mult, (Racc, Rrec[i]), (Ros,))

                    for n in range(NT + 1):
                        if n < NT:
                            sw_front(n, n % 2)
                        if n >= 1:
                            sw_back(n - 1, (n - 1) % 2)
                    self.out_proj_partial("ev_w_out", j, 512 + g * 256, 2, (lambda k, c: osc[:, k, c * CH:(c + 1) * CH]), (Ros,), first[0])
                    first[0] = False
            with self.phase() as p3:
                A = self.attn_bufs(p3)
                qT = [self.sb(p3, "ml_qT%d" % i, [128, S], BF16) for i in range(2)]
                vaug = self.sb(p3, "ml_vaug", [128, NT, 192], BF16)
                osc = self.sb(p3, "ml_osc", [128, 2, S], BF16)
                wqrot = self.sb(p3, "ml_wqrot", [128, 3, 2, 32], BF16)
                t1 = self.sb(p3, "ml_t1", [128, CH], F32)
                t2 = self.sb(p3, "ml_t2", [128, CH], F32)
                RqT = [cx.reg("ml_qT%d" % i) for i in range(2)]
                Rva, Ros, Rwr, Rt1, Rt2 = (cx.reg(n) for n in ("ml_vaug", "ml_osc", "ml_wqrot", "ml_t1", "ml_t2"))
                self.memset(vaug[:, :, 64:128], 1.0, (Rva,))
                scale = float(96.0 ** -0.5)
                for pr in range(4):
                    wkv, Rwkv = self.W.get(self.w["ev_w_ukv"][j], 0, 256, 0, 1024)
                    for t4 in range(4):
                        pv, Rpv = self.psum()
                        for tt_ in range(4):
                            t = t4 * 4 + tt_
                            for hh in range(2):
                                h = pr * 2 + hh
                                for k in range(2):
                                    self.mm(pv[:, tt_ * 128 + hh * 64:tt_ * 128 + hh * 64 + 64], ckvn[:, k, t * 128:(t + 1) * 128],
                                            wkv[:, k, h * 128 + 64:h * 128 + 128], k == 0, k == 1, (Rwkv, Rckv[t // 4]), (Rpv,),
                                            inc=(k == 1 and hh == 1 and tt_ == 3))
                        for hh in range(2):
                            src = pv.rearrange("p (a b) -> p a b", a=4)[:, :, hh * 64:hh * 64 + 64]
                            dst = vaug[:, t4 * 4:(t4 + 1) * 4, hh * 128:hh * 128 + 64]
                            if hh == 0:
                                self.acopy(dst, src, (Rpv,), (Rva,))
                            else:
                                self.vcopy(dst, src, (Rpv,), (Rva,))
                    for hh in range(2):
                        h = pr * 2 + hh
                        for c in range(NCH):
                            pk, Rpk = self.psum()
                            for k in range(2):
                                self.mm(pk[0:64, :], wkv[:, k, h * 128:h * 128 + 64], ckvn[:, k, c * CH:(c + 1) * CH], k == 0, k == 1,
                                        (Rwkv, Rckv[c]), (Rpk,), inc=(k == 1))
                            self.acopy(kT[hh][0:64, c * CH:(c + 1) * CH], pk[0:64, :], (Rpk,), (RkT[hh],))
                    wq, Rwq = self.W.get(self.w["ev_w_uq"][j], 0, 384, pr * 192, 192)
                    for hh in range(2):
                        self.ts(wqrot[:, :, hh, 0:16], wq[:, :, hh * 96 + 80:hh * 96 + 96], -1.0, None, ALU.mult, None, (Rwq,), (Rwr,))
                        self.vcopy(wqrot[:, :, hh, 16:32], wq[:, :, hh * 96 + 64:hh * 96 + 80], (Rwq,), (Rwr,))
                    for hh in range(2):
                        for c in range(NCH):
                            sl = slice(c * CH, (c + 1) * CH)
                            pq, Rpq = self.psum()
                            for k in range(3):
                                self.mm(pq[0:96, :], wq[:, k, hh * 96:(hh + 1) * 96], cqn[:, k, sl], k == 0, k == 2, (Rwq, Rcq[c]), (Rpq,),
                                        inc=(k == 2))
                            prr, Rprr = self.psum()
                            for k in range(3):
                                self.mm(prr[0:32, :], wqrot[:, k, hh, :], cqn[:, k, sl], k == 0, k == 2, (Rwr, Rcq[c]), (Rprr,), inc=(k == 2))
                            self.acopy(qT[hh][0:64, sl], pq[0:64, :], (Rpq,), (RqT[hh],))
                            self.tt(t1[64:96, :], pq[64:96, :], self.cs[64:96, c * CH:(c + 1) * CH], ALU.mult, (Rpq, self.R_cs), (Rt1,))
                            self.tt(t2[64:96, :], prr[0:32, :], self.cs[0:32, S + c * CH:S + (c + 1) * CH], ALU.mult, (Rprr, self.R_cs), (Rt2,))
                            self.tt(qT[hh][64:96, sl], t1[64:96, :], t2[64:96, :], ALU.add, (Rt1, Rt2), (RqT[hh],))
                    for hh in range(2):
                        vo = hh * 64
                        self.attn_causal(qT[hh], RqT[hh], kT[hh], _Multi((RkT[hh], RkR[hh])), 96,
                                         (lambda jt, vo=vo: vaug[:, jt, vo:vo + 128]), Rva, scale, None, None, hh,
                                         (lambda c, hh=hh, pr=pr: osc[hh * 64:hh * 64 + 64, pr % 2, c * CH:(c + 1) * CH]), Ros, A,
                                         act_recip=True)
                    if pr % 2 == 1:
                        self.out_proj_partial("ev_w_out", j, (pr - 1) * 128, 2, (lambda k, c: osc[:, k, c * CH:(c + 1) * CH]), (Ros,), first[0])
                        first[0] = False


class _Multi:
    def __init__(self, parts):
        self.parts = parts


def _expand(regs):
    out = []
    for r in regs:
        if isinstance(r, _Multi):
            out.extend(r.parts)
        else:
            out.append(r)
    return out


_orig_collect = Ctx._collect
_orig_record = Ctx._record


def _collect2(self, reads, writes, own=()):
    return _orig_collect(self, _expand(reads), _expand(writes), own)


def _record2(self, ev, reads, writes):
    return _orig_record(self, ev, _expand(reads), _expand(writes))


Ctx._collect = _collect2
Ctx._record = _record2


def _t5_bucket(dist):
    exact = 16
    d = np.maximum(dist, 1).astype(np.float32)
    large = exact + (np.log(d / np.float32(exact)) / np.float32(math.log(128 / exact)) * np.float32(32 - exact)).astype(np.int32)
    large = np.minimum(large, 31)
    return np.where(dist < exact, dist, large)


def host_consts(inputs):
    consts = np.zeros((128, 384), np.float32)
    consts[:, 0:128] = np.eye(128, dtype=np.float32)
    s_idx = np.arange(128)[:, None]
    t_idx = np.arange(128)[None, :]
    consts[:, 128:256] = (s_idx <= t_idx).astype(np.float32)
    consts[:, 256:384] = (s_idx > t_idx).astype(np.float32)
    vecs = np.zeros((128, 192), np.float32)

    def pc(v):
        return np.ascontiguousarray(v.reshape(-1, 128).T)
    for i in range(DEPTH):
        b = i * 40
        vecs[:, b + 0:b + 8] = pc(inputs["ln1_g"][i])
        vecs[:, b + 8:b + 16] = pc(inputs["ln1_b"][i])
        vecs[:, b + 16:b + 24] = pc(inputs["ln2_g"][i])
        vecs[:, b + 24:b + 32] = pc(inputs["ln2_b"][i])
        vecs[:, b + 32:b + 40] = pc(inputs["ple_b_gate"][i])
    for j in range(2):
        b = 160 + j * 16
        vecs[:, b:b + 3] = pc(inputs["ev_q_norm"][j])
        vecs[:, b + 3:b + 5] = pc(inputs["ev_kv_norm"][j])
    rows = np.zeros((1, 32), np.float32)
    for j in range(2):
        vecs[:, 160 + j * 16 + 5:160 + j * 16 + 13] = inputs["ev_sinks"][j][None, :]
        rows[0, j * 16:(j + 1) * 16] = inputs["od_b_f"][j]
    inv = (1.0 / (np.float32(10000.0) ** (np.arange(0, 32, 2, dtype=np.float32) / np.float32(32)))).astype(np.float32)
    ang = (np.arange(S, dtype=np.float32)[None, :] * inv[:, None]).astype(np.float32)
    cs = np.zeros((128, 2 * S), np.float32)
    cs[:, 0:S] = np.tile(np.cos(ang), (8, 1))
    cs[:, S:2 * S] = np.tile(np.sin(ang), (8, 1))
    kk = np.arange(128)[:, None]
    a = np.arange(128)[None, :]
    biasg = np.zeros((128, 2, 8, 128), np.float32)
    for kt in range(2):
        dist = (a + 128 - kk) if kt == 0 else (a - kk)
        bk = _t5_bucket(np.maximum(dist, 0))
        g = inputs["rel_bias"][bk]
        biasg[:, kt] = np.transpose(g, (0, 2, 1))
    return dict(consts=consts, vecs=vecs, rows=rows, cs=cs, biasg=np.ascontiguousarray(biasg.reshape(128, 2048)))


WEIGHT_NAMES = ["ev_w_in", "ev_w_uq", "ev_w_ukv", "ev_w_out", "od_w_in", "od_w_out", "w_up", "w_down", "ple_w_proj", "ple_w_gate"]

_CACHE = {}


def get_builder(nseq, layers, stop=None):
    key = (nseq, tuple(layers), stop)
    if key not in _CACHE:
        _CACHE[key] = Builder(nseq, layers, stop)
    return _CACHE[key]


def run(inputs, n_cores, nseq, layers=(0, 1, 2, 3), stop=None, batch0=0):
    bld = get_builder(nseq, layers, stop)
    hc = host_consts(inputs)
    in_maps = []
    for c in range(n_cores):
        b0 = batch0 + c * nseq
        m = {"x": np.ascontiguousarray(inputs["x"][b0:b0 + nseq]),
             "p": np.ascontiguousarray(inputs["p"][:, b0:b0 + nseq])}
        for wn in WEIGHT_NAMES:
            m[wn] = np.ascontiguousarray(inputs[wn])
        m.update(hc)
        in_maps.append(m)
    res = run_bass_kernel_spmd(bld.nc, in_maps, core_ids=list(range(n_cores)))
    return np.concatenate([np.asarray(r["out"]) for r in res.results], axis=0)


def kernel(**inputs):
    inputs = {k: np.asarray(v) for k, v in inputs.items()}
    out = run(inputs, N_CORES, SEQ_PER_CORE)
    return out.astype(np.float32)
```
